# Optimizing a Trainium2 kernel written in Bass

```python
import math
import jax
import jax.numpy as jnp
from jax import lax
import numpy as np

D_MODEL = 1024
BATCH = 4
SEQ = 8192
DEPTH = 2
DEC_BATCH = 8
DEC_SEQ = 8192
PAST_LEN = 128

N_MIXERS = 2
N_LRU_LAYERS = (DEPTH + 1) // 2
N_RET_LAYERS = DEPTH // 2
EPS = 1e-6

LRU_WIDTH = D_MODEL
LRU_BLOCKS = 4
LRU_BLOCK_W = LRU_WIDTH // LRU_BLOCKS
LRU_CONV_W = 4
LRU_CONV_LEFT = 2
LRU_C = 8.0

RET_HEADS = 4
RET_DK = D_MODEL // RET_HEADS
RET_DV = 2 * RET_DK
RET_QK = RET_HEADS * RET_DK
RET_V = RET_HEADS * RET_DV
RET_CHUNK = 128
ROPE_BASE = 10000.0

D_FF = 2816
FFN_CONV_W = 3

kernel_name = "hybrid_rglru_retention_encoder"


def rms_norm(x, g):
    xf = x.astype(jnp.float32)
    y = xf * lax.rsqrt(jnp.mean(xf * xf, axis=-1, keepdims=True) + EPS)
    return (y * g.astype(jnp.float32)).astype(x.dtype)


def depthwise_conv(x, w, b, pad_left):
    width = w.shape[0]
    S = x.shape[1]
    xp = jnp.pad(x, ((0, 0), (pad_left, width - 1 - pad_left), (0, 0)))
    y = xp[:, 0:S] * w[0]
    for k in range(1, width):
        y = y + xp[:, k:k + S] * w[k]
    return y + b


def linear_scan(a, b, reverse):
    def step(h, ab):
        h = ab[0] * h + ab[1]
        return h, h
    h0 = jnp.zeros((a.shape[0], a.shape[2]), jnp.float32)
    _, hs = lax.scan(step, h0, (jnp.swapaxes(a, 0, 1), jnp.swapaxes(b, 0, 1)), reverse=reverse)
    return jnp.swapaxes(hs, 0, 1)


def rglru_mixer(x, w_in, conv_w, conv_b, w_a, b_a, w_x, b_x, lam, w_out):
    B, S, _ = x.shape
    gate_in, rec = jnp.split(x @ w_in, 2, axis=-1)
    gate = jax.nn.gelu(gate_in, approximate=True)
    xc = depthwise_conv(rec, conv_w, conv_b, LRU_CONV_LEFT)
    xb = xc.reshape(B, S, LRU_BLOCKS, LRU_BLOCK_W)
    r = jax.nn.sigmoid(jnp.einsum('bsnj,dnjk->dbsnk', xb, w_a).reshape(2, B, S, LRU_WIDTH)
                       + b_a[:, None, None, :])
    i = jax.nn.sigmoid(jnp.einsum('bsnj,dnjk->dbsnk', xb, w_x).reshape(2, B, S, LRU_WIDTH)
                       + b_x[:, None, None, :])
    log_a = -LRU_C * r.astype(jnp.float32) * jax.nn.softplus(-lam.astype(jnp.float32))[:, None, None, :]
    a = jnp.exp(log_a)
    u = jnp.sqrt(-jnp.expm1(2.0 * log_a)) * (i * xc[None]).astype(jnp.float32)
    h = linear_scan(a[0], u[0], reverse=False) + linear_scan(a[1], u[1], reverse=True)
    return (h.astype(x.dtype) * gate) @ w_out


def rotary(t, cos, sin):
    t1, t2 = jnp.split(t, 2, axis=-1)
    return jnp.concatenate([t1 * cos - t2 * sin, t2 * cos + t1 * sin], axis=-1)


def retention_one_direction(q, k, v, log_g, include_diag):
    B, S, H, dk = q.shape
    dv = v.shape[-1]
    n_chunks = S // RET_CHUNK

    def to_chunks(t):
        return t.reshape(B, n_chunks, RET_CHUNK, H, t.shape[-1]).transpose(1, 0, 3, 2, 4)

    qc, kc, vc = to_chunks(q), to_chunks(k), to_chunks(v)
    pos = jnp.arange(RET_CHUNK, dtype=jnp.float32)
    diff = pos[:, None] - pos[None, :]
    mask = (diff >= 0) if include_diag else (diff > 0)
    decay = jnp.where(mask[None], jnp.exp(log_g[:, None, None] * jnp.maximum(diff, 0.0)[None]), 0.0)
    xi = jnp.exp(log_g[:, None] * (pos + 1.0)[None])
    zeta = jnp.exp(log_g[:, None] * (RET_CHUNK - 1.0 - pos)[None])
    g_chunk = jnp.exp(log_g * RET_CHUNK)

    def step(state, chunk):
        qb, kb, vb = chunk
        scores = jnp.einsum('bhnk,bhmk->bhnm', qb, kb) * decay[None]
        out = (jnp.einsum('bhnm,bhmv->bhnv', scores, vb)
               + jnp.einsum('bhnk,bhkv->bhnv', qb, state) * xi[None, :, :, None])
        state = (state * g_chunk[None, :, None, None]
                 + jnp.einsum('bhmk,bhmv->bhkv', kb * zeta[None, :, :, None], vb))
        return state, out

    state0 = jnp.zeros((B, H, dk, dv), jnp.float32)
    _, out = lax.scan(step, state0, (qc, kc, vc))
    return out.transpose(1, 0, 3, 2, 4).reshape(B, S, H, dv)


def retention_mixer(x, w_in, decay_logit, norm_g, w_out):
    B, S, _ = x.shape
    q, k, v, g = jnp.split(x @ w_in, [RET_QK, 2 * RET_QK, 2 * RET_QK + RET_V], axis=-1)
    q = q.reshape(B, S, RET_HEADS, RET_DK).astype(jnp.float32)
    k = k.reshape(B, S, RET_HEADS, RET_DK).astype(jnp.float32) * (RET_DK ** -0.5)
    v = v.reshape(B, S, RET_HEADS, RET_DV).astype(jnp.float32)
    half = RET_DK // 2
    theta = ROPE_BASE ** (-jnp.arange(half, dtype=jnp.float32) / half)
    ang = jnp.arange(S, dtype=jnp.float32)[:, None] * theta[None, :]
    cos = jnp.cos(ang)[None, :, None, :]
    sin = jnp.sin(ang)[None, :, None, :]
    q = rotary(q, cos, sin)
    k = rotary(k, cos, sin)
    log_g = jax.nn.log_sigmoid(decay_logit.astype(jnp.float32))
    y_fwd = retention_one_direction(q, k, v, log_g[0], include_diag=True)
    y_bwd = jnp.flip(retention_one_direction(jnp.flip(q, 1), jnp.flip(k, 1), jnp.flip(v, 1),
                                             log_g[1], include_diag=False), 1)
    y = y_fwd + y_bwd
    y = y * lax.rsqrt(jnp.mean(y * y, axis=-1, keepdims=True) + EPS)
    y = (y.reshape(B, S, RET_V) * norm_g.astype(jnp.float32)).astype(x.dtype)
    return (jax.nn.silu(g) * y) @ w_out


def conv_ffn(x, w_in, conv_w, conv_b, w_out):
    u, v = jnp.split(x @ w_in, 2, axis=-1)
    u = depthwise_conv(u, conv_w, conv_b, FFN_CONV_W // 2)
    return (jax.nn.gelu(u, approximate=True) * v) @ w_out


def encoder(x, norm_mix, norm_ffn, norm_final,
            lru_w_in, lru_conv_w, lru_conv_b, lru_w_a, lru_b_a, lru_w_x, lru_b_x, lru_lambda, lru_w_out,
            ret_w_in, ret_decay_logit, ret_norm, ret_w_out,
            ffn_w_in, ffn_conv_w, ffn_conv_b, ffn_w_out):
    for i in range(DEPTH):
        j = i // N_MIXERS
        h = rms_norm(x, norm_mix[i])
        if i % N_MIXERS == 0:
            h = rglru_mixer(h, lru_w_in[j], lru_conv_w[j], lru_conv_b[j], lru_w_a[j], lru_b_a[j],
                            lru_w_x[j], lru_b_x[j], lru_lambda[j], lru_w_out[j])
        else:
            h = retention_mixer(h, ret_w_in[j], ret_decay_logit[j], ret_norm[j], ret_w_out[j])
        x = x + h
        x = x + conv_ffn(rms_norm(x, norm_ffn[i]), ffn_w_in[i], ffn_conv_w[i], ffn_conv_b[i], ffn_w_out[i])
    return rms_norm(x, norm_final)


def setup_inputs(seed: int = 0) -> dict:
    key = jax.random.key(seed)
    ks = jax.random.split(key, 24)
    f32 = jnp.float32

    def nrm(k, shape, scale):
        return jax.random.normal(k, shape, f32) * scale

    a_pow_c = jax.random.uniform(ks[10], (N_LRU_LAYERS, 2, LRU_WIDTH), f32, 0.9, 0.999)
    a_base = jnp.exp(jnp.log(a_pow_c) / LRU_C)
    lru_lambda = jnp.log(a_base) - jnp.log1p(-a_base)
    h_idx = jnp.arange(RET_HEADS, dtype=f32)
    gamma0 = 1.0 - 2.0 ** (-5.0 - h_idx)
    logit0 = jnp.log(gamma0) - jnp.log1p(-gamma0)
    ret_decay_logit = logit0[None, None, :] + nrm(ks[13], (N_RET_LAYERS, 2, RET_HEADS), 0.05)

    return {
        "x_prompt": nrm(ks[0], (BATCH, SEQ, D_MODEL), 1.0),
        "x_sample": nrm(ks[1], (DEC_BATCH, DEC_SEQ, D_MODEL), 1.0),
        "norm_mix": 1.0 + nrm(ks[2], (DEPTH, D_MODEL), 0.02),
        "norm_ffn": 1.0 + nrm(ks[3], (DEPTH, D_MODEL), 0.02),
        "norm_final": 1.0 + nrm(ks[4], (D_MODEL,), 0.02),
        "lru_w_in": nrm(ks[5], (N_LRU_LAYERS, D_MODEL, 2 * LRU_WIDTH), D_MODEL ** -0.5),
        "lru_conv_w": nrm(ks[6], (N_LRU_LAYERS, LRU_CONV_W, LRU_WIDTH), LRU_CONV_W ** -0.5),
        "lru_conv_b": nrm(ks[7], (N_LRU_LAYERS, LRU_WIDTH), 0.01),
        "lru_w_a": nrm(ks[8], (N_LRU_LAYERS, 2, LRU_BLOCKS, LRU_BLOCK_W, LRU_BLOCK_W), LRU_BLOCK_W ** -0.5),
        "lru_b_a": nrm(ks[9], (N_LRU_LAYERS, 2, LRU_WIDTH), 0.01),
        "lru_w_x": nrm(ks[11], (N_LRU_LAYERS, 2, LRU_BLOCKS, LRU_BLOCK_W, LRU_BLOCK_W), LRU_BLOCK_W ** -0.5),
        "lru_b_x": nrm(ks[12], (N_LRU_LAYERS, 2, LRU_WIDTH), 0.01),
        "lru_lambda": lru_lambda,
        "lru_w_out": nrm(ks[14], (N_LRU_LAYERS, LRU_WIDTH, D_MODEL), LRU_WIDTH ** -0.5),
        "ret_w_in": nrm(ks[15], (N_RET_LAYERS, D_MODEL, 2 * RET_QK + 2 * RET_V), D_MODEL ** -0.5),
        "ret_decay_logit": ret_decay_logit,
        "ret_norm": 1.0 + nrm(ks[16], (N_RET_LAYERS, RET_V), 0.02),
        "ret_w_out": nrm(ks[17], (N_RET_LAYERS, RET_V, D_MODEL), RET_V ** -0.5),
        "ffn_w_in": nrm(ks[18], (DEPTH, D_MODEL, 2 * D_FF), D_MODEL ** -0.5),
        "ffn_conv_w": nrm(ks[19], (DEPTH, FFN_CONV_W, D_FF), FFN_CONV_W ** -0.5),
        "ffn_conv_b": nrm(ks[20], (DEPTH, D_FF), 0.01),
        "ffn_w_out": nrm(ks[21], (DEPTH, D_FF, D_MODEL), D_FF ** -0.5),
    }


def reference(x_prompt, x_sample, norm_mix, norm_ffn, norm_final,
              lru_w_in, lru_conv_w, lru_conv_b, lru_w_a, lru_b_a, lru_w_x, lru_b_x, lru_lambda, lru_w_out,
              ret_w_in, ret_decay_logit, ret_norm, ret_w_out,
              ffn_w_in, ffn_conv_w, ffn_conv_b, ffn_w_out):
    y_prompt = encoder(x_prompt, norm_mix, norm_ffn, norm_final,
                       lru_w_in, lru_conv_w, lru_conv_b, lru_w_a, lru_b_a, lru_w_x, lru_b_x, lru_lambda, lru_w_out,
                       ret_w_in, ret_decay_logit, ret_norm, ret_w_out,
                       ffn_w_in, ffn_conv_w, ffn_conv_b, ffn_w_out)
    y_sample = encoder(x_sample, norm_mix, norm_ffn, norm_final,
                       lru_w_in, lru_conv_w, lru_conv_b, lru_w_a, lru_b_a, lru_w_x, lru_b_x, lru_lambda, lru_w_out,
                       ret_w_in, ret_decay_logit, ret_norm, ret_w_out,
                       ffn_w_in, ffn_conv_w, ffn_conv_b, ffn_w_out)
    return (y_prompt, y_sample)
```

```python
import math
from contextlib import ExitStack
import numpy as np
import ml_dtypes
import concourse.bass as bass
import concourse.mybir as mybir
from concourse.bass_utils import run_bass_kernel_spmd

F32 = mybir.dt.float32
BF16 = mybir.dt.bfloat16
AF = mybir.ActivationFunctionType
ALU = mybir.AluOpType

D = 1024
T = 512
XW = 515
TMPW = 520
DFF = 2816
NFC = 22
EPS = 1e-6
N_CORES = 8

W_LAYOUT = {}
_off = 0
for _n, _sz in [("lru_in", 16 * 8 * 128), ("ga0", 4 * 2 * 2 * 128), ("gx0", 4 * 2 * 2 * 128),
                ("ga1", 4 * 2 * 2 * 128), ("gx1", 4 * 2 * 2 * 128), ("lru_out", 8 * 1024),
                ("ffn_in0", 11 * 4 * 8 * 128), ("ffn_out0", 22 * 1024),
                ("ffn_in1", 11 * 4 * 8 * 128), ("ffn_out1", 22 * 1024)] + \
               [("ret_qkg%d" % h, 8 * 8 * 128) for h in range(4)] + \
               [("ret_v%d" % h, 8 * 512) for h in range(4)] + \
               [("ret_out%d" % h, 4 * 1024) for h in range(4)]:
    W_LAYOUT[_n] = (_off, _sz)
    _off += _sz
NW = _off

C_LAYOUT = {}
_off = 0
for _n, _sz in [("ident", 128), ("tri", 256), ("nrow", 256), ("pcols", 4), ("gmix", 16), ("gffn", 16),
                ("gfin", 1024), ("lru_cw", 32), ("lru_cb", 8), ("b_a", 16), ("b_x", 16), ("lam", 16),
                ("ffn_cw0", 66), ("ffn_cw1", 66), ("ffn_cb0", 22), ("ffn_cb1", 22), ("ret_norm", 16),
                ("dlogit", 8)]:
    C_LAYOUT[_n] = (_off, _sz)
    _off += _sz
NCP = _off


def _cols(v, nchunk):
    return np.ascontiguousarray(v.reshape(nchunk, 128).T)


def _stat(w, fcs):
    K = w.shape[0]
    w4 = w.reshape(K // 128, 128, w.shape[1] // 128, 128)
    return np.ascontiguousarray(w4[:, :, fcs, :].transpose(1, 2, 0, 3))


def _mov(w):
    K = w.shape[0]
    return np.ascontiguousarray(w.reshape(K // 128, 128, w.shape[1]).transpose(1, 0, 2))


def pack_weights(inp):
    wp = np.empty((128, NW), np.float32)

    def put(name, arr):
        o, sz = W_LAYOUT[name]
        wp[:, o:o + sz] = arr.reshape(128, sz)

    put("lru_in", _stat(inp["lru_w_in"][0], list(range(16))))
    for d in range(2):
        for nm, key in (("ga", "lru_w_a"), ("gx", "lru_w_x")):
            w = inp[key][0, d]
            w5 = w.reshape(4, 2, 128, 2, 128)
            put("%s%d" % (nm, d), np.ascontiguousarray(w5.transpose(2, 0, 1, 3, 4)))
    put("lru_out", _mov(inp["lru_w_out"][0]))
    for l in range(2):
        w = inp["ffn_w_in"][l]
        slabs = []
        for s in range(11):
            slabs.append(_stat(w, [2 * s, 2 * s + 1, NFC + 2 * s, NFC + 2 * s + 1]))
        put("ffn_in%d" % l, np.stack(slabs, axis=1))
        put("ffn_out%d" % l, _mov(inp["ffn_w_out"][l]))
    rw = inp["ret_w_in"][0]
    for h in range(4):
        fcs = [2 * h, 2 * h + 1, 8 + 2 * h, 8 + 2 * h + 1] + [32 + 4 * h + i for i in range(4)]
        put("ret_qkg%d" % h, _stat(rw, fcs))
        put("ret_v%d" % h, _mov(rw[:, 2048 + 512 * h:2048 + 512 * (h + 1)]))
        put("ret_out%d" % h, _mov(inp["ret_w_out"][0][512 * h:512 * (h + 1), :]))
    return wp


def pack_consts(inp):
    cp = np.zeros((128, NCP), np.float32)

    def put(name, arr):
        o, sz = C_LAYOUT[name]
        cp[:, o:o + sz] = np.asarray(arr, np.float32).reshape(128, sz)

    put("ident", np.eye(128, dtype=np.float32))
    m = np.arange(128)[:, None]
    n = np.arange(128)[None, :]
    tri = np.stack([(n >= m), (m > n)], axis=1).astype(np.float32)
    put("tri", tri)
    nrow = np.stack([np.broadcast_to(n + 1.0, (128, 128)), np.broadcast_to(128.0 - n, (128, 128))], axis=1)
    put("nrow", nrow)
    mm = np.arange(128, dtype=np.float32)
    put("pcols", np.stack([-(mm + 1), mm - 128, 127 - mm, mm], axis=1))
    put("gmix", np.stack([_cols(inp["norm_mix"][l], 8) for l in range(2)], axis=1))
    put("gffn", np.stack([_cols(inp["norm_ffn"][l], 8) for l in range(2)], axis=1))
    put("gfin", np.broadcast_to(inp["norm_final"][None, :], (128, 1024)))
    cw = inp["lru_conv_w"][0]
    put("lru_cw", np.stack([_cols(cw[k], 8) for k in range(4)], axis=2))
    put("lru_cb", _cols(inp["lru_conv_b"][0], 8))
    for nm, key in (("b_a", "lru_b_a"), ("b_x", "lru_b_x"), ("lam", "lru_lambda")):
        put(nm, np.stack([_cols(inp[key][0, d], 8) for d in range(2)], axis=1))
    for l in range(2):
        cw = inp["ffn_conv_w"][l]
        put("ffn_cw%d" % l, np.stack([_cols(cw[k], NFC) for k in range(3)], axis=2))
        put("ffn_cb%d" % l, _cols(inp["ffn_conv_b"][l], NFC))
    put("ret_norm", _cols(inp["ret_norm"][0], 16))
    put("dlogit", np.broadcast_to(inp["ret_decay_logit"][0].reshape(1, 8), (128, 8)))
    return cp


def rope_tables(S):
    half = 128
    theta = (10000.0 ** (-np.arange(half, dtype=np.float32) / np.float32(half))).astype(np.float32)
    ang = (np.arange(S, dtype=np.float32)[None, :] * theta[:, None]).astype(np.float32)
    c = np.cos(ang.astype(np.float64)).astype(np.float32)
    s = np.sin(ang.astype(np.float64)).astype(np.float32)
    return np.ascontiguousarray(np.stack([c, s, c / 16.0, s / 16.0], axis=0))


import heapq


class Res:
    __slots__ = ("w", "r", "const")

    def __init__(self, const=False):
        self.w = None
        self.r = []
        self.const = const


class Chan:
    __slots__ = ("sem", "cnt", "last")

    def __init__(self, sem):
        self.sem = sem
        self.cnt = 0
        self.last = None


class Op:
    __slots__ = ("idx", "eng", "name", "kw", "deps", "cost", "lat", "ch", "batch", "n", "fin", "npred", "succ",
                 "ready", "cls", "boost")


_ACT_CLASS = {AF.Tanh: 1, AF.Exp: 1, AF.Sqrt: 2, AF.Gelu_apprx_tanh: 3, AF.Silu: 4, AF.Ln: 5, AF.Sigmoid: 6}
NO_LATE = False
PREP_STQ = "sp"
PREP_MOD = 2
TMPS_A = 44
PW_LATE = 1024
GELU_BOOST = 3.0e5
ACT_SWITCH_NS = 1300.0
ACT_STARVE_NS = 6000.0


def _free_size(ap):
    sh = ap.shape
    n = 1
    for v in sh[1:]:
        n *= int(v)
    return n


def _is_psum(ap):
    try:
        return "psum" in str(ap.space).lower() or "ps" == str(ap.tensor.name)[:2]
    except Exception:
        return False


class Prog:
    SELF_SYNC = ("act", "dve", "pool")
    DMA_BW = 200.0
    DMA_LAT = 2000.0

    def __init__(self, nc, es):
        self.nc = nc
        self.es = es
        self.engs = {"pe": nc.tensor, "act": nc.scalar, "dve": nc.vector, "pool": nc.gpsimd, "sp": nc.sync}
        self.sem = {}
        self.cnt = {}
        self.seen = {}
        for e in self.engs:
            self.sem[e] = es.enter_context(nc.semaphore("s_" + e))
            self.cnt[e] = 0
            self.seen[e] = {}
        self.chans = []
        self.free_ch = {}
        self.scope_ch = []
        self.n_ins = 0
        self.pending = []
        self.batch = 0
        self.sim_total = 0.0
        self.act_cls = [0]
        self.crit = None
        self.crit_batch = -1
        self.verbose = False

    def chan(self, kind="sp"):
        fl = self.free_ch.setdefault(kind, [])
        if fl:
            c = fl.pop()
        else:
            c = Chan(self.es.enter_context(self.nc.semaphore("ch%d" % len(self.chans))))
            self.chans.append(c)
        self.scope_ch.append((kind, c))
        return c

    def end_scope(self):
        for kind, c in self.scope_ch:
            self.free_ch.setdefault(kind, []).append(c)
        self.scope_ch = []

    def _cost(self, e, name, kw):
        if e == "pe":
            if name == "transpose":
                n = _free_size(kw["identity"])
            else:
                n = _free_size(kw["rhs"])
            return max(n * 0.42 + 10.0, 56.0), 120.0
        out = kw.get("out", kw.get("ap"))
        n = _free_size(out)
        src = kw.get("in_", kw.get("in0", kw.get("data0", None)))
        ps = 60.0 if (src is not None and _is_psum(src)) else 0.0
        if e == "act":
            return 190.0 + n * 0.86 + ps, 120.0
        if e == "dve":
            f = 2.2 if name == "tensor_tensor_scan" else 1.2
            if out.dtype == BF16 and src is not None and src.dtype == BF16:
                f *= 0.5
            return 70.0 + n * f + ps, 120.0
        if e == "pool":
            f = 4.2 if name == "tensor_copy" else 2.4
            return 120.0 + n * f, 150.0
        return 500.0, 100.0

    def _mkop(self, e, name, kw, reads, writes):
        o = Op()
        o.idx = len(self.pending)
        o.eng = e
        o.name = name
        o.kw = kw
        o.batch = self.batch
        o.ch = None
        o.cls = 0
        o.boost = 0.0
        deps = set()
        for R in reads:
            if R.w is not None:
                deps.add(R.w)
        for R in writes:
            if R.w is not None:
                deps.add(R.w)
            for x in R.r:
                deps.add(x)
        for R in reads:
            if not R.const:
                R.r.append(o)
        for R in writes:
            R.w = o
            R.r = []
        deps.discard(o)
        o.deps = [d for d in deps if d.batch == self.batch]
        self.pending.append(o)
        return o

    def op(self, e, name, kw, reads=(), writes=(), boost=0.0):
        o = self._mkop(e, name, kw, reads, writes)
        o.cost, o.lat = self._cost(e, name, kw)
        o.boost = boost
        if e == "act" and name == "activation":
            o.cls = _ACT_CLASS.get(kw["func"], 0)

    def dma(self, q, out, in_, ch, reads=(), writes=()):
        o = self._mkop(q, "dma_start", dict(out=out, in_=in_), reads, writes)
        o.ch = ch
        if ch.last is not None and ch.last.batch == self.batch and ch.last not in o.deps:
            o.deps.append(ch.last)
        ch.last = o
        nbytes = _free_size(out) * int(out.shape[0]) * (2 if out.dtype == BF16 else 4)
        o.cost = 150.0 if q == "sp" else 600.0
        o.lat = float(nbytes)

    def flush(self):
        ops = self.pending
        self.pending = []
        self.batch += 1
        if not ops:
            return
        for o in ops:
            o.succ = []
            o.npred = len(o.deps)
            o.fin = None
        for o in ops:
            for d in o.deps:
                d.succ.append(o)
        for o in reversed(ops):
            b = 0.0
            for sc in o.succ:
                if sc.ready > b:
                    b = sc.ready
            tl = (o.lat / self.DMA_BW + self.DMA_LAT) if o.ch is not None else o.lat
            o.ready = b + o.cost + tl + o.boost
        for o in ops:
            o.idx = (-o.ready, o.idx)
        free = {e: 0.0 for e in self.engs}
        fut = {e: [] for e in self.engs}
        avl = {e: {} for e in self.engs}
        busy = {e: 0.0 for e in self.engs}
        for o in ops:
            if o.npred == 0:
                o.ready = 0.0
                heapq.heappush(fut[o.eng], (0.0, o.idx, o))
        streams = {e: [] for e in self.engs}
        dma_free = 0.0
        ncnt = dict(self.cnt)
        nleft = len(ops)
        engl = list(self.engs)
        cur_cls = self.act_cls

        def pick(e, t, pop):
            a = avl[e]
            best = None
            bestc = None
            for c, h in a.items():
                if h and (best is None or h[0][0] < best[0]):
                    best, bestc = h[0], c
            if best is None:
                return None
            if e == "act" and bestc not in (0, cur_cls[0]):
                alt, altc = None, None
                for c in (0, cur_cls[0]):
                    h = a.get(c)
                    if h and (alt is None or h[0][0] < alt[0]):
                        alt, altc = h[0], c
                if alt is not None and t - best[1] < ACT_STARVE_NS:
                    best, bestc = alt, altc
            if pop:
                heapq.heappop(a[bestc])
            return best

        while nleft:
            bsel = None
            for e in engl:
                f, t = fut[e], free[e]
                while f and f[0][0] <= t:
                    x = heapq.heappop(f)
                    heapq.heappush(avl[e].setdefault(x[2].cls, []), (x[1], x[0], x[2]))
                c = pick(e, t, False)
                if c is not None:
                    cs, ci = t, c[0]
                elif f:
                    cs, ci = f[0][0], f[0][1]
                else:
                    continue
                if bsel is None or (cs, ci) < bsel[0]:
                    bsel = ((cs, ci), e)
            (cs, ci), e = bsel
            c = pick(e, cs, True)
            if c is not None:
                o = c[2]
            else:
                o = heapq.heappop(fut[e])[2]
            start = cs
            if self.crit is not None:
                bd = None
                for d in o.deps:
                    if bd is None or d.fin > bd.fin:
                        bd = d
                prev = streams[e][-1] if streams[e] else None
                if bd is not None and bd.fin >= cs - 1e-6:
                    self.crit[o] = ("dep", bd, start)
                else:
                    self.crit[o] = ("eng", prev, start)
            if e == "act" and o.cls != 0 and o.cls != cur_cls[0]:
                start += ACT_SWITCH_NS
                busy[e] += ACT_SWITCH_NS
                cur_cls[0] = o.cls
            free[e] = start + o.cost
            busy[e] += o.cost
            if o.ch is not None:
                t0 = max(start + self.DMA_LAT * 0.5, dma_free)
                dma_free = t0 + o.lat / self.DMA_BW
                o.fin = dma_free + self.DMA_LAT * 0.5
            else:
                o.fin = start + o.cost + o.lat
            streams[e].append(o)
            if o.ch is None:
                ncnt[e] += 1
                o.n = ncnt[e]
            else:
                o.ch.cnt += 1
                o.n = o.ch.cnt
            nleft -= 1
            for sc in o.succ:
                sc.npred -= 1
                if sc.npred == 0:
                    r = 0.0
                    for d in sc.deps:
                        f = d.fin - d.lat if (d.eng == "pe" and sc.eng == "pe") else d.fin
                        if f > r:
                            r = f
                    sc.ready = r
                    heapq.heappush(fut[sc.eng], (r, sc.idx, sc))
        span = max([o.fin for o in ops])
        if self.crit is not None and self.batch == self.crit_batch:
            o = max(ops, key=lambda x: x.fin)
            chain = []
            while o is not None and len(chain) < 400:
                kind, p, st = self.crit[o]
                kw = o.kw or {}
                oo = kw.get("out", kw.get("ap"))
                try:
                    nm = oo.tensor.name
                except Exception:
                    nm = "?"
                chain.append("%9.1f %-4s %-22s cost=%6.0f fin=%9.1f via %s -> %s" % (st, o.eng, o.name, o.cost, o.fin, kind, nm))
                o = p
            print("\n".join(reversed(chain)))
        if self.crit is not None:
            self.crit = {}
        self.sim_total += span
        if self.verbose:
            print("[sched] batch %d: %d ops, sim span %.1f us, busy: %s" % (
                self.batch, len(ops), span / 1e3,
                " ".join("%s=%.0f" % (e, busy[e] / 1e3) for e in engl)), flush=True)
        for e in engl:
            eng = self.engs[e]
            seen = self.seen[e]
            for o in streams[e]:
                for d in o.deps:
                    if d.ch is not None:
                        src, sem, v = d.ch, d.ch.sem, 16 * d.n
                    else:
                        if d.eng == e and e not in self.SELF_SYNC:
                            continue
                        src, sem, v = d.eng, self.sem[d.eng], d.n
                    if seen.get(src, 0) < d.n:
                        eng.wait_ge(sem, v)
                        seen[src] = d.n
                        self.n_ins += 1
                ins = getattr(eng, o.name)(**o.kw)
                if o.ch is None:
                    ins.then_inc(self.sem[e], 1)
                    self.cnt[e] = o.n
                else:
                    ins.then_inc(o.ch.sem, 16)
                self.n_ins += 1
                o.kw = None
                o.succ = None

    def barrier(self):
        self.flush()
        for e in self.engs:
            eng = self.engs[e]
            seen = self.seen[e]
            for e2 in self.engs:
                if e2 != e and self.cnt[e2] > seen.get(e2, 0):
                    eng.wait_ge(self.sem[e2], self.cnt[e2])
                    seen[e2] = self.cnt[e2]
            for c in self.chans:
                if c.cnt > seen.get(c, 0):
                    eng.wait_ge(c.sem, 16 * c.cnt)
                    seen[c] = c.cnt
        self.end_scope()


class Buf:
    __slots__ = ("t", "R", "P", "chs")

    def __init__(self, t, P=None):
        self.t = t
        self.R = Res()
        self.P = P
        self.chs = {}

    def chq(self, q):
        c = self.chs.get(q)
        if c is None:
            c = self.P.chan(q)
            self.chs[q] = c
        return c


class Ring:
    def __init__(self, bufs):
        self.bufs = bufs
        self.i = 0

    def next(self):
        b = self.bufs[self.i % len(self.bufs)]
        self.i += 1
        return b


def build_program(S, NSEQ, debug=False, only=None):
    NT = S // T

    def want(p):
        return only is None or p in only
    nc = bass.Bass("TRN2", target_bir_lowering=False)
    dk = "ExternalOutput" if debug else "Internal"
    xin = nc.dram_tensor("xin", [NSEQ, S, D], F32, kind="ExternalInput").ap()
    wpack = nc.dram_tensor("wpack", [128, NW], F32, kind="ExternalInput").ap()
    cpack = nc.dram_tensor("cpack", [128, NCP], F32, kind="ExternalInput").ap()
    rope = nc.dram_tensor("rope", [4, 128, S], F32, kind="ExternalInput").ap()
    yout = nc.dram_tensor("yout", [NSEQ, S, D], F32, kind="ExternalOutput").ap()
    wsc = nc.dram_tensor("wsc", [128, NW], BF16, kind="Internal").ap()
    hb_all = nc.dram_tensor("hb", [NSEQ, NT, 128, 8, T], F32, kind=dk).ap()
    x1s_all = nc.dram_tensor("x1s", [NSEQ, S, D], F32, kind=dk).ap()
    x2s_all = nc.dram_tensor("x2s", [NSEQ, S, D], F32, kind=dk).ap()
    x3s_all = nc.dram_tensor("x3s", [NSEQ, S, D], F32, kind=dk).ap()
    ybs_all = nc.dram_tensor("ybs", [NSEQ, 4, S, 512], F32, kind=dk).ap()
    xnl_all = nc.dram_tensor("xnl", [NSEQ, NT, 128, 8, T], BF16, kind="Internal").ap()
    xnrs_all = nc.dram_tensor("xnrs", [NSEQ, NT, 128, 8, T], BF16, kind="Internal").ap()
    xcs_all = nc.dram_tensor("xcs", [NSEQ, NT, 128, 8, T], F32, kind="Internal").ap()
    qrs_all = nc.dram_tensor("qrs", [NSEQ, 4, NT, 128, 2, T], F32, kind="Internal").ap()
    kTs_all = nc.dram_tensor("kTs", [NSEQ, 4, NT, 128, 2, T], BF16, kind="Internal").ap()
    vbs_all = nc.dram_tensor("vbs", [NSEQ, 4, NT, 128, 4, 512], BF16, kind="Internal").ap()
    ygs_all = nc.dram_tensor("ygs", [NSEQ, NT, 128, 12, T], BF16, kind="Internal").ap()

    ges = ExitStack()
    with ges:
        P = Prog(nc, ges)
        P.verbose = debug

        uid = [0]

        def sb(es, name, shape, dt, ch=False):
            uid[0] += 1
            t = es.enter_context(nc.sbuf_tensor("%s_%d" % (name, uid[0]), shape, dt))
            return Buf(t, P)

        cpb = sb(ges, "cp", [128, NCP], F32, ch=True)
        cpb.R.const = True
        cp = cpb.t

        def cview(name):
            o, sz = C_LAYOUT[name]
            return cp[:, o:o + sz]

        identb = sb(ges, "identb", [128, 128], BF16)
        clru = sb(ges, "clru", [128, 16], F32)
        lgt = sb(ges, "lgt", [128, 8], F32)
        smr = Ring([sb(ges, "smallf%d" % i, [128, 16], F32) for i in range(8)])
        hbias = sb(ges, "hbias", [128, 32], F32)
        clh = sb(ges, "clh", [128, 16], F32)
        negh = sb(ges, "negh", [128, 16], F32)
        banks = [Buf(ges.enter_context(nc.psum_tensor("ps%d" % i, [128, 512], F32)), P) for i in range(8)]
        bring = Ring(banks)
        roles = {}

        def set_roles(cfg):
            roles.clear()
            for names, idxs in cfg:
                r = Ring([banks[i] for i in idxs])
                for nm in names.split(","):
                    roles[nm] = r

        def bk(role):
            return roles[role].next() if role in roles else bring.next()
        wscR = Res()
        hbR = [Res() for _ in range(NT)]
        x1R = [Res() for _ in range(NT)]
        x2R = [Res() for _ in range(NT)]
        x3R = [Res() for _ in range(NT)]
        ybR = [[Res() for _ in range(NT)] for _ in range(4)]
        outR = Res()
        inR = Res()

        P.scope_ch = []
        P.dma("sp", cp[:, :], cpack[:, :], cpb.chq("sp"), writes=[cpb.R])

        PW = 2048
        prep_k = [0]

        def prep_pieces(names, sring, bring2, pw=PW, ldq="sp", stq="pool"):
            for nm in names:
                o0, sz = W_LAYOUT[nm]
                o = o0
                while o < o0 + sz:
                    w = min(pw, o0 + sz - o)
                    s_, b_ = sring.next(), bring2.next()
                    P.dma(ldq, s_.t[:, :w], wpack[:, o:o + w], s_.chq(ldq), writes=[s_.R])
                    if prep_k[0] % PREP_MOD != 1:
                        P.op("act", "activation", dict(out=b_.t[:, :w], in_=s_.t[:, :w], func=AF.Copy),
                             reads=[s_.R], writes=[b_.R])
                    else:
                        P.op("dve", "tensor_copy", dict(out=b_.t[:, :w], in_=s_.t[:, :w]),
                             reads=[s_.R], writes=[b_.R])
                    P.dma(stq, wsc[:, o:o + w], b_.t[:, :w], b_.chq(stq), reads=[b_.R])
                    o += w
                    prep_k[0] += 1

        EARLY_W = ["lru_in", "ga1", "gx1"]
        LATE_A = ["ga0", "gx0", "lru_out", "ffn_in0", "ffn_out0"]
        LATE_B2 = [n for n in W_LAYOUT if n not in EARLY_W and n not in LATE_A]
        with ExitStack() as es:
            sring = Ring([sb(es, "pst%d" % i, [128, PW], F32, ch=True) for i in range(3)])
            bring2 = Ring([sb(es, "pbt%d" % i, [128, PW], BF16, ch=True) for i in range(3)])
            prep_pieces(EARLY_W + LATE_A, sring, bring2)
            P.op("dve", "tensor_copy", dict(out=identb.t[:, :], in_=cview("ident")), reads=[cpb.R],
                 writes=[identb.R])
            P.op("act", "activation", dict(out=clru.t[:, :], in_=cview("lam"), func=AF.Exp, scale=-1.0),
                 reads=[cpb.R], writes=[clru.R])
            P.op("act", "activation", dict(out=clru.t[:, :], in_=clru.t[:, :], func=AF.Ln, bias=1.0),
                 reads=[clru.R], writes=[clru.R])
            P.op("dve", "tensor_scalar", dict(out=clru.t[:, :], in0=clru.t[:, :], scalar1=-8.0, scalar2=None,
                                                  op0=ALU.mult), reads=[clru.R], writes=[clru.R])
            P.op("dve", "tensor_scalar", dict(out=clh.t[:, :], in0=clru.t[:, :], scalar1=0.5, scalar2=None,
                                              op0=ALU.mult), reads=[clru.R], writes=[clh.R])
            P.op("dve", "tensor_scalar", dict(out=hbias.t[:, 0:16], in0=cview("b_a"), scalar1=0.5, scalar2=None,
                                              op0=ALU.mult), reads=[cpb.R], writes=[hbias.R])
            P.op("dve", "tensor_scalar", dict(out=hbias.t[:, 16:32], in0=cview("b_x"), scalar1=0.5, scalar2=None,
                                              op0=ALU.mult), reads=[cpb.R], writes=[hbias.R])
            P.op("dve", "memset", dict(ap=negh.t[:, :], constant=-0.5), writes=[negh.R])
            P.op("act", "activation", dict(out=lgt.t[:, :], in_=cview("dlogit"), func=AF.Exp, scale=-1.0),
                 reads=[cpb.R], writes=[lgt.R])
            P.op("act", "activation", dict(out=lgt.t[:, :], in_=lgt.t[:, :], func=AF.Ln, bias=1.0),
                 reads=[lgt.R], writes=[lgt.R])
            P.op("dve", "tensor_scalar", dict(out=lgt.t[:, :], in0=lgt.t[:, :], scalar1=-1.0, scalar2=None,
                                                  op0=ALU.mult), reads=[lgt.R], writes=[lgt.R])
            P.barrier()

        def load_w(es, name, tag, sub=None):
            o, sz = W_LAYOUT[name]
            if sub is not None:
                o, sz = o + sub[0], sub[1]
            b = sb(es, "w_" + tag, [128, sz], BF16, ch=True)
            P.dma("sp", b.t[:, :], wsc[:, o:o + sz], b.chq("sp"), writes=[b.R])
            return b

        def norm_T(tmps, xt, xR, ntile, nparts, gcol, xn, col_of):
            sm = smr.next()
            ss = sm.t
            for tt in range(ntile):
                jk = tmps.next()
                jv = jk.t[:, :].bitcast(BF16)
                P.op("act", "activation", dict(out=jv[:nparts, 0:1024], in_=xt(tt), func=AF.Square,
                                                   accum_out=ss[:nparts, tt:tt + 1]),
                     reads=[xR], writes=[jk.R, sm.R])
            P.op("dve", "tensor_scalar", dict(out=ss[:nparts, 4:4 + ntile], in0=ss[:nparts, 0:ntile],
                                              scalar1=1.0 / D, scalar2=EPS, op0=ALU.mult, op1=ALU.add),
                 reads=[sm.R], writes=[sm.R])
            P.op("pool", "tensor_tensor", dict(out=ss[:nparts, 8:8 + ntile], in0=ss[:nparts, 4:4 + ntile],
                                               in1=negh.t[:nparts, 0:ntile], op=ALU.pow),
                 reads=[sm.R, negh.R], writes=[sm.R])
            for tt in range(ntile):
                xs = tmps.next()
                xv = xs.t[:, :].bitcast(BF16)
                P.op("dve", "tensor_scalar", dict(out=xv[:nparts, 0:1024], in0=xt(tt),
                                                      scalar1=ss[:nparts, 8 + tt:9 + tt], scalar2=None, op0=ALU.mult),
                     reads=[xR, sm.R], writes=[xs.R])
                bk_ = bk("nt")
                bv = bk_.t[:, :].bitcast(BF16).rearrange("p (k c) -> p k c", k=8)
                for kc in range(8):
                    P.op("pe", "transpose", dict(out=bv[:, kc, 0:nparts], in_=xv[:nparts, kc * 128:(kc + 1) * 128],
                                                     identity=identb.t[:nparts, :nparts]),
                         reads=[xs.R, identb.R], writes=[bk_.R])
                for (c0, n, s0) in col_of(tt):
                    P.op("dve", "tensor_tensor", dict(out=xn.t[:, :, c0:c0 + n], in0=bv[:, :, s0:s0 + n],
                                                          in1=gcol.unsqueeze(2).to_broadcast([128, 8, n]),
                                                          op=ALU.mult),
                         reads=[bk_.R, cpb.R], writes=[xn.R])

        def load_x_and_norm(es_bufs, src, srcR, j, gcol, halo):
            xb, hx, xn, tmps = es_bufs
            s0 = j * T
            P.dma("sp", xb.t[:, :, :], src[s0:s0 + T, :].rearrange("(t p) d -> p t d", p=128), xb.chq("sp"),
                  writes=[xb.R])
            if halo:
                P.op("dve", "memset", dict(ap=hx.t[0:3, :], constant=0.0), writes=[hx.R])
                if j > 0:
                    P.dma("sp", hx.t[0:2, :], src[s0 - 2:s0, :], hx.chq("sp"), writes=[hx.R])
                if j < NT - 1:
                    P.dma("sp", hx.t[2:3, :], src[s0 + T:s0 + T + 1, :], hx.chq("sp"), writes=[hx.R])
                norm_T(tmps, lambda tt: hx.t[0:3, :], hx.R, 1, 3, gcol, xn, lambda tt: [(0, 2, 0), (XW - 1, 1, 2)])
            norm_T(tmps, lambda tt: xb.t[:, tt, :], xb.R, 4, 128, gcol, xn,
                   lambda tt: [(2 + tt * 128, 128, 0)])

        def proj_split(wl, xn, wR):
            bA, bB = bk("ps"), bk("ps")
            for kc in range(8):
                P.op("pe", "matmul", dict(out=bA.t[:, 0:258], lhsT=wl(kc), rhs=xn.t[:, kc, 0:258], start=(kc == 0),
                                              stop=(kc == 7)), reads=[wR, xn.R], writes=[bA.R])
                P.op("pe", "matmul", dict(out=bB.t[:, 0:257], lhsT=wl(kc), rhs=xn.t[:, kc, 258:515], start=(kc == 0),
                                              stop=(kc == 7)), reads=[wR, xn.R], writes=[bB.R])
            return bA, bB

        def proj_main(wl, xn, wR):
            bk_ = bk("pm")
            for kc in range(8):
                P.op("pe", "matmul", dict(out=bk_.t[:, :], lhsT=wl(kc), rhs=xn.t[:, kc, 2:2 + T], start=(kc == 0),
                                              stop=(kc == 7)), reads=[wR, xn.R], writes=[bk_.R])
            return bk_

        def evac_split(bA, bB, tmps):
            ub = tmps.next()
            P.op("act", "activation", dict(out=ub.t[:, 0:258], in_=bA.t[:, 0:258], func=AF.Copy),
                 reads=[bA.R], writes=[ub.R])
            P.op("act", "activation", dict(out=ub.t[:, 258:515], in_=bB.t[:, 0:257], func=AF.Copy),
                 reads=[bB.R], writes=[ub.R])
            return ub

        def out_proj_resid(act_lhs, actR, nk, wout, xb, after_tile=None):
            wv = wout.t[:, :].rearrange("p (k f) -> p k f", k=nk)
            for tt in range(4):
                for hf in range(2):
                    bk_ = bk("op")
                    for kc in range(nk):
                        P.op("pe", "matmul", dict(out=bk_.t[:, :], lhsT=act_lhs(kc, tt),
                                                      rhs=wv[:, kc, hf * 512:(hf + 1) * 512],
                                                      start=(kc == 0), stop=(kc == nk - 1)),
                             reads=(actR if isinstance(actR, list) else [actR]) + [wout.R], writes=[bk_.R])
                    P.op("dve", "tensor_tensor", dict(out=xb.t[:, tt, hf * 512:(hf + 1) * 512], in0=bk_.t[:, :],
                                                          in1=xb.t[:, tt, hf * 512:(hf + 1) * 512], op=ALU.add),
                         reads=[bk_.R, xb.R], writes=[xb.R])
                if after_tile is not None:
                    after_tile(tt)

        def store_x(xb, dst, dstR, j):
            s0 = j * T
            P.dma("pool", dst[s0:s0 + T, :].rearrange("(t p) d -> p t d", p=128), xb.t[:, :, :], xb.chq("pool"),
                  reads=[xb.R])

        for _once in (0,):
            pass
            xinR = [inR] * NT

            for pas in ("A", "B1"):
                if not want(pas):
                    continue
                with ExitStack() as es:
                    d = 1 if pas == "A" else 0
                    if pas == "A":
                        set_roles([("nt,ps", [0, 1, 2, 3]), ("g", [4, 5, 6, 7])])
                    else:
                        set_roles([("pm", [0, 1]), ("g", [2, 3, 4, 5]), ("op", [6, 7])])
                    w_in = load_w(es, "lru_in", "lin", sub=((8 * 8 * 128, 8 * 8 * 128) if pas == "A" else (0, 8 * 8 * 128)))
                    w_ga = load_w(es, "ga%d" % d, "ga")
                    w_gx = load_w(es, "gx%d" % d, "gx")
                    w_out = load_w(es, "lru_out", "lout") if pas == "B1" else None
                    winv = w_in.t[:, :].rearrange("p (fc kc f) -> p fc kc f", fc=8, kc=8)
                    gav = w_ga.t[:, :].rearrange("p (n kc oc f) -> p n kc oc f", n=4, kc=2, oc=2)
                    gxv = w_gx.t[:, :].rearrange("p (n kc oc f) -> p n kc oc f", n=4, kc=2, oc=2)
                    xring = Ring([sb(es, "x%d" % i, [128, 4, D], F32, ch=True) for i in range(2)])
                    hx = sb(es, "hx", [4, D], F32, ch=True) if pas == "A" else None
                    xnr = Ring([sb(es, "xn%d" % i, [128, 8, XW], BF16, ch=True) for i in range(2)])
                    tmps = Ring([sb(es, "tmp%d" % i, [128, TMPW], F32, ch=True)
                                 for i in range(TMPS_A if pas == "A" else 21)])
                    psr = pbr = None
                    xcr = Ring([sb(es, "xc%d" % i, [128, 2, T], F32, ch=True) for i in range(4 if pas == "A" else 3)])
                    xcbr = Ring([sb(es, "xcb%d" % i, [128, 2, T], BF16) for i in range(3)])
                    hgr = Ring([sb(es, "hg%d" % i, [128, 8, T], BF16) for i in range(2)]) if pas == "B1" else None
                    gtr = Ring([sb(es, "gt%d" % i, [128, T], F32) for i in range(8)]) if pas == "B1" else None
                    carry = sb(es, "carry", [128, 8], F32)
                    carryR = [Res() for _ in range(8)]
                    cwv = cview("lru_cw").rearrange("p (k t) -> p k t", k=8)
                    cbv = cview("lru_cb")
                    bav = cview("b_a")
                    bxv = cview("b_x")
                    order = range(NT - 1, -1, -1) if pas == "A" else range(NT)
                    for seq in range(NSEQ):
                        hb, x1s, x2s, x3s, ybs = hb_all[seq], x1s_all[seq], x2s_all[seq], x3s_all[seq], ybs_all[seq]
                        xnl, xnrs, xcs = xnl_all[seq], xnrs_all[seq], xcs_all[seq]
                        qrs, kTs, vbs, ygs = qrs_all[seq], kTs_all[seq], vbs_all[seq], ygs_all[seq]
                        xsrc = xin[seq]
                        P.op("dve", "memset", dict(ap=carry.t[:, :], constant=0.0), writes=[carry.R] + carryR)
                        for j in order:
                            xb, xn = xring.next(), xnr.next()
                            hg = hgr.next() if pas == "B1" else None
                            if pas == "A":
                                load_x_and_norm((xb, hx, xn, tmps), xsrc, xinR, j, cview("gmix")[:, 0:8], True)
                                P.dma("pool", xnl[j], xn.t[:, :, 2:2 + T], xn.chq("pool"), reads=[xn.R])
                            else:
                                P.dma("sp", xb.t[:, :, :], xsrc[j * T:(j + 1) * T, :].rearrange("(t p) d -> p t d", p=128),
                                      xb.chq("sp"), writes=[xb.R])
                                P.dma("sp", xn.t[:, :, 2:2 + T], xnl[j], xn.chq("sp"), writes=[xn.R])
                            for n in range(4):
                                xc, xcb = xcr.next(), xcbr.next()
                                if pas == "B1":
                                    P.dma("sp", xc.t[:, :, :], xcs[j, :, 2 * n:2 * n + 2, :], xc.chq("sp"), writes=[xc.R])
                                for o2 in range(2):
                                    c = 2 * n + o2
                                    if pas == "B1":
                                        P.op("dve", "tensor_copy", dict(out=xcb.t[:, o2, :], in_=xc.t[:, o2, :]),
                                             reads=[xc.R], writes=[xcb.R])
                                        continue
                                    bA, bB = proj_split(lambda kc: winv[:, c, kc, :], xn, w_in.R)
                                    ub = evac_split(bA, bB, tmps)
                                    P.op("dve", "tensor_scalar", dict(out=xc.t[:, o2, :], in0=ub.t[:, 0:T],
                                                                          scalar1=cwv[:, c, 0:1], scalar2=cbv[:, c:c + 1],
                                                                          op0=ALU.mult, op1=ALU.add),
                                         reads=[ub.R, cpb.R], writes=[xc.R])
                                    for k in range(1, 4):
                                        P.op("dve", "scalar_tensor_tensor", dict(out=xc.t[:, o2, :],
                                                                                     in0=ub.t[:, k:k + T],
                                                                                     scalar=cwv[:, c, k:k + 1],
                                                                                     in1=xc.t[:, o2, :], op0=ALU.mult,
                                                                                     op1=ALU.add),
                                             reads=[ub.R, cpb.R, xc.R], writes=[xc.R])
                                    P.op("act", "activation", dict(out=xcb.t[:, o2, :], in_=xc.t[:, o2, :], func=AF.Copy),
                                         reads=[xc.R], writes=[xcb.R])
                                if pas == "A":
                                    P.dma("pool", xcs[j, :, 2 * n:2 * n + 2, :], xc.t[:, :, :], xc.chq("pool"), reads=[xc.R])
                                for o2 in range(2):
                                    c = 2 * n + o2
                                    pa, px = bk("g"), bk("g")
                                    for kc in range(2):
                                        P.op("pe", "matmul", dict(out=pa.t[:, :], lhsT=gav[:, n, kc, o2, :],
                                                                      rhs=xcb.t[:, kc, :], start=(kc == 0),
                                                                      stop=(kc == 1)),
                                             reads=[w_ga.R, xcb.R], writes=[pa.R])
                                    for kc in range(2):
                                        P.op("pe", "matmul", dict(out=px.t[:, :], lhsT=gxv[:, n, kc, o2, :],
                                                                      rhs=xcb.t[:, kc, :], start=(kc == 0),
                                                                      stop=(kc == 1)),
                                             reads=[w_gx.R, xcb.R], writes=[px.R])
                                    r_, i_, a_, s_, u_, h_ = [tmps.next() for _ in range(6)]
                                    ia, ix = d * 8 + c, 16 + d * 8 + c
                                    P.op("act", "activation", dict(out=r_.t[:, 0:T], in_=pa.t[:, :], func=AF.Tanh,
                                                                   scale=0.5, bias=hbias.t[:, ia:ia + 1]),
                                         reads=[pa.R, hbias.R], writes=[r_.R])
                                    P.op("act", "activation", dict(out=i_.t[:, 0:T], in_=px.t[:, :], func=AF.Tanh,
                                                                   scale=0.5, bias=hbias.t[:, ix:ix + 1]),
                                         reads=[px.R, hbias.R], writes=[i_.R])
                                    P.op("act", "activation", dict(out=a_.t[:, 0:T], in_=r_.t[:, 0:T], func=AF.Exp,
                                                                   scale=clh.t[:, ia:ia + 1], bias=clh.t[:, ia:ia + 1]),
                                         reads=[r_.R, clh.R], writes=[a_.R])
                                    P.op("act", "activation", dict(out=s_.t[:, 0:T], in_=a_.t[:, 0:T], func=AF.Square),
                                         reads=[a_.R], writes=[s_.R])
                                    P.op("act", "activation", dict(out=s_.t[:, 0:T], in_=s_.t[:, 0:T], func=AF.Sqrt,
                                                                   scale=-0.25, bias=0.25),
                                         reads=[s_.R], writes=[s_.R])
                                    P.op("dve", "scalar_tensor_tensor", dict(out=u_.t[:, 0:T], in0=i_.t[:, 0:T],
                                                                             scalar=1.0, in1=xc.t[:, o2, :],
                                                                             op0=ALU.add, op1=ALU.mult),
                                         reads=[i_.R, xc.R], writes=[u_.R])
                                    P.op("dve", "tensor_tensor", dict(out=u_.t[:, 0:T], in0=u_.t[:, 0:T],
                                                                      in1=s_.t[:, 0:T], op=ALU.mult),
                                         reads=[u_.R, s_.R], writes=[u_.R])
                                    if pas == "A":
                                        P.op("dve", "tensor_tensor_scan", dict(out=h_.t[:, 0:T][:, ::-1],
                                                                                   data0=a_.t[:, 0:T][:, ::-1],
                                                                                   data1=u_.t[:, 0:T][:, ::-1],
                                                                                   initial=carry.t[:, c:c + 1],
                                                                                   op0=ALU.mult, op1=ALU.add),
                                             reads=[a_.R, u_.R, carryR[c]], writes=[h_.R])
                                        P.op("pool", "tensor_copy", dict(out=carry.t[:, c:c + 1], in_=h_.t[:, 0:1]),
                                             reads=[h_.R], writes=[carryR[c]])
                                        P.dma("pool", hb[j, :, c, :], h_.t[:, 0:T], h_.chq("pool"), reads=[h_.R])
                                    else:
                                        hbt, gt = tmps.next(), gtr.next()
                                        P.dma("sp", hbt.t[:, 0:T], hb[j, :, c, :], hbt.chq("sp"), writes=[hbt.R])
                                        P.op("dve", "tensor_tensor_scan", dict(out=h_.t[:, 0:T], data0=a_.t[:, 0:T],
                                                                                   data1=u_.t[:, 0:T],
                                                                                   initial=carry.t[:, c:c + 1],
                                                                                   op0=ALU.mult, op1=ALU.add),
                                             reads=[a_.R, u_.R, carryR[c]], writes=[h_.R])
                                        P.op("dve", "tensor_copy", dict(out=carry.t[:, c:c + 1], in_=h_.t[:, T - 1:T]),
                                             reads=[h_.R], writes=[carryR[c]])
                                        pg = proj_main(lambda kc: winv[:, c, kc, :], xn, w_in.R)
                                        P.op("act", "activation", dict(out=gt.t[:, 0:T], in_=pg.t[:, :],
                                                                           func=AF.Gelu_apprx_tanh),
                                             reads=[pg.R], writes=[gt.R], boost=GELU_BOOST)
                                        P.op("dve", "tensor_tensor", dict(out=h_.t[:, 0:T], in0=h_.t[:, 0:T],
                                                                          in1=hbt.t[:, 0:T], op=ALU.add),
                                             reads=[h_.R, hbt.R], writes=[h_.R])
                                        P.op("dve", "tensor_tensor", dict(out=hg.t[:, c, :], in0=h_.t[:, 0:T],
                                                                              in1=gt.t[:, 0:T], op=ALU.mult),
                                             reads=[h_.R, gt.R], writes=[hg.R])
                            if pas == "B1":
                                out_proj_resid(lambda kc, tt: hg.t[:, kc, tt * 128:(tt + 1) * 128], hg.R, 8, w_out, xb)
                                store_x(xb, x1s, x1R, j)
                    if pas == "A":
                        prep_pieces([], psr, pbr, PW_LATE)
                    P.barrier()

            def ffn_pass(l, src_all, srcR, dst_all, dstR, final):
                with ExitStack() as es:
                    set_roles([])
                    w_out = load_w(es, "ffn_out%d" % l, "fout")
                    o_in, _ = W_LAYOUT["ffn_in%d" % l]
                    SL = 4 * 8 * 128
                    slabs = Ring([sb(es, "slab%d" % i, [128, SL], BF16, ch=True) for i in range(3)])
                    xring = Ring([sb(es, "x%d" % i, [128, 4, D], F32, ch=True) for i in range(2)])
                    hx = sb(es, "hx", [4, D], F32, ch=True)
                    xnr = Ring([sb(es, "xn%d" % i, [128, 8, XW], BF16) for i in range(2)])
                    tmps = Ring([sb(es, "tmp%d" % i, [128, TMPW], F32, ch=True) for i in range(16)])
                    actT = sb(es, "actT", [128, NFC, T], BF16)
                    if l == 0:
                        psr = Ring([sb(es, "pst%d" % i, [128, PW_LATE], F32, ch=True) for i in range(2)])
                        pbr = Ring([sb(es, "pbt%d" % i, [128, PW_LATE], BF16, ch=True) for i in range(2)])
                    oring = Ring([sb(es, "ot%d" % i, [128, D], F32, ch=True) for i in range(2)]) if final else None
                    cwv = cview("ffn_cw%d" % l).rearrange("p (k t) -> p k t", k=NFC)
                    cbv = cview("ffn_cb%d" % l)
                    gfv = cview("gfin")
                    for seq in range(NSEQ):
                        hb, x1s, x2s, x3s, ybs = hb_all[seq], x1s_all[seq], x2s_all[seq], x3s_all[seq], ybs_all[seq]
                        xnl, xnrs, xcs = xnl_all[seq], xnrs_all[seq], xcs_all[seq]
                        qrs, kTs, vbs, ygs = qrs_all[seq], kTs_all[seq], vbs_all[seq], ygs_all[seq]
                        xsrc = xin[seq]
                        src, dst = src_all[seq], dst_all[seq]
                        for j in range(NT):
                            xb, xn = xring.next(), xnr.next()
                            load_x_and_norm((xb, hx, xn, tmps), src, srcR, j, cview("gffn")[:, l * 8:(l + 1) * 8], True)
                            for s in range(11):
                                sl = slabs.next()
                                P.dma("sp", sl.t[:, :], wsc[:, o_in + s * SL:o_in + (s + 1) * SL], sl.chq("sp"), writes=[sl.R])
                                slv = sl.t[:, :].rearrange("p (fc kc f) -> p fc kc f", fc=4, kc=8)
                                for q in range(2):
                                    fc = 2 * s + q
                                    bA, bB = proj_split(lambda kc: slv[:, q, kc, :], xn, sl.R)
                                    ub = evac_split(bA, bB, tmps)
                                    cv, gu = tmps.next(), tmps.next()
                                    P.op("dve", "tensor_scalar", dict(out=cv.t[:, 0:T], in0=ub.t[:, 1:1 + T],
                                                                          scalar1=cwv[:, fc, 0:1],
                                                                          scalar2=cbv[:, fc:fc + 1],
                                                                          op0=ALU.mult, op1=ALU.add),
                                         reads=[ub.R, cpb.R], writes=[cv.R])
                                    for k in range(1, 3):
                                        P.op("dve", "scalar_tensor_tensor", dict(out=cv.t[:, 0:T],
                                                                                     in0=ub.t[:, 1 + k:1 + k + T],
                                                                                     scalar=cwv[:, fc, k:k + 1],
                                                                                     in1=cv.t[:, 0:T], op0=ALU.mult,
                                                                                     op1=ALU.add),
                                             reads=[ub.R, cpb.R, cv.R], writes=[cv.R])
                                    P.op("act", "activation", dict(out=gu.t[:, 0:T], in_=cv.t[:, 0:T],
                                                                       func=AF.Gelu_apprx_tanh),
                                         reads=[cv.R], writes=[gu.R])
                                    pv = proj_main(lambda kc: slv[:, 2 + q, kc, :], xn, sl.R)
                                    P.op("dve", "tensor_tensor", dict(out=actT.t[:, fc, :], in0=pv.t[:, :],
                                                                          in1=gu.t[:, 0:T], op=ALU.mult),
                                         reads=[pv.R, gu.R], writes=[actT.R])

                            def fin_tile(tt):
                                ot = oring.next()
                                jk = tmps.next()
                                jv = jk.t[:, :].bitcast(BF16)
                                sm = smr.next()
                                ss = sm.t
                                P.op("act", "activation", dict(out=jv[:, 0:1024], in_=xb.t[:, tt, :], func=AF.Square,
                                                                   accum_out=ss[:, 12:13]),
                                     reads=[xb.R], writes=[jk.R, sm.R])
                                P.op("dve", "tensor_scalar", dict(out=ss[:, 13:14], in0=ss[:, 12:13], scalar1=1.0 / D,
                                                                  scalar2=EPS, op0=ALU.mult, op1=ALU.add),
                                     reads=[sm.R], writes=[sm.R])
                                P.op("pool", "tensor_tensor", dict(out=ss[:, 14:15], in0=ss[:, 13:14],
                                                                   in1=negh.t[:, 0:1], op=ALU.pow),
                                     reads=[sm.R, negh.R], writes=[sm.R])
                                P.op("dve", "scalar_tensor_tensor", dict(out=ot.t[:, :], in0=xb.t[:, tt, :],
                                                                             scalar=ss[:, 14:15], in1=gfv,
                                                                             op0=ALU.mult, op1=ALU.mult),
                                     reads=[xb.R, sm.R, cpb.R], writes=[ot.R])
                                r0 = j * T + tt * 128
                                P.dma("pool", dst[r0:r0 + 128, :], ot.t[:, :], ot.chq("pool"), reads=[ot.R])

                            out_proj_resid(lambda kc, tt: actT.t[:, kc, tt * 128:(tt + 1) * 128], actT.R, NFC, w_out, xb,
                                           after_tile=fin_tile if final else None)
                            if not final:
                                store_x(xb, dst, dstR, j)
                    if l == 0:
                        prep_pieces(LATE_B2 if not NO_LATE else [], psr, pbr, PW_LATE, ldq="pool", stq="pool")
                    P.barrier()

            if want("B2"):
                ffn_pass(0, x1s_all, x1R, x2s_all, x2R, False)

            for pas in ("C", "D"):
                d = 1 if pas == "C" else 0
                for h in range(4):
                    if not want(pas):
                        continue
                    with ExitStack() as es:
                        set_roles([("nt,pm,op", [0, 1]), ("sc,tk", [2, 3]), ("po,ty", [4, 5]), ("su", [6, 7])])
                        if pas == "C":
                            w_qkg = load_w(es, "ret_qkg%d" % h, "qkg", sub=(0, 4 * 8 * 128))
                            w_v = load_w(es, "ret_v%d" % h, "wv")
                            w_o = None
                            wvv = w_v.t[:, :].rearrange("p (kc f) -> p kc f", kc=8)
                        else:
                            w_qkg = load_w(es, "ret_qkg%d" % h, "qkg", sub=(4 * 8 * 128, 4 * 8 * 128))
                            w_v = None
                            w_o = load_w(es, "ret_out0", "wo", sub=(0, 16 * 1024)) if h == 3 else None
                        qkgv = w_qkg.t[:, :].rearrange("p (fc kc f) -> p fc kc f", fc=4, kc=8)
                        first = (pas == "C" and h == 0)
                        xring = Ring([sb(es, "x%d" % i, [128, 4, D], F32, ch=True) for i in range(2)]) \
                            if first else None
                        xrr = Ring([sb(es, "xr%d" % i, [128, 4, D], F32, ch=True) for i in range(2)]) \
                            if (pas == "D" and h == 3) else None
                        ygor = Ring([sb(es, "ygo%d" % i, [128, 12, T], BF16, ch=True) for i in range(2)]) \
                            if (pas == "D" and h == 3) else None
                        xnr = Ring([sb(es, "xn%d" % i, [128, 8, XW], BF16, ch=True) for i in range(2)])
                        tmps = Ring([sb(es, "tmp%d" % i, [128, TMPW], F32, ch=True)
                                     for i in range(12 if pas == "D" else 16)])
                        rpr = Ring([sb(es, "rp%d" % i, [128, 4, T], F32, ch=True) for i in range(2)]) \
                            if pas == "C" else None
                        qrr = Ring([sb(es, "qr%d" % i, [128, 2, T], F32, ch=True) for i in range(2)])
                        qxr = Ring([sb(es, "qx%d" % i, [128, 2, T], BF16) for i in range(2)])
                        kTr = Ring([sb(es, "kT%d" % i, [128, 2, T], BF16, ch=True) for i in range(2)])
                        vbr = Ring([sb(es, "vb%d" % i, [128, 4, 512], BF16, ch=True) for i in range(2)])
                        kzr = Ring([sb(es, "kz%d" % i, [128, 256], BF16) for i in range(4)])
                        scr_ = Ring([sb(es, "sT%d" % i, [128, 128], BF16) for i in range(3)])
                        st = sb(es, "st", [128, 2, 512], F32)
                        stb = sb(es, "stb", [128, 2, 512], BF16)
                        ygr = Ring([sb(es, "ygT%d" % i, [128, 4, T], BF16, ch=True) for i in range(2)]) \
                            if pas == "D" else None
                        sg = sb(es, "sg", [128, 4, T], F32) if pas == "D" else None
                        tabs = sb(es, "tabs", [128, 128 + 128 + 4], F32)
                        maskT = tabs.t[:, 0:128]
                        xit = tabs.t[:, 128:256]
                        zcol = tabs.t[:, 256:257]
                        g128 = tabs.t[:, 257:258]
                        mcol = tabs.t[:, 258:259]
                        lg = lgt.t[:, d * 4 + h:d * 4 + h + 1]
                        pc = cview("pcols")
                        tri = cview("tri").rearrange("p (d n) -> p d n", d=2)
                        nrow = cview("nrow").rearrange("p (d n) -> p d n", d=2)
                        di = 0 if pas == "D" else 1
                        P.op("act", "activation", dict(out=mcol, in_=pc[:, di:di + 1], func=AF.Exp, scale=lg),
                             reads=[cpb.R, lgt.R], writes=[tabs.R])
                        P.op("act", "activation", dict(out=zcol, in_=pc[:, 2 + di:3 + di], func=AF.Exp, scale=lg),
                             reads=[cpb.R, lgt.R], writes=[tabs.R])
                        P.op("act", "activation", dict(out=xit, in_=nrow[:, di, :], func=AF.Exp, scale=lg),
                             reads=[cpb.R, lgt.R], writes=[tabs.R])
                        P.op("act", "activation", dict(out=g128, in_=lg, func=AF.Exp, scale=128.0),
                             reads=[lgt.R], writes=[tabs.R])
                        P.op("dve", "tensor_scalar", dict(out=maskT, in0=tri[:, di, :], scalar1=mcol, scalar2=None,
                                                              op0=ALU.mult), reads=[cpb.R, tabs.R], writes=[tabs.R])
                        gnv = cview("ret_norm")
                        order = range(NT - 1, -1, -1) if pas == "C" else range(NT)
                        for seq in range(NSEQ):
                            hb, x1s, x2s, x3s, ybs = hb_all[seq], x1s_all[seq], x2s_all[seq], x3s_all[seq], ybs_all[seq]
                            xnl, xnrs, xcs = xnl_all[seq], xnrs_all[seq], xcs_all[seq]
                            qrs, kTs, vbs, ygs = qrs_all[seq], kTs_all[seq], vbs_all[seq], ygs_all[seq]
                            xsrc = xin[seq]
                            P.op("dve", "memset", dict(ap=st.t[:, :, :], constant=0.0), writes=[st.R])
                            P.op("dve", "memset", dict(ap=stb.t[:, :, :], constant=0.0), writes=[stb.R])
                            for j in order:
                                xn = xnr.next()
                                qx, kT, vb, qr = qxr.next(), kTr.next(), vbr.next(), qrr.next()
                                ygT = ygr.next() if pas == "D" else None
                                s0 = j * T
                                xib = xit.unsqueeze(1).to_broadcast([128, 4, 128])
                                if pas == "C":
                                    rp = rpr.next()
                                    P.dma("sp", rp.t[:, :, :], rope[:, :, s0:s0 + T].rearrange("r p t -> p r t"), rp.chq("sp"),
                                          writes=[rp.R])
                                if first:
                                    xb = xring.next()
                                    load_x_and_norm((xb, None, xn, tmps), x2s, x2R, j, cview("gmix")[:, 8:16], False)
                                    P.dma("pool", xnrs[j], xn.t[:, :, 2:2 + T], xn.chq("pool"), reads=[xn.R])
                                else:
                                    P.dma("sp", xn.t[:, :, 2:2 + T], xnrs[j], xn.chq("sp"), writes=[xn.R])
                                if pas == "D":
                                    P.dma("sp", qr.t[:, :, :], qrs[h, j], qr.chq("sp"), writes=[qr.R])
                                    P.dma("sp", kT.t[:, :, :], kTs[h, j], kT.chq("sp"), writes=[kT.R])
                                    P.dma("sp", vb.t[:, :, :], vbs[h, j], vb.chq("sp"), writes=[vb.R])
                                    for c2 in range(2):
                                        P.op("dve", "tensor_tensor", dict(
                                            out=qx.t[:, c2, :].rearrange("p (c n) -> p c n", c=4),
                                            in0=qr.t[:, c2, :].rearrange("p (c n) -> p c n", c=4), in1=xib, op=ALU.mult),
                                             reads=[qr.R, tabs.R], writes=[qx.R])
                                for qk in (range(2) if pas == "C" else ()):
                                    p0 = proj_main(lambda kc: qkgv[:, 2 * qk, kc, :], xn, w_qkg.R)
                                    p1 = proj_main(lambda kc: qkgv[:, 2 * qk + 1, kc, :], xn, w_qkg.R)
                                    cs, sn = rp.t[:, 2 * qk, :], rp.t[:, 2 * qk + 1, :]
                                    e0, e1 = tmps.next(), tmps.next()
                                    t1, t2, t3, t4 = [tmps.next() for _ in range(4)]
                                    dst = qx if qk == 0 else kT
                                    P.op("act", "activation", dict(out=e0.t[:, 0:T], in_=p0.t[:, :], func=AF.Copy),
                                         reads=[p0.R], writes=[e0.R])
                                    P.op("act", "activation", dict(out=e1.t[:, 0:T], in_=p1.t[:, :], func=AF.Copy),
                                         reads=[p1.R], writes=[e1.R])
                                    P.op("dve", "tensor_tensor", dict(out=t1.t[:, 0:T], in0=e0.t[:, 0:T], in1=cs,
                                                                      op=ALU.mult), reads=[e0.R, rp.R], writes=[t1.R])
                                    P.op("dve", "tensor_tensor", dict(out=t2.t[:, 0:T], in0=e1.t[:, 0:T], in1=sn,
                                                                      op=ALU.mult), reads=[e1.R, rp.R], writes=[t2.R])
                                    P.op("dve", "tensor_tensor", dict(out=t3.t[:, 0:T], in0=e1.t[:, 0:T], in1=cs,
                                                                      op=ALU.mult), reads=[e1.R, rp.R], writes=[t3.R])
                                    P.op("dve", "tensor_tensor", dict(out=t4.t[:, 0:T], in0=e0.t[:, 0:T], in1=sn,
                                                                      op=ALU.mult), reads=[e0.R, rp.R], writes=[t4.R])
                                    if qk == 0:
                                        P.op("dve", "tensor_tensor", dict(out=qr.t[:, 0, :], in0=t1.t[:, 0:T],
                                                                              in1=t2.t[:, 0:T], op=ALU.subtract),
                                             reads=[t1.R, t2.R], writes=[qr.R])
                                        P.op("dve", "tensor_tensor", dict(out=qr.t[:, 1, :], in0=t3.t[:, 0:T],
                                                                              in1=t4.t[:, 0:T], op=ALU.add),
                                             reads=[t3.R, t4.R], writes=[qr.R])
                                        P.dma("pool", qrs[h, j], qr.t[:, :, :], qr.chq("pool"), reads=[qr.R])
                                        for c2 in range(2):
                                            P.op("dve", "tensor_tensor", dict(
                                                out=qx.t[:, c2, :].rearrange("p (c n) -> p c n", c=4),
                                                in0=qr.t[:, c2, :].rearrange("p (c n) -> p c n", c=4), in1=xib,
                                                op=ALU.mult), reads=[qr.R, tabs.R], writes=[qx.R])
                                    else:
                                        P.op("dve", "tensor_tensor", dict(out=kT.t[:, 0, :], in0=t1.t[:, 0:T],
                                                                              in1=t2.t[:, 0:T], op=ALU.subtract),
                                             reads=[t1.R, t2.R], writes=[kT.R])
                                        P.op("dve", "tensor_tensor", dict(out=kT.t[:, 1, :], in0=t3.t[:, 0:T],
                                                                              in1=t4.t[:, 0:T], op=ALU.add),
                                             reads=[t3.R, t4.R], writes=[kT.R])
                                        P.dma("pool", kTs[h, j], kT.t[:, :, :], kT.chq("pool"), reads=[kT.R])
                                for tt in (range(4) if pas == "C" else ()):
                                    bk_ = bk("pm")
                                    for kc in range(8):
                                        P.op("pe", "matmul", dict(out=bk_.t[:, :],
                                                                      lhsT=xn.t[:, kc, 2 + tt * 128:2 + (tt + 1) * 128],
                                                                      rhs=wvv[:, kc, :], start=(kc == 0), stop=(kc == 7)),
                                             reads=[xn.R, w_v.R], writes=[bk_.R])
                                    P.op("act", "activation", dict(out=vb.t[:, tt, :], in_=bk_.t[:, :], func=AF.Copy),
                                         reads=[bk_.R], writes=[vb.R])
                                if pas == "C":
                                    P.dma("pool", vbs[h, j], vb.t[:, :, :], vb.chq("pool"), reads=[vb.R])
                                if pas == "D":
                                    for fc in range(4):
                                        pg = proj_main(lambda kc: qkgv[:, fc, kc, :], xn, w_qkg.R)
                                        P.op("act", "activation", dict(out=sg.t[:, fc, :], in_=pg.t[:, :],
                                                                           func=AF.Silu), reads=[pg.R], writes=[sg.R])
                                        P.op("dve", "tensor_scalar", dict(out=sg.t[:, fc, :], in0=sg.t[:, fc, :],
                                                                              scalar1=gnv[:, 4 * h + fc:4 * h + fc + 1],
                                                                              scalar2=None, op0=ALU.mult),
                                             reads=[sg.R, cpb.R], writes=[sg.R])
                                    if h == 3:
                                        xr, ygo = xrr.next(), ygor.next()
                                        P.dma("sp", xr.t[:, :, :], x2s[s0:s0 + T, :].rearrange("(t p) d -> p t d", p=128),
                                              xr.chq("sp"), writes=[xr.R])
                                        P.dma("sp", ygo.t[:, :, :], ygs[j], ygo.chq("sp"), writes=[ygo.R])
                                corder = range(3, -1, -1) if pas == "C" else range(4)
                                for cc in corder:
                                    csl = slice(cc * 128, (cc + 1) * 128)
                                    r0 = s0 + cc * 128
                                    ps_ = bk("sc")
                                    for c2 in range(2):
                                        P.op("pe", "matmul", dict(out=ps_.t[:, 0:128], lhsT=kT.t[:, c2, csl],
                                                                      rhs=qx.t[:, c2, csl], start=(c2 == 0),
                                                                      stop=(c2 == 1)),
                                             reads=[kT.R, qx.R], writes=[ps_.R])
                                    sT = scr_.next()
                                    P.op("dve", "tensor_tensor", dict(out=sT.t[:, :], in0=ps_.t[:, 0:128], in1=maskT,
                                                                          op=ALU.mult),
                                         reads=[ps_.R, tabs.R], writes=[sT.R])
                                    po = bk("po")
                                    P.op("pe", "matmul", dict(out=po.t[:, :], lhsT=sT.t[:, :], rhs=vb.t[:, cc, :],
                                                                  start=True, stop=False),
                                         reads=[sT.R, vb.R], writes=[po.R])
                                    for c2 in range(2):
                                        P.op("pe", "matmul", dict(out=po.t[:, :], lhsT=qx.t[:, c2, csl],
                                                                      rhs=stb.t[:, c2, :], start=False, stop=(c2 == 1)),
                                             reads=[qx.R, stb.R], writes=[po.R])
                                    pk = bk("tk")
                                    pkv = pk.t[:, :].bitcast(BF16)
                                    for c2 in range(2):
                                        P.op("pe", "transpose", dict(out=pkv[:, c2 * 128:(c2 + 1) * 128],
                                                                         in_=kT.t[:, c2, csl], identity=identb.t[:, :]),
                                             reads=[kT.R, identb.R], writes=[pk.R])
                                    kz = kzr.next()
                                    P.op("act", "activation", dict(out=kz.t[:, :], in_=pkv[:, 0:256], func=AF.Copy,
                                                                       scale=zcol), reads=[pk.R, tabs.R], writes=[kz.R])
                                    yt = tmps.next()
                                    if pas == "C":
                                        P.op("act", "activation", dict(out=yt.t[:, 0:512], in_=po.t[:, :],
                                                                           func=AF.Copy), reads=[po.R], writes=[yt.R])
                                        P.dma("pool", ybs[h, r0:r0 + 128, :], yt.t[:, 0:512], yt.chq("pool"), reads=[yt.R])
                                    else:
                                        ybt = tmps.next()
                                        P.dma("sp", ybt.t[:, 0:512], ybs[h, r0:r0 + 128, :], ybt.chq("sp"), writes=[ybt.R])
                                        P.op("dve", "tensor_tensor", dict(out=yt.t[:, 0:512], in0=po.t[:, :],
                                                                              in1=ybt.t[:, 0:512], op=ALU.add),
                                             reads=[po.R, ybt.R], writes=[yt.R])
                                        jk = tmps.next()
                                        sm = smr.next()
                                        ss = sm.t
                                        P.op("act", "activation", dict(out=jk.t[:, 0:512], in_=yt.t[:, 0:512],
                                                                           func=AF.Square, accum_out=ss[:, 12:13]),
                                             reads=[yt.R], writes=[jk.R, sm.R])
                                        P.op("dve", "tensor_scalar", dict(out=ss[:, 13:14], in0=ss[:, 12:13],
                                                                          scalar1=1.0 / 512, scalar2=EPS,
                                                                          op0=ALU.mult, op1=ALU.add),
                                             reads=[sm.R], writes=[sm.R])
                                        P.op("pool", "tensor_tensor", dict(out=ss[:, 14:15], in0=ss[:, 13:14],
                                                                           in1=negh.t[:, 0:1], op=ALU.pow),
                                             reads=[sm.R, negh.R], writes=[sm.R])
                                        ynb = tmps.next()
                                        ynv = ynb.t[:, :].bitcast(BF16)
                                        P.op("act", "activation", dict(out=ynv[:, 0:512], in_=yt.t[:, 0:512],
                                                                           func=AF.Copy, scale=ss[:, 14:15]),
                                             reads=[yt.R, sm.R], writes=[ynb.R])
                                        pt = bk("ty")
                                        ptv = pt.t[:, :].bitcast(BF16).rearrange("p (k c) -> p k c", k=8)
                                        for fc in range(4):
                                            P.op("pe", "transpose", dict(out=ptv[:, fc, :],
                                                                             in_=ynv[:, fc * 128:(fc + 1) * 128],
                                                                             identity=identb.t[:, :]),
                                                 reads=[ynb.R, identb.R], writes=[pt.R])
                                        P.op("dve", "tensor_tensor", dict(out=ygT.t[:, :, csl], in0=ptv[:, 0:4, :],
                                                                              in1=sg.t[:, :, csl], op=ALU.mult),
                                             reads=[pt.R, sg.R], writes=[ygT.R])
                                    for c2 in range(2):
                                        pu = bk("su")
                                        P.op("pe", "matmul", dict(out=pu.t[:, :], lhsT=kz.t[:, c2 * 128:(c2 + 1) * 128],
                                                                      rhs=vb.t[:, cc, :], start=True, stop=True),
                                             reads=[kz.R, vb.R], writes=[pu.R])
                                        P.op("dve", "scalar_tensor_tensor", dict(out=st.t[:, c2, :],
                                                                                     in0=st.t[:, c2, :], scalar=g128,
                                                                                     in1=pu.t[:, :], op0=ALU.mult,
                                                                                     op1=ALU.add),
                                             reads=[st.R, tabs.R, pu.R], writes=[st.R])
                                    P.op("act", "activation", dict(out=stb.t[:, :, :], in_=st.t[:, :, :],
                                                                       func=AF.Copy), reads=[st.R], writes=[stb.R])
                                if pas == "D" and h < 3:
                                    P.dma("pool", ygs[j, :, 4 * h:4 * h + 4, :], ygT.t[:, :, :], ygT.chq("pool"), reads=[ygT.R])
                                if pas == "D" and h == 3:
                                    out_proj_resid(lambda kc, tt: (ygo.t[:, kc, tt * 128:(tt + 1) * 128] if kc < 12 else
                                                                   ygT.t[:, kc - 12, tt * 128:(tt + 1) * 128]),
                                                   [ygo.R, ygT.R], 16, w_o, xr)
                                    store_x(xr, x3s, x3R, j)
                        P.barrier()

            if want("E"):
                ffn_pass(1, x3s_all, x3R, yout, None, True)
        P.barrier()
        P.n_total = P.n_ins
        if debug:
            print("[build] channels", len(P.chans), "instructions", P.n_ins, flush=True)
    return nc


_CACHE = {}


def _get_program(S, NSEQ, debug=False):
    key = (S, NSEQ, debug)
    if key not in _CACHE:
        _CACHE[key] = build_program(S, NSEQ, debug)
    return _CACHE[key]


def kernel(**inputs):
    inp = {k: np.asarray(v) for k, v in inputs.items()}
    xp, xs = inp["x_prompt"], inp["x_sample"]
    S = xp.shape[1]
    seqs = [xp[i] for i in range(xp.shape[0])] + [xs[i] for i in range(xs.shape[0])]
    NSEQ = 2
    nslots = N_CORES * NSEQ
    wp = pack_weights(inp)
    cpk = pack_consts(inp)
    rp = rope_tables(S)
    zero = np.zeros((S, D), np.float32)
    in_maps = []
    for c in range(N_CORES):
        a = seqs[c]
        b = seqs[c + N_CORES] if c + N_CORES < len(seqs) else zero
        in_maps.append({"xin": np.ascontiguousarray(np.stack([a, b], axis=0)), "wpack": wp, "cpack": cpk,
                        "rope": rp})
    nc = _get_program(S, NSEQ)
    res = run_bass_kernel_spmd(nc, in_maps, core_ids=list(range(N_CORES)))
    outs = [None] * len(seqs)
    for c in range(N_CORES):
        y = res.results[c]["yout"]
        outs[c] = y[0]
        if c + N_CORES < len(seqs):
            outs[c + N_CORES] = y[1]
    nb = xp.shape[0]
    y_prompt = np.ascontiguousarray(np.stack(outs[:nb], axis=0)).astype(np.float32)
    y_sample = np.ascontiguousarray(np.stack(outs[nb:], axis=0)).astype(np.float32)
    return (y_prompt, y_sample)
```

```python
import math
from contextlib import ExitStack
import numpy as np
import ml_dtypes
import concourse.bass as bass
import concourse.mybir as mybir
from concourse.bass_utils import run_bass_kernel_spmd

F32 = mybir.dt.float32
BF16 = mybir.dt.bfloat16
AF = mybir.ActivationFunctionType
ALU = mybir.AluOpType

D = 1024
T = 512
XW = 515
TMPW = 520
DFF = 2816
NFC = 22
EPS = 1e-6
N_CORES = 8

W_LAYOUT = {}
_off = 0
for _n, _sz in [("lru_in", 16 * 8 * 128), ("ga0", 4 * 2 * 2 * 128), ("gx0", 4 * 2 * 2 * 128),
                ("ga1", 4 * 2 * 2 * 128), ("gx1", 4 * 2 * 2 * 128), ("lru_out", 8 * 1024),
                ("ffn_in0", 11 * 4 * 8 * 128), ("ffn_out0", 22 * 1024),
                ("ffn_in1", 11 * 4 * 8 * 128), ("ffn_out1", 22 * 1024)] + \
               [("ret_qkg%d" % h, 8 * 8 * 128) for h in range(4)] + \
               [("ret_v%d" % h, 8 * 512) for h in range(4)] + \
               [("ret_out%d" % h, 4 * 1024) for h in range(4)]:
    W_LAYOUT[_n] = (_off, _sz)
    _off += _sz
NW = _off

C_LAYOUT = {}
_off = 0
for _n, _sz in [("ident", 128), ("tri", 256), ("nrow", 256), ("pcols", 4), ("gmix", 16), ("gffn", 16),
                ("gfin", 1024), ("lru_cw", 32), ("lru_cb", 8), ("b_a", 16), ("b_x", 16), ("lam", 16),
                ("ffn_cw0", 66), ("ffn_cw1", 66), ("ffn_cb0", 22), ("ffn_cb1", 22), ("ret_norm", 16),
                ("dlogit", 8)]:
    C_LAYOUT[_n] = (_off, _sz)
    _off += _sz
NCP = _off


def _cols(v, nchunk):
    return np.ascontiguousarray(v.reshape(nchunk, 128).T)


def _stat(w, fcs):
    K = w.shape[0]
    w4 = w.reshape(K // 128, 128, w.shape[1] // 128, 128)
    return np.ascontiguousarray(w4[:, :, fcs, :].transpose(1, 2, 0, 3))


def _mov(w):
    K = w.shape[0]
    return np.ascontiguousarray(w.reshape(K // 128, 128, w.shape[1]).transpose(1, 0, 2))


def pack_weights(inp):
    wp = np.empty((128, NW), np.float32)

    def put(name, arr):
        o, sz = W_LAYOUT[name]
        wp[:, o:o + sz] = arr.reshape(128, sz)

    put("lru_in", _stat(inp["lru_w_in"][0], list(range(16))))
    for d in range(2):
        for nm, key in (("ga", "lru_w_a"), ("gx", "lru_w_x")):
            w = inp[key][0, d]
            w5 = w.reshape(4, 2, 128, 2, 128)
            put("%s%d" % (nm, d), np.ascontiguousarray(w5.transpose(2, 0, 1, 3, 4)))
    put("lru_out", _mov(inp["lru_w_out"][0]))
    for l in range(2):
        w = inp["ffn_w_in"][l]
        slabs = []
        for s in range(11):
            slabs.append(_stat(w, [2 * s, 2 * s + 1, NFC + 2 * s, NFC + 2 * s + 1]))
        put("ffn_in%d" % l, np.stack(slabs, axis=1))
        put("ffn_out%d" % l, _mov(inp["ffn_w_out"][l]))
    rw = inp["ret_w_in"][0]
    for h in range(4):
        fcs = [2 * h, 2 * h + 1, 8 + 2 * h, 8 + 2 * h + 1] + [32 + 4 * h + i for i in range(4)]
        put("ret_qkg%d" % h, _stat(rw, fcs))
        put("ret_v%d" % h, _mov(rw[:, 2048 + 512 * h:2048 + 512 * (h + 1)]))
        put("ret_out%d" % h, _mov(inp["ret_w_out"][0][512 * h:512 * (h + 1), :]))
    return wp


def pack_consts(inp):
    cp = np.zeros((128, NCP), np.float32)

    def put(name, arr):
        o, sz = C_LAYOUT[name]
        cp[:, o:o + sz] = np.asarray(arr, np.float32).reshape(128, sz)

    put("ident", np.eye(128, dtype=np.float32))
    m = np.arange(128)[:, None]
    n = np.arange(128)[None, :]
    tri = np.stack([(n >= m), (m > n)], axis=1).astype(np.float32)
    put("tri", tri)
    nrow = np.stack([np.broadcast_to(n + 1.0, (128, 128)), np.broadcast_to(128.0 - n, (128, 128))], axis=1)
    put("nrow", nrow)
    mm = np.arange(128, dtype=np.float32)
    put("pcols", np.stack([-(mm + 1), mm - 128, 127 - mm, mm], axis=1))
    put("gmix", np.stack([_cols(inp["norm_mix"][l], 8) for l in range(2)], axis=1))
    put("gffn", np.stack([_cols(inp["norm_ffn"][l], 8) for l in range(2)], axis=1))
    put("gfin", np.broadcast_to(inp["norm_final"][None, :], (128, 1024)))
    cw = inp["lru_conv_w"][0]
    put("lru_cw", np.stack([_cols(cw[k], 8) for k in range(4)], axis=2))
    put("lru_cb", _cols(inp["lru_conv_b"][0], 8))
    for nm, key in (("b_a", "lru_b_a"), ("b_x", "lru_b_x"), ("lam", "lru_lambda")):
        put(nm, np.stack([_cols(inp[key][0, d], 8) for d in range(2)], axis=1))
    for l in range(2):
        cw = inp["ffn_conv_w"][l]
        put("ffn_cw%d" % l, np.stack([_cols(cw[k], NFC) for k in range(3)], axis=2))
        put("ffn_cb%d" % l, _cols(inp["ffn_conv_b"][l], NFC))
    put("ret_norm", _cols(inp["ret_norm"][0], 16))
    put("dlogit", np.broadcast_to(inp["ret_decay_logit"][0].reshape(1, 8), (128, 8)))
    return cp


def rope_tables(S):
    half = 128
    theta = (10000.0 ** (-np.arange(half, dtype=np.float32) / np.float32(half))).astype(np.float32)
    ang = (np.arange(S, dtype=np.float32)[None, :] * theta[:, None]).astype(np.float32)
    c = np.cos(ang.astype(np.float64)).astype(np.float32)
    s = np.sin(ang.astype(np.float64)).astype(np.float32)
    return np.ascontiguousarray(np.stack([c, s, c / 16.0, s / 16.0], axis=0))


import heapq


class Res:
    __slots__ = ("w", "r", "const")

    def __init__(self, const=False):
        self.w = None
        self.r = []
        self.const = const


class Chan:
    __slots__ = ("sem", "cnt", "last")

    def __init__(self, sem):
        self.sem = sem
        self.cnt = 0
        self.last = None


class Op:
    __slots__ = ("idx", "eng", "name", "kw", "deps", "cost", "lat", "ch", "batch", "n", "fin", "npred", "succ",
                 "ready", "cls", "boost")


_ACT_CLASS = {AF.Tanh: 1, AF.Exp: 1, AF.Sqrt: 2, AF.Gelu_apprx_tanh: 3, AF.Silu: 4, AF.Ln: 5, AF.Sigmoid: 6}
NO_LATE = False
PREP_STQ = "sp"
PREP_MOD = 2
TMPS_A = 44
TMPS_B1 = 21
PW_LATE = 1024
GELU_BOOST = 3.0e5
ACT_SWITCH_NS = 1300.0
ACT_STARVE_NS = 6000.0


def _free_size(ap):
    sh = ap.shape
    n = 1
    for v in sh[1:]:
        n *= int(v)
    return n


def _is_psum(ap):
    try:
        return "psum" in str(ap.space).lower() or "ps" == str(ap.tensor.name)[:2]
    except Exception:
        return False


class Prog:
    SELF_SYNC = ("act", "dve", "pool")
    DMA_BW = 200.0
    DMA_LAT = 2000.0

    def __init__(self, nc, es):
        self.nc = nc
        self.es = es
        self.engs = {"pe": nc.tensor, "act": nc.scalar, "dve": nc.vector, "pool": nc.gpsimd, "sp": nc.sync}
        self.sem = {}
        self.cnt = {}
        self.seen = {}
        for e in self.engs:
            self.sem[e] = es.enter_context(nc.semaphore("s_" + e))
            self.cnt[e] = 0
            self.seen[e] = {}
        self.chans = []
        self.free_ch = {}
        self.scope_ch = []
        self.n_ins = 0
        self.pending = []
        self.batch = 0
        self.sim_total = 0.0
        self.act_cls = [0]
        self.crit = None
        self.crit_batch = -1
        self.verbose = False

    def chan(self, kind="sp"):
        fl = self.free_ch.setdefault(kind, [])
        if fl:
            c = fl.pop()
        else:
            c = Chan(self.es.enter_context(self.nc.semaphore("ch%d" % len(self.chans))))
            self.chans.append(c)
        self.scope_ch.append((kind, c))
        return c

    def end_scope(self):
        for kind, c in self.scope_ch:
            self.free_ch.setdefault(kind, []).append(c)
        self.scope_ch = []

    def _cost(self, e, name, kw):
        if e == "pe":
            if name == "transpose":
                n = _free_size(kw["identity"])
            else:
                n = _free_size(kw["rhs"])
            return max(n * 0.42 + 10.0, 56.0), 120.0
        out = kw.get("out", kw.get("ap"))
        n = _free_size(out)
        src = kw.get("in_", kw.get("in0", kw.get("data0", None)))
        ps = 60.0 if (src is not None and _is_psum(src)) else 0.0
        if e == "act":
            return 190.0 + n * 0.86 + ps, 120.0
        if e == "dve":
            f = 2.2 if name == "tensor_tensor_scan" else 1.2
            if out.dtype == BF16 and src is not None and src.dtype == BF16:
                f *= 0.5
            return 70.0 + n * f + ps, 120.0
        if e == "pool":
            f = 4.2 if name == "tensor_copy" else 2.4
            return 120.0 + n * f, 150.0
        return 500.0, 100.0

    def _mkop(self, e, name, kw, reads, writes):
        o = Op()
        o.idx = len(self.pending)
        o.eng = e
        o.name = name
        o.kw = kw
        o.batch = self.batch
        o.ch = None
        o.cls = 0
        o.boost = 0.0
        deps = set()
        for R in reads:
            if R.w is not None:
                deps.add(R.w)
        for R in writes:
            if R.w is not None:
                deps.add(R.w)
            for x in R.r:
                deps.add(x)
        for R in reads:
            if not R.const:
                R.r.append(o)
        for R in writes:
            R.w = o
            R.r = []
        deps.discard(o)
        o.deps = [d for d in deps if d.batch == self.batch]
        self.pending.append(o)
        return o

    def op(self, e, name, kw, reads=(), writes=(), boost=0.0):
        o = self._mkop(e, name, kw, reads, writes)
        o.cost, o.lat = self._cost(e, name, kw)
        o.boost = boost
        if e == "act" and name == "activation":
            o.cls = _ACT_CLASS.get(kw["func"], 0)

    def dma(self, q, out, in_, ch, reads=(), writes=()):
        o = self._mkop(q, "dma_start", dict(out=out, in_=in_), reads, writes)
        o.ch = ch
        if ch.last is not None and ch.last.batch == self.batch and ch.last not in o.deps:
            o.deps.append(ch.last)
        ch.last = o
        nbytes = _free_size(out) * int(out.shape[0]) * (2 if out.dtype == BF16 else 4)
        o.cost = 150.0 if q == "sp" else 600.0
        o.lat = float(nbytes)

    def flush(self):
        ops = self.pending
        self.pending = []
        self.batch += 1
        if not ops:
            return
        for o in ops:
            o.succ = []
            o.npred = len(o.deps)
            o.fin = None
        for o in ops:
            for d in o.deps:
                d.succ.append(o)
        for o in reversed(ops):
            b = 0.0
            for sc in o.succ:
                if sc.ready > b:
                    b = sc.ready
            tl = (o.lat / self.DMA_BW + self.DMA_LAT) if o.ch is not None else o.lat
            o.ready = b + o.cost + tl + o.boost
        for o in ops:
            o.idx = (-o.ready, o.idx)
        free = {e: 0.0 for e in self.engs}
        fut = {e: [] for e in self.engs}
        avl = {e: {} for e in self.engs}
        busy = {e: 0.0 for e in self.engs}
        for o in ops:
            if o.npred == 0:
                o.ready = 0.0
                heapq.heappush(fut[o.eng], (0.0, o.idx, o))
        streams = {e: [] for e in self.engs}
        dma_free = 0.0
        ncnt = dict(self.cnt)
        nleft = len(ops)
        engl = list(self.engs)
        cur_cls = self.act_cls

        def pick(e, t, pop):
            a = avl[e]
            best = None
            bestc = None
            for c, h in a.items():
                if h and (best is None or h[0][0] < best[0]):
                    best, bestc = h[0], c
            if best is None:
                return None
            if e == "act" and bestc not in (0, cur_cls[0]):
                alt, altc = None, None
                for c in (0, cur_cls[0]):
                    h = a.get(c)
                    if h and (alt is None or h[0][0] < alt[0]):
                        alt, altc = h[0], c
                if alt is not None and t - best[1] < ACT_STARVE_NS:
                    best, bestc = alt, altc
            if pop:
                heapq.heappop(a[bestc])
            return best

        while nleft:
            bsel = None
            for e in engl:
                f, t = fut[e], free[e]
                while f and f[0][0] <= t:
                    x = heapq.heappop(f)
                    heapq.heappush(avl[e].setdefault(x[2].cls, []), (x[1], x[0], x[2]))
                c = pick(e, t, False)
                if c is not None:
                    cs, ci = t, c[0]
                elif f:
                    cs, ci = f[0][0], f[0][1]
                else:
                    continue
                if bsel is None or (cs, ci) < bsel[0]:
                    bsel = ((cs, ci), e)
            (cs, ci), e = bsel
            c = pick(e, cs, True)
            if c is not None:
                o = c[2]
            else:
                o = heapq.heappop(fut[e])[2]
            start = cs
            if self.crit is not None:
                bd = None
                for d in o.deps:
                    if bd is None or d.fin > bd.fin:
                        bd = d
                prev = streams[e][-1] if streams[e] else None
                if bd is not None and bd.fin >= cs - 1e-6:
                    self.crit[o] = ("dep", bd, start)
                else:
                    self.crit[o] = ("eng", prev, start)
            if e == "act" and o.cls != 0 and o.cls != cur_cls[0]:
                start += ACT_SWITCH_NS
                busy[e] += ACT_SWITCH_NS
                cur_cls[0] = o.cls
            free[e] = start + o.cost
            busy[e] += o.cost
            if o.ch is not None:
                t0 = max(start + self.DMA_LAT * 0.5, dma_free)
                dma_free = t0 + o.lat / self.DMA_BW
                o.fin = dma_free + self.DMA_LAT * 0.5
            else:
                o.fin = start + o.cost + o.lat
            streams[e].append(o)
            if o.ch is None:
                ncnt[e] += 1
                o.n = ncnt[e]
            else:
                o.ch.cnt += 1
                o.n = o.ch.cnt
            nleft -= 1
            for sc in o.succ:
                sc.npred -= 1
                if sc.npred == 0:
                    r = 0.0
                    for d in sc.deps:
                        f = d.fin - d.lat if (d.eng == "pe" and sc.eng == "pe") else d.fin
                        if f > r:
                            r = f
                    sc.ready = r
                    heapq.heappush(fut[sc.eng], (r, sc.idx, sc))
        span = max([o.fin for o in ops])
        if self.crit is not None and self.batch == self.crit_batch:
            o = max(ops, key=lambda x: x.fin)
            chain = []
            while o is not None and len(chain) < 400:
                kind, p, st = self.crit[o]
                kw = o.kw or {}
                oo = kw.get("out", kw.get("ap"))
                try:
                    nm = oo.tensor.name
                except Exception:
                    nm = "?"
                chain.append("%9.1f %-4s %-22s cost=%6.0f fin=%9.1f via %s -> %s" % (st, o.eng, o.name, o.cost, o.fin, kind, nm))
                o = p
            print("\n".join(reversed(chain)))
        if self.crit is not None:
            self.crit = {}
        self.sim_total += span
        if self.verbose:
            print("[sched] batch %d: %d ops, sim span %.1f us, busy: %s" % (
                self.batch, len(ops), span / 1e3,
                " ".join("%s=%.0f" % (e, busy[e] / 1e3) for e in engl)), flush=True)
        for e in engl:
            eng = self.engs[e]
            seen = self.seen[e]
            for o in streams[e]:
                for d in o.deps:
                    if d.ch is not None:
                        src, sem, v = d.ch, d.ch.sem, 16 * d.n
                    else:
                        if d.eng == e and e not in self.SELF_SYNC:
                            continue
                        src, sem, v = d.eng, self.sem[d.eng], d.n
                    if seen.get(src, 0) < d.n:
                        eng.wait_ge(sem, v)
                        seen[src] = d.n
                        self.n_ins += 1
                ins = getattr(eng, o.name)(**o.kw)
                if o.ch is None:
                    ins.then_inc(self.sem[e], 1)
                    self.cnt[e] = o.n
                else:
                    ins.then_inc(o.ch.sem, 16)
                self.n_ins += 1
                o.kw = None
                o.succ = None

    def barrier(self):
        self.flush()
        for e in self.engs:
            eng = self.engs[e]
            seen = self.seen[e]
            for e2 in self.engs:
                if e2 != e and self.cnt[e2] > seen.get(e2, 0):
                    eng.wait_ge(self.sem[e2], self.cnt[e2])
                    seen[e2] = self.cnt[e2]
            for c in self.chans:
                if c.cnt > seen.get(c, 0):
                    eng.wait_ge(c.sem, 16 * c.cnt)
                    seen[c] = c.cnt
        self.end_scope()


class Buf:
    __slots__ = ("t", "R", "P", "chs")

    def __init__(self, t, P=None):
        self.t = t
        self.R = Res()
        self.P = P
        self.chs = {}

    def chq(self, q):
        c = self.chs.get(q)
        if c is None:
            c = self.P.chan(q)
            self.chs[q] = c
        return c


class Ring:
    def __init__(self, bufs):
        self.bufs = bufs
        self.i = 0

    def next(self):
        b = self.bufs[self.i % len(self.bufs)]
        self.i += 1
        return b


def build_program(S, NSEQ, debug=False, only=None):
    NT = S // T

    def want(p):
        return only is None or p in only
    nc = bass.Bass("TRN2", target_bir_lowering=False)
    dk = "ExternalOutput" if debug else "Internal"
    xin = nc.dram_tensor("xin", [NSEQ, S, D], F32, kind="ExternalInput").ap()
    wpack = nc.dram_tensor("wpack", [128, NW], F32, kind="ExternalInput").ap()
    cpack = nc.dram_tensor("cpack", [128, NCP], F32, kind="ExternalInput").ap()
    rope = nc.dram_tensor("rope", [4, 128, S], F32, kind="ExternalInput").ap()
    yout = nc.dram_tensor("yout", [NSEQ, S, D], F32, kind="ExternalOutput").ap()
    wsc = nc.dram_tensor("wsc", [128, NW], BF16, kind="Internal").ap()
    hb_all = nc.dram_tensor("hb", [NSEQ, NT, 128, 8, T], F32, kind=dk).ap()
    x1s_all = nc.dram_tensor("x1s", [NSEQ, S, D], F32, kind=dk).ap()
    x2s_all = nc.dram_tensor("x2s", [NSEQ, S, D], F32, kind=dk).ap()
    x3s_all = nc.dram_tensor("x3s", [NSEQ, S, D], F32, kind=dk).ap()
    ybs_all = nc.dram_tensor("ybs", [NSEQ, 4, S, 512], F32, kind=dk).ap()
    xnl_all = nc.dram_tensor("xnl", [NSEQ, NT, 128, 8, T], BF16, kind="Internal").ap()
    xnrs_all = nc.dram_tensor("xnrs", [NSEQ, NT, 128, 8, T], BF16, kind="Internal").ap()
    xcs_all = nc.dram_tensor("xcs", [NSEQ, NT, 128, 8, T], F32, kind="Internal").ap()
    qrs_all = nc.dram_tensor("qrs", [NSEQ, 4, NT, 128, 2, T], F32, kind="Internal").ap()
    kTs_all = nc.dram_tensor("kTs", [NSEQ, 4, NT, 128, 2, T], BF16, kind="Internal").ap()
    vbs_all = nc.dram_tensor("vbs", [NSEQ, 4, NT, 128, 4, 512], BF16, kind="Internal").ap()
    ygs_all = nc.dram_tensor("ygs", [NSEQ, NT, 128, 12, T], BF16, kind="Internal").ap()

    ges = ExitStack()
    with ges:
        P = Prog(nc, ges)
        P.verbose = debug

        uid = [0]

        def sb(es, name, shape, dt, ch=False):
            uid[0] += 1
            t = es.enter_context(nc.sbuf_tensor("%s_%d" % (name, uid[0]), shape, dt))
            return Buf(t, P)

        cpb = sb(ges, "cp", [128, NCP], F32, ch=True)
        cpb.R.const = True
        cp = cpb.t

        def cview(name):
            o, sz = C_LAYOUT[name]
            return cp[:, o:o + sz]

        identb = sb(ges, "identb", [128, 128], BF16)
        clru = sb(ges, "clru", [128, 16], F32)
        lgt = sb(ges, "lgt", [128, 8], F32)
        smr = Ring([sb(ges, "smallf%d" % i, [128, 16], F32) for i in range(8)])
        hbias = sb(ges, "hbias", [128, 32], F32)
        clh = sb(ges, "clh", [128, 16], F32)
        negh = sb(ges, "negh", [128, 16], F32)
        banks = [Buf(ges.enter_context(nc.psum_tensor("ps%d" % i, [128, 512], F32)), P) for i in range(8)]
        bring = Ring(banks)
        roles = {}

        def set_roles(cfg):
            roles.clear()
            for names, idxs in cfg:
                r = Ring([banks[i] for i in idxs])
                for nm in names.split(","):
                    roles[nm] = r

        def bk(role):
            return roles[role].next() if role in roles else bring.next()
        wscR = Res()
        hbR = [Res() for _ in range(NT)]
        x1R = [Res() for _ in range(NT)]
        x2R = [Res() for _ in range(NT)]
        x3R = [Res() for _ in range(NT)]
        ybR = [[Res() for _ in range(NT)] for _ in range(4)]
        outR = Res()
        inR = Res()

        P.scope_ch = []
        P.dma("sp", cp[:, :], cpack[:, :], cpb.chq("sp"), writes=[cpb.R])

        PW = 2048
        prep_k = [0]

        def prep_pieces(names, sring, bring2, pw=PW, ldq="sp", stq="pool"):
            for nm in names:
                o0, sz = W_LAYOUT[nm]
                o = o0
                while o < o0 + sz:
                    w = min(pw, o0 + sz - o)
                    s_, b_ = sring.next(), bring2.next()
                    P.dma(ldq, s_.t[:, :w], wpack[:, o:o + w], s_.chq(ldq), writes=[s_.R])
                    if prep_k[0] % PREP_MOD != 1:
                        P.op("act", "activation", dict(out=b_.t[:, :w], in_=s_.t[:, :w], func=AF.Copy),
                             reads=[s_.R], writes=[b_.R])
                    else:
                        P.op("dve", "tensor_copy", dict(out=b_.t[:, :w], in_=s_.t[:, :w]),
                             reads=[s_.R], writes=[b_.R])
                    P.dma(stq, wsc[:, o:o + w], b_.t[:, :w], b_.chq(stq), reads=[b_.R])
                    o += w
                    prep_k[0] += 1

        EARLY_W = ["lru_in", "ga1", "gx1"]
        LATE_A = ["ga0", "gx0", "lru_out", "ffn_in0", "ffn_out0"]
        LATE_B2 = [n for n in W_LAYOUT if n not in EARLY_W and n not in LATE_A]
        with ExitStack() as es:
            sring = Ring([sb(es, "pst%d" % i, [128, PW], F32, ch=True) for i in range(3)])
            bring2 = Ring([sb(es, "pbt%d" % i, [128, PW], BF16, ch=True) for i in range(3)])
            prep_pieces(EARLY_W + LATE_A, sring, bring2)
            P.op("dve", "tensor_copy", dict(out=identb.t[:, :], in_=cview("ident")), reads=[cpb.R],
                 writes=[identb.R])
            P.op("act", "activation", dict(out=clru.t[:, :], in_=cview("lam"), func=AF.Exp, scale=-1.0),
                 reads=[cpb.R], writes=[clru.R])
            P.op("act", "activation", dict(out=clru.t[:, :], in_=clru.t[:, :], func=AF.Ln, bias=1.0),
                 reads=[clru.R], writes=[clru.R])
            P.op("dve", "tensor_scalar", dict(out=clru.t[:, :], in0=clru.t[:, :], scalar1=-8.0, scalar2=None,
                                                  op0=ALU.mult), reads=[clru.R], writes=[clru.R])
            P.op("dve", "tensor_scalar", dict(out=clh.t[:, :], in0=clru.t[:, :], scalar1=0.5, scalar2=None,
                                              op0=ALU.mult), reads=[clru.R], writes=[clh.R])
            P.op("dve", "tensor_scalar", dict(out=hbias.t[:, 0:16], in0=cview("b_a"), scalar1=0.5, scalar2=None,
                                              op0=ALU.mult), reads=[cpb.R], writes=[hbias.R])
            P.op("dve", "tensor_scalar", dict(out=hbias.t[:, 16:32], in0=cview("b_x"), scalar1=0.5, scalar2=None,
                                              op0=ALU.mult), reads=[cpb.R], writes=[hbias.R])
            P.op("dve", "memset", dict(ap=negh.t[:, :], constant=-0.5), writes=[negh.R])
            P.op("act", "activation", dict(out=lgt.t[:, :], in_=cview("dlogit"), func=AF.Exp, scale=-1.0),
                 reads=[cpb.R], writes=[lgt.R])
            P.op("act", "activation", dict(out=lgt.t[:, :], in_=lgt.t[:, :], func=AF.Ln, bias=1.0),
                 reads=[lgt.R], writes=[lgt.R])
            P.op("dve", "tensor_scalar", dict(out=lgt.t[:, :], in0=lgt.t[:, :], scalar1=-1.0, scalar2=None,
                                                  op0=ALU.mult), reads=[lgt.R], writes=[lgt.R])
            P.barrier()

        def load_w(es, name, tag, sub=None):
            o, sz = W_LAYOUT[name]
            if sub is not None:
                o, sz = o + sub[0], sub[1]
            b = sb(es, "w_" + tag, [128, sz], BF16, ch=True)
            P.dma("sp", b.t[:, :], wsc[:, o:o + sz], b.chq("sp"), writes=[b.R])
            return b

        def norm_T(tmps, xt, xR, ntile, nparts, gcol, xn, col_of):
            sm = smr.next()
            ss = sm.t
            for tt in range(ntile):
                jk = tmps.next()
                jv = jk.t[:, :].bitcast(BF16)
                P.op("act", "activation", dict(out=jv[:nparts, 0:1024], in_=xt(tt), func=AF.Square,
                                                   accum_out=ss[:nparts, tt:tt + 1]),
                     reads=[xR], writes=[jk.R, sm.R])
            P.op("dve", "tensor_scalar", dict(out=ss[:nparts, 4:4 + ntile], in0=ss[:nparts, 0:ntile],
                                              scalar1=1.0 / D, scalar2=EPS, op0=ALU.mult, op1=ALU.add),
                 reads=[sm.R], writes=[sm.R])
            P.op("pool", "tensor_tensor", dict(out=ss[:nparts, 8:8 + ntile], in0=ss[:nparts, 4:4 + ntile],
                                               in1=negh.t[:nparts, 0:ntile], op=ALU.pow),
                 reads=[sm.R, negh.R], writes=[sm.R])
            for tt in range(ntile):
                xs = tmps.next()
                xv = xs.t[:, :].bitcast(BF16)
                P.op("dve", "tensor_scalar", dict(out=xv[:nparts, 0:1024], in0=xt(tt),
                                                      scalar1=ss[:nparts, 8 + tt:9 + tt], scalar2=None, op0=ALU.mult),
                     reads=[xR, sm.R], writes=[xs.R])
                bk_ = bk("nt")
                bv = bk_.t[:, :].bitcast(BF16).rearrange("p (k c) -> p k c", k=8)
                for kc in range(8):
                    P.op("pe", "transpose", dict(out=bv[:, kc, 0:nparts], in_=xv[:nparts, kc * 128:(kc + 1) * 128],
                                                     identity=identb.t[:nparts, :nparts]),
                         reads=[xs.R, identb.R], writes=[bk_.R])
                for (c0, n, s0) in col_of(tt):
                    P.op("dve", "tensor_tensor", dict(out=xn.t[:, :, c0:c0 + n], in0=bv[:, :, s0:s0 + n],
                                                          in1=gcol.unsqueeze(2).to_broadcast([128, 8, n]),
                                                          op=ALU.mult),
                         reads=[bk_.R, cpb.R], writes=[xn.R])

        def load_x_and_norm(es_bufs, src, srcR, j, gcol, halo):
            xb, hx, xn, tmps = es_bufs
            s0 = j * T
            P.dma("sp", xb.t[:, :, :], src[s0:s0 + T, :].rearrange("(t p) d -> p t d", p=128), xb.chq("sp"),
                  writes=[xb.R])
            if halo:
                P.op("dve", "memset", dict(ap=hx.t[0:3, :], constant=0.0), writes=[hx.R])
                if j > 0:
                    P.dma("sp", hx.t[0:2, :], src[s0 - 2:s0, :], hx.chq("sp"), writes=[hx.R])
                if j < NT - 1:
                    P.dma("sp", hx.t[2:3, :], src[s0 + T:s0 + T + 1, :], hx.chq("sp"), writes=[hx.R])
                norm_T(tmps, lambda tt: hx.t[0:3, :], hx.R, 1, 3, gcol, xn, lambda tt: [(0, 2, 0), (XW - 1, 1, 2)])
            norm_T(tmps, lambda tt: xb.t[:, tt, :], xb.R, 4, 128, gcol, xn,
                   lambda tt: [(2 + tt * 128, 128, 0)])

        def proj_split(wl, xn, wR):
            bA, bB = bk("ps"), bk("ps")
            for kc in range(8):
                P.op("pe", "matmul", dict(out=bA.t[:, 0:258], lhsT=wl(kc), rhs=xn.t[:, kc, 0:258], start=(kc == 0),
                                              stop=(kc == 7)), reads=[wR, xn.R], writes=[bA.R])
                P.op("pe", "matmul", dict(out=bB.t[:, 0:257], lhsT=wl(kc), rhs=xn.t[:, kc, 258:515], start=(kc == 0),
                                              stop=(kc == 7)), reads=[wR, xn.R], writes=[bB.R])
            return bA, bB

        def proj_main(wl, xn, wR):
            bk_ = bk("pm")
            for kc in range(8):
                P.op("pe", "matmul", dict(out=bk_.t[:, :], lhsT=wl(kc), rhs=xn.t[:, kc, 2:2 + T], start=(kc == 0),
                                              stop=(kc == 7)), reads=[wR, xn.R], writes=[bk_.R])
            return bk_

        def evac_split(bA, bB, tmps):
            ub = tmps.next()
            P.op("act", "activation", dict(out=ub.t[:, 0:258], in_=bA.t[:, 0:258], func=AF.Copy),
                 reads=[bA.R], writes=[ub.R])
            P.op("act", "activation", dict(out=ub.t[:, 258:515], in_=bB.t[:, 0:257], func=AF.Copy),
                 reads=[bB.R], writes=[ub.R])
            return ub

        def out_proj_resid(act_lhs, actR, nk, wout, xb, after_tile=None):
            wv = wout.t[:, :].rearrange("p (k f) -> p k f", k=nk)
            for tt in range(4):
                for hf in range(2):
                    bk_ = bk("op")
                    for kc in range(nk):
                        P.op("pe", "matmul", dict(out=bk_.t[:, :], lhsT=act_lhs(kc, tt),
                                                      rhs=wv[:, kc, hf * 512:(hf + 1) * 512],
                                                      start=(kc == 0), stop=(kc == nk - 1)),
                             reads=(actR if isinstance(actR, list) else [actR]) + [wout.R], writes=[bk_.R])
                    P.op("dve", "tensor_tensor", dict(out=xb.t[:, tt, hf * 512:(hf + 1) * 512], in0=bk_.t[:, :],
                                                          in1=xb.t[:, tt, hf * 512:(hf + 1) * 512], op=ALU.add),
                         reads=[bk_.R, xb.R], writes=[xb.R])
                if after_tile is not None:
                    after_tile(tt)

        def store_x(xb, dst, dstR, j):
            s0 = j * T
            P.dma("pool", dst[s0:s0 + T, :].rearrange("(t p) d -> p t d", p=128), xb.t[:, :, :], xb.chq("pool"),
                  reads=[xb.R])

        for _once in (0,):
            pass
            xinR = [inR] * NT

            for pas in ("A", "B1"):
                if not want(pas):
                    continue
                with ExitStack() as es:
                    d = 1 if pas == "A" else 0
                    if pas == "A":
                        set_roles([("nt,ps", [0, 1, 2, 3]), ("g", [4, 5, 6, 7])])
                    else:
                        set_roles([("pm", [0, 1]), ("g", [2, 3, 4, 5]), ("op", [6, 7])])
                    w_in = load_w(es, "lru_in", "lin", sub=((8 * 8 * 128, 8 * 8 * 128) if pas == "A" else (0, 8 * 8 * 128)))
                    w_ga = load_w(es, "ga%d" % d, "ga")
                    w_gx = load_w(es, "gx%d" % d, "gx")
                    w_out = load_w(es, "lru_out", "lout") if pas == "B1" else None
                    winv = w_in.t[:, :].rearrange("p (fc kc f) -> p fc kc f", fc=8, kc=8)
                    gav = w_ga.t[:, :].rearrange("p (n kc oc f) -> p n kc oc f", n=4, kc=2, oc=2)
                    gxv = w_gx.t[:, :].rearrange("p (n kc oc f) -> p n kc oc f", n=4, kc=2, oc=2)
                    xring = Ring([sb(es, "x%d" % i, [128, 4, D], F32, ch=True) for i in range(2)])
                    hx = sb(es, "hx", [4, D], F32, ch=True) if pas == "A" else None
                    xnr = Ring([sb(es, "xn%d" % i, [128, 8, XW], BF16, ch=True) for i in range(2)])
                    tmps = Ring([sb(es, "tmp%d" % i, [128, TMPW], F32, ch=True)
                                 for i in range(TMPS_A if pas == "A" else TMPS_B1)])
                    psr = pbr = None
                    xcr = Ring([sb(es, "xc%d" % i, [128, 2, T], F32, ch=True) for i in range(4 if pas == "A" else 3)])
                    xcbr = Ring([sb(es, "xcb%d" % i, [128, 2, T], BF16) for i in range(3)])
                    hgr = Ring([sb(es, "hg%d" % i, [128, 8, T], BF16) for i in range(2)]) if pas == "B1" else None
                    gtr = Ring([sb(es, "gt%d" % i, [128, T], F32) for i in range(8)]) if pas == "B1" else None
                    carry = sb(es, "carry", [128, 8], F32)
                    carryR = [Res() for _ in range(8)]
                    cwv = cview("lru_cw").rearrange("p (k t) -> p k t", k=8)
                    cbv = cview("lru_cb")
                    bav = cview("b_a")
                    bxv = cview("b_x")
                    order = range(NT - 1, -1, -1) if pas == "A" else range(NT)
                    for seq in range(NSEQ):
                        hb, x1s, x2s, x3s, ybs = hb_all[seq], x1s_all[seq], x2s_all[seq], x3s_all[seq], ybs_all[seq]
                        xnl, xnrs, xcs = xnl_all[seq], xnrs_all[seq], xcs_all[seq]
                        qrs, kTs, vbs, ygs = qrs_all[seq], kTs_all[seq], vbs_all[seq], ygs_all[seq]
                        xsrc = xin[seq]
                        P.op("dve", "memset", dict(ap=carry.t[:, :], constant=0.0), writes=[carry.R] + carryR)
                        for j in order:
                            xb, xn = xring.next(), xnr.next()
                            hg = hgr.next() if pas == "B1" else None
                            if pas == "A":
                                load_x_and_norm((xb, hx, xn, tmps), xsrc, xinR, j, cview("gmix")[:, 0:8], True)
                                P.dma("pool", xnl[j], xn.t[:, :, 2:2 + T], xn.chq("pool"), reads=[xn.R])
                            else:
                                P.dma("sp", xb.t[:, :, :], xsrc[j * T:(j + 1) * T, :].rearrange("(t p) d -> p t d", p=128),
                                      xb.chq("sp"), writes=[xb.R])
                                P.dma("sp", xn.t[:, :, 2:2 + T], xnl[j], xn.chq("sp"), writes=[xn.R])
                            for n in range(4):
                                xc, xcb = xcr.next(), xcbr.next()
                                if pas == "B1":
                                    P.dma("sp", xc.t[:, :, :], xcs[j, :, 2 * n:2 * n + 2, :], xc.chq("sp"), writes=[xc.R])
                                for o2 in range(2):
                                    c = 2 * n + o2
                                    if pas == "B1":
                                        P.op("dve", "tensor_copy", dict(out=xcb.t[:, o2, :], in_=xc.t[:, o2, :]),
                                             reads=[xc.R], writes=[xcb.R])
                                        continue
                                    bA, bB = proj_split(lambda kc: winv[:, c, kc, :], xn, w_in.R)
                                    ub = evac_split(bA, bB, tmps)
                                    P.op("dve", "tensor_scalar", dict(out=xc.t[:, o2, :], in0=ub.t[:, 0:T],
                                                                          scalar1=cwv[:, c, 0:1], scalar2=cbv[:, c:c + 1],
                                                                          op0=ALU.mult, op1=ALU.add),
                                         reads=[ub.R, cpb.R], writes=[xc.R])
                                    for k in range(1, 4):
                                        P.op("dve", "scalar_tensor_tensor", dict(out=xc.t[:, o2, :],
                                                                                     in0=ub.t[:, k:k + T],
                                                                                     scalar=cwv[:, c, k:k + 1],
                                                                                     in1=xc.t[:, o2, :], op0=ALU.mult,
                                                                                     op1=ALU.add),
                                             reads=[ub.R, cpb.R, xc.R], writes=[xc.R])
                                    P.op("act", "activation", dict(out=xcb.t[:, o2, :], in_=xc.t[:, o2, :], func=AF.Copy),
                                         reads=[xc.R], writes=[xcb.R])
                                if pas == "A":
                                    P.dma("pool", xcs[j, :, 2 * n:2 * n + 2, :], xc.t[:, :, :], xc.chq("pool"), reads=[xc.R])
                                for o2 in range(2):
                                    c = 2 * n + o2
                                    pa, px = bk("g"), bk("g")
                                    for kc in range(2):
                                        P.op("pe", "matmul", dict(out=pa.t[:, :], lhsT=gav[:, n, kc, o2, :],
                                                                      rhs=xcb.t[:, kc, :], start=(kc == 0),
                                                                      stop=(kc == 1)),
                                             reads=[w_ga.R, xcb.R], writes=[pa.R])
                                    for kc in range(2):
                                        P.op("pe", "matmul", dict(out=px.t[:, :], lhsT=gxv[:, n, kc, o2, :],
                                                                      rhs=xcb.t[:, kc, :], start=(kc == 0),
                                                                      stop=(kc == 1)),
                                             reads=[w_gx.R, xcb.R], writes=[px.R])
                                    r_, i_, a_, s_, u_, h_ = [tmps.next() for _ in range(6)]
                                    ia, ix = d * 8 + c, 16 + d * 8 + c
                                    P.op("act", "activation", dict(out=r_.t[:, 0:T], in_=pa.t[:, :], func=AF.Tanh,
                                                                   scale=0.5, bias=hbias.t[:, ia:ia + 1]),
                                         reads=[pa.R, hbias.R], writes=[r_.R])
                                    P.op("act", "activation", dict(out=i_.t[:, 0:T], in_=px.t[:, :], func=AF.Tanh,
                                                                   scale=0.5, bias=hbias.t[:, ix:ix + 1]),
                                         reads=[px.R, hbias.R], writes=[i_.R])
                                    P.op("act", "activation", dict(out=a_.t[:, 0:T], in_=r_.t[:, 0:T], func=AF.Exp,
                                                                   scale=clh.t[:, ia:ia + 1], bias=clh.t[:, ia:ia + 1]),
                                         reads=[r_.R, clh.R], writes=[a_.R])
                                    P.op("act", "activation", dict(out=s_.t[:, 0:T], in_=a_.t[:, 0:T], func=AF.Square),
                                         reads=[a_.R], writes=[s_.R])
                                    P.op("act", "activation", dict(out=s_.t[:, 0:T], in_=s_.t[:, 0:T], func=AF.Sqrt,
                                                                   scale=-0.25, bias=0.25),
                                         reads=[s_.R], writes=[s_.R])
                                    P.op("dve", "scalar_tensor_tensor", dict(out=u_.t[:, 0:T], in0=i_.t[:, 0:T],
                                                                             scalar=1.0, in1=xc.t[:, o2, :],
                                                                             op0=ALU.add, op1=ALU.mult),
                                         reads=[i_.R, xc.R], writes=[u_.R])
                                    P.op("dve", "tensor_tensor", dict(out=u_.t[:, 0:T], in0=u_.t[:, 0:T],
                                                                      in1=s_.t[:, 0:T], op=ALU.mult),
                                         reads=[u_.R, s_.R], writes=[u_.R])
                                    if pas == "A":
                                        P.op("dve", "tensor_tensor_scan", dict(out=h_.t[:, 0:T][:, ::-1],
                                                                                   data0=a_.t[:, 0:T][:, ::-1],
                                                                                   data1=u_.t[:, 0:T][:, ::-1],
                                                                                   initial=carry.t[:, c:c + 1],
                                                                                   op0=ALU.mult, op1=ALU.add),
                                             reads=[a_.R, u_.R, carryR[c]], writes=[h_.R])
                                        P.op("pool", "tensor_copy", dict(out=carry.t[:, c:c + 1], in_=h_.t[:, 0:1]),
                                             reads=[h_.R], writes=[carryR[c]])
                                        P.dma("pool", hb[j, :, c, :], h_.t[:, 0:T], h_.chq("pool"), reads=[h_.R])
                                    else:
                                        hbt, gt = tmps.next(), gtr.next()
                                        P.dma("sp", hbt.t[:, 0:T], hb[j, :, c, :], hbt.chq("sp"), writes=[hbt.R])
                                        P.op("dve", "tensor_tensor_scan", dict(out=h_.t[:, 0:T], data0=a_.t[:, 0:T],
                                                                                   data1=u_.t[:, 0:T],
                                                                                   initial=carry.t[:, c:c + 1],
                                                                                   op0=ALU.mult, op1=ALU.add),
                                             reads=[a_.R, u_.R, carryR[c]], writes=[h_.R])
                                        P.op("dve", "tensor_copy", dict(out=carry.t[:, c:c + 1], in_=h_.t[:, T - 1:T]),
                                             reads=[h_.R], writes=[carryR[c]])
                                        pg = proj_main(lambda kc: winv[:, c, kc, :], xn, w_in.R)
                                        P.op("act", "activation", dict(out=gt.t[:, 0:T], in_=pg.t[:, :],
                                                                           func=AF.Gelu_apprx_tanh),
                                             reads=[pg.R], writes=[gt.R], boost=GELU_BOOST)
                                        P.op("dve", "tensor_tensor", dict(out=h_.t[:, 0:T], in0=h_.t[:, 0:T],
                                                                          in1=hbt.t[:, 0:T], op=ALU.add),
                                             reads=[h_.R, hbt.R], writes=[h_.R])
                                        P.op("dve", "tensor_tensor", dict(out=hg.t[:, c, :], in0=h_.t[:, 0:T],
                                                                              in1=gt.t[:, 0:T], op=ALU.mult),
                                             reads=[h_.R, gt.R], writes=[hg.R])
                            if pas == "B1":
                                out_proj_resid(lambda kc, tt: hg.t[:, kc, tt * 128:(tt + 1) * 128], hg.R, 8, w_out, xb)
                                store_x(xb, x1s, x1R, j)
                    if pas == "A":
                        prep_pieces([], psr, pbr, PW_LATE)
                    P.barrier()

            def ffn_pass(l, src_all, srcR, dst_all, dstR, final):
                with ExitStack() as es:
                    set_roles([])
                    w_out = load_w(es, "ffn_out%d" % l, "fout")
                    o_in, _ = W_LAYOUT["ffn_in%d" % l]
                    SL = 4 * 8 * 128
                    slabs = Ring([sb(es, "slab%d" % i, [128, SL], BF16, ch=True) for i in range(3)])
                    xring = Ring([sb(es, "x%d" % i, [128, 4, D], F32, ch=True) for i in range(2)])
                    hx = sb(es, "hx", [4, D], F32, ch=True)
                    xnr = Ring([sb(es, "xn%d" % i, [128, 8, XW], BF16) for i in range(2)])
                    tmps = Ring([sb(es, "tmp%d" % i, [128, TMPW], F32, ch=True) for i in range(16)])
                    actT = sb(es, "actT", [128, NFC, T], BF16)
                    if l == 0:
                        psr = Ring([sb(es, "pst%d" % i, [128, PW_LATE], F32, ch=True) for i in range(2)])
                        pbr = Ring([sb(es, "pbt%d" % i, [128, PW_LATE], BF16, ch=True) for i in range(2)])
                    oring = Ring([sb(es, "ot%d" % i, [128, D], F32, ch=True) for i in range(2)]) if final else None
                    cwv = cview("ffn_cw%d" % l).rearrange("p (k t) -> p k t", k=NFC)
                    cbv = cview("ffn_cb%d" % l)
                    gfv = cview("gfin")
                    for seq in range(NSEQ):
                        hb, x1s, x2s, x3s, ybs = hb_all[seq], x1s_all[seq], x2s_all[seq], x3s_all[seq], ybs_all[seq]
                        xnl, xnrs, xcs = xnl_all[seq], xnrs_all[seq], xcs_all[seq]
                        qrs, kTs, vbs, ygs = qrs_all[seq], kTs_all[seq], vbs_all[seq], ygs_all[seq]
                        xsrc = xin[seq]
                        src, dst = src_all[seq], dst_all[seq]
                        for j in range(NT):
                            xb, xn = xring.next(), xnr.next()
                            load_x_and_norm((xb, hx, xn, tmps), src, srcR, j, cview("gffn")[:, l * 8:(l + 1) * 8], True)
                            for s in range(11):
                                sl = slabs.next()
                                P.dma("sp", sl.t[:, :], wsc[:, o_in + s * SL:o_in + (s + 1) * SL], sl.chq("sp"), writes=[sl.R])
                                slv = sl.t[:, :].rearrange("p (fc kc f) -> p fc kc f", fc=4, kc=8)
                                for q in range(2):
                                    fc = 2 * s + q
                                    bA, bB = proj_split(lambda kc: slv[:, q, kc, :], xn, sl.R)
                                    ub = evac_split(bA, bB, tmps)
                                    cv, gu = tmps.next(), tmps.next()
                                    P.op("dve", "tensor_scalar", dict(out=cv.t[:, 0:T], in0=ub.t[:, 1:1 + T],
                                                                          scalar1=cwv[:, fc, 0:1],
                                                                          scalar2=cbv[:, fc:fc + 1],
                                                                          op0=ALU.mult, op1=ALU.add),
                                         reads=[ub.R, cpb.R], writes=[cv.R])
                                    for k in range(1, 3):
                                        P.op("dve", "scalar_tensor_tensor", dict(out=cv.t[:, 0:T],
                                                                                     in0=ub.t[:, 1 + k:1 + k + T],
                                                                                     scalar=cwv[:, fc, k:k + 1],
                                                                                     in1=cv.t[:, 0:T], op0=ALU.mult,
                                                                                     op1=ALU.add),
                                             reads=[ub.R, cpb.R, cv.R], writes=[cv.R])
                                    P.op("act", "activation", dict(out=gu.t[:, 0:T], in_=cv.t[:, 0:T],
                                                                       func=AF.Gelu_apprx_tanh),
                                         reads=[cv.R], writes=[gu.R])
                                    pv = proj_main(lambda kc: slv[:, 2 + q, kc, :], xn, sl.R)
                                    P.op("dve", "tensor_tensor", dict(out=actT.t[:, fc, :], in0=pv.t[:, :],
                                                                          in1=gu.t[:, 0:T], op=ALU.mult),
                                         reads=[pv.R, gu.R], writes=[actT.R])

                            def fin_tile(tt):
                                ot = oring.next()
                                jk = tmps.next()
                                jv = jk.t[:, :].bitcast(BF16)
                                sm = smr.next()
                                ss = sm.t
                                P.op("act", "activation", dict(out=jv[:, 0:1024], in_=xb.t[:, tt, :], func=AF.Square,
                                                                   accum_out=ss[:, 12:13]),
                                     reads=[xb.R], writes=[jk.R, sm.R])
                                P.op("dve", "tensor_scalar", dict(out=ss[:, 13:14], in0=ss[:, 12:13], scalar1=1.0 / D,
                                                                  scalar2=EPS, op0=ALU.mult, op1=ALU.add),
                                     reads=[sm.R], writes=[sm.R])
                                P.op("pool", "tensor_tensor", dict(out=ss[:, 14:15], in0=ss[:, 13:14],
                                                                   in1=negh.t[:, 0:1], op=ALU.pow),
                                     reads=[sm.R, negh.R], writes=[sm.R])
                                P.op("dve", "scalar_tensor_tensor", dict(out=ot.t[:, :], in0=xb.t[:, tt, :],
                                                                             scalar=ss[:, 14:15], in1=gfv,
                                                                             op0=ALU.mult, op1=ALU.mult),
                                     reads=[xb.R, sm.R, cpb.R], writes=[ot.R])
                                r0 = j * T + tt * 128
                                P.dma("pool", dst[r0:r0 + 128, :], ot.t[:, :], ot.chq("pool"), reads=[ot.R])

                            out_proj_resid(lambda kc, tt: actT.t[:, kc, tt * 128:(tt + 1) * 128], actT.R, NFC, w_out, xb,
                                           after_tile=fin_tile if final else None)
                            if not final:
                                store_x(xb, dst, dstR, j)
                    if l == 0:
                        prep_pieces(LATE_B2 if not NO_LATE else [], psr, pbr, PW_LATE, ldq="pool", stq="pool")
                    P.barrier()

            if want("B2"):
                ffn_pass(0, x1s_all, x1R, x2s_all, x2R, False)

            for pas in ("C", "D"):
                d = 1 if pas == "C" else 0
                for h in range(4):
                    if not want(pas):
                        continue
                    with ExitStack() as es:
                        set_roles([("nt,pm,op", [0, 1]), ("sc,tk", [2, 3]), ("po,ty", [4, 5]), ("su", [6, 7])])
                        if pas == "C":
                            w_qkg = load_w(es, "ret_qkg%d" % h, "qkg", sub=(0, 4 * 8 * 128))
                            w_v = load_w(es, "ret_v%d" % h, "wv")
                            w_o = None
                            wvv = w_v.t[:, :].rearrange("p (kc f) -> p kc f", kc=8)
                        else:
                            w_qkg = load_w(es, "ret_qkg%d" % h, "qkg", sub=(4 * 8 * 128, 4 * 8 * 128))
                            w_v = None
                            w_o = load_w(es, "ret_out0", "wo", sub=(0, 16 * 1024)) if h == 3 else None
                        qkgv = w_qkg.t[:, :].rearrange("p (fc kc f) -> p fc kc f", fc=4, kc=8)
                        first = (pas == "C" and h == 0)
                        xring = Ring([sb(es, "x%d" % i, [128, 4, D], F32, ch=True) for i in range(2)]) \
                            if first else None
                        xrr = Ring([sb(es, "xr%d" % i, [128, 4, D], F32, ch=True) for i in range(2)]) \
                            if (pas == "D" and h == 3) else None
                        ygor = Ring([sb(es, "ygo%d" % i, [128, 12, T], BF16, ch=True) for i in range(2)]) \
                            if (pas == "D" and h == 3) else None
                        xnr = Ring([sb(es, "xn%d" % i, [128, 8, XW], BF16, ch=True) for i in range(2)])
                        tmps = Ring([sb(es, "tmp%d" % i, [128, TMPW], F32, ch=True)
                                     for i in range(12 if pas == "D" else 16)])
                        rpr = Ring([sb(es, "rp%d" % i, [128, 4, T], F32, ch=True) for i in range(2)]) \
                            if pas == "C" else None
                        qrr = Ring([sb(es, "qr%d" % i, [128, 2, T], F32, ch=True) for i in range(2)])
                        qxr = Ring([sb(es, "qx%d" % i, [128, 2, T], BF16) for i in range(2)])
                        kTr = Ring([sb(es, "kT%d" % i, [128, 2, T], BF16, ch=True) for i in range(2)])
                        vbr = Ring([sb(es, "vb%d" % i, [128, 4, 512], BF16, ch=True) for i in range(2)])
                        kzr = Ring([sb(es, "kz%d" % i, [128, 256], BF16) for i in range(4)])
                        scr_ = Ring([sb(es, "sT%d" % i, [128, 128], BF16) for i in range(3)])
                        st = sb(es, "st", [128, 2, 512], F32)
                        stb = sb(es, "stb", [128, 2, 512], BF16)
                        ygr = Ring([sb(es, "ygT%d" % i, [128, 4, T], BF16, ch=True) for i in range(2)]) \
                            if pas == "D" else None
                        sg = sb(es, "sg", [128, 4, T], F32) if pas == "D" else None
                        tabs = sb(es, "tabs", [128, 128 + 128 + 4], F32)
                        maskT = tabs.t[:, 0:128]
                        xit = tabs.t[:, 128:256]
                        zcol = tabs.t[:, 256:257]
                        g128 = tabs.t[:, 257:258]
                        mcol = tabs.t[:, 258:259]
                        lg = lgt.t[:, d * 4 + h:d * 4 + h + 1]
                        pc = cview("pcols")
                        tri = cview("tri").rearrange("p (d n) -> p d n", d=2)
                        nrow = cview("nrow").rearrange("p (d n) -> p d n", d=2)
                        di = 0 if pas == "D" else 1
                        P.op("act", "activation", dict(out=mcol, in_=pc[:, di:di + 1], func=AF.Exp, scale=lg),
                             reads=[cpb.R, lgt.R], writes=[tabs.R])
                        P.op("act", "activation", dict(out=zcol, in_=pc[:, 2 + di:3 + di], func=AF.Exp, scale=lg),
                             reads=[cpb.R, lgt.R], writes=[tabs.R])
                        P.op("act", "activation", dict(out=xit, in_=nrow[:, di, :], func=AF.Exp, scale=lg),
                             reads=[cpb.R, lgt.R], writes=[tabs.R])
                        P.op("act", "activation", dict(out=g128, in_=lg, func=AF.Exp, scale=128.0),
                             reads=[lgt.R], writes=[tabs.R])
                        P.op("dve", "tensor_scalar", dict(out=maskT, in0=tri[:, di, :], scalar1=mcol, scalar2=None,
                                                              op0=ALU.mult), reads=[cpb.R, tabs.R], writes=[tabs.R])
                        gnv = cview("ret_norm")
                        order = range(NT - 1, -1, -1) if pas == "C" else range(NT)
                        for seq in range(NSEQ):
                            hb, x1s, x2s, x3s, ybs = hb_all[seq], x1s_all[seq], x2s_all[seq], x3s_all[seq], ybs_all[seq]
                            xnl, xnrs, xcs = xnl_all[seq], xnrs_all[seq], xcs_all[seq]
                            qrs, kTs, vbs, ygs = qrs_all[seq], kTs_all[seq], vbs_all[seq], ygs_all[seq]
                            xsrc = xin[seq]
                            P.op("dve", "memset", dict(ap=st.t[:, :, :], constant=0.0), writes=[st.R])
                            P.op("dve", "memset", dict(ap=stb.t[:, :, :], constant=0.0), writes=[stb.R])
                            for j in order:
                                xn = xnr.next()
                                qx, kT, vb, qr = qxr.next(), kTr.next(), vbr.next(), qrr.next()
                                ygT = ygr.next() if pas == "D" else None
                                s0 = j * T
                                xib = xit.unsqueeze(1).to_broadcast([128, 4, 128])
                                if pas == "C":
                                    rp = rpr.next()
                                    P.dma("sp", rp.t[:, :, :], rope[:, :, s0:s0 + T].rearrange("r p t -> p r t"), rp.chq("sp"),
                                          writes=[rp.R])
                                if first:
                                    xb = xring.next()
                                    load_x_and_norm((xb, None, xn, tmps), x2s, x2R, j, cview("gmix")[:, 8:16], False)
                                    P.dma("pool", xnrs[j], xn.t[:, :, 2:2 + T], xn.chq("pool"), reads=[xn.R])
                                else:
                                    P.dma("sp", xn.t[:, :, 2:2 + T], xnrs[j], xn.chq("sp"), writes=[xn.R])
                                if pas == "D":
                                    P.dma("sp", qr.t[:, :, :], qrs[h, j], qr.chq("sp"), writes=[qr.R])
                                    P.dma("sp", kT.t[:, :, :], kTs[h, j], kT.chq("sp"), writes=[kT.R])
                                    P.dma("sp", vb.t[:, :, :], vbs[h, j], vb.chq("sp"), writes=[vb.R])
                                    for c2 in range(2):
                                        P.op("dve", "tensor_tensor", dict(
                                            out=qx.t[:, c2, :].rearrange("p (c n) -> p c n", c=4),
                                            in0=qr.t[:, c2, :].rearrange("p (c n) -> p c n", c=4), in1=xib, op=ALU.mult),
                                             reads=[qr.R, tabs.R], writes=[qx.R])
                                for qk in (range(2) if pas == "C" else ()):
                                    p0 = proj_main(lambda kc: qkgv[:, 2 * qk, kc, :], xn, w_qkg.R)
                                    p1 = proj_main(lambda kc: qkgv[:, 2 * qk + 1, kc, :], xn, w_qkg.R)
                                    cs, sn = rp.t[:, 2 * qk, :], rp.t[:, 2 * qk + 1, :]
                                    e0, e1 = tmps.next(), tmps.next()
                                    t1, t2, t3, t4 = [tmps.next() for _ in range(4)]
                                    dst = qx if qk == 0 else kT
                                    P.op("act", "activation", dict(out=e0.t[:, 0:T], in_=p0.t[:, :], func=AF.Copy),
                                         reads=[p0.R], writes=[e0.R])
                                    P.op("act", "activation", dict(out=e1.t[:, 0:T], in_=p1.t[:, :], func=AF.Copy),
                                         reads=[p1.R], writes=[e1.R])
                                    P.op("dve", "tensor_tensor", dict(out=t1.t[:, 0:T], in0=e0.t[:, 0:T], in1=cs,
                                                                      op=ALU.mult), reads=[e0.R, rp.R], writes=[t1.R])
                                    P.op("dve", "tensor_tensor", dict(out=t2.t[:, 0:T], in0=e1.t[:, 0:T], in1=sn,
                                                                      op=ALU.mult), reads=[e1.R, rp.R], writes=[t2.R])
                                    P.op("dve", "tensor_tensor", dict(out=t3.t[:, 0:T], in0=e1.t[:, 0:T], in1=cs,
                                                                      op=ALU.mult), reads=[e1.R, rp.R], writes=[t3.R])
                                    P.op("dve", "tensor_tensor", dict(out=t4.t[:, 0:T], in0=e0.t[:, 0:T], in1=sn,
                                                                      op=ALU.mult), reads=[e0.R, rp.R], writes=[t4.R])
                                    if qk == 0:
                                        P.op("dve", "tensor_tensor", dict(out=qr.t[:, 0, :], in0=t1.t[:, 0:T],
                                                                              in1=t2.t[:, 0:T], op=ALU.subtract),
                                             reads=[t1.R, t2.R], writes=[qr.R])
                                        P.op("dve", "tensor_tensor", dict(out=qr.t[:, 1, :], in0=t3.t[:, 0:T],
                                                                              in1=t4.t[:, 0:T], op=ALU.add),
                                             reads=[t3.R, t4.R], writes=[qr.R])
                                        P.dma("pool", qrs[h, j], qr.t[:, :, :], qr.chq("pool"), reads=[qr.R])
                                        for c2 in range(2):
                                            P.op("dve", "tensor_tensor", dict(
                                                out=qx.t[:, c2, :].rearrange("p (c n) -> p c n", c=4),
                                                in0=qr.t[:, c2, :].rearrange("p (c n) -> p c n", c=4), in1=xib,
                                                op=ALU.mult), reads=[qr.R, tabs.R], writes=[qx.R])
                                    else:
                                        P.op("dve", "tensor_tensor", dict(out=kT.t[:, 0, :], in0=t1.t[:, 0:T],
                                                                              in1=t2.t[:, 0:T], op=ALU.subtract),
                                             reads=[t1.R, t2.R], writes=[kT.R])
                                        P.op("dve", "tensor_tensor", dict(out=kT.t[:, 1, :], in0=t3.t[:, 0:T],
                                                                              in1=t4.t[:, 0:T], op=ALU.add),
                                             reads=[t3.R, t4.R], writes=[kT.R])
                                        P.dma("pool", kTs[h, j], kT.t[:, :, :], kT.chq("pool"), reads=[kT.R])
                                for tt in (range(4) if pas == "C" else ()):
                                    bk_ = bk("pm")
                                    for kc in range(8):
                                        P.op("pe", "matmul", dict(out=bk_.t[:, :],
                                                                      lhsT=xn.t[:, kc, 2 + tt * 128:2 + (tt + 1) * 128],
                                                                      rhs=wvv[:, kc, :], start=(kc == 0), stop=(kc == 7)),
                                             reads=[xn.R, w_v.R], writes=[bk_.R])
                                    P.op("act", "activation", dict(out=vb.t[:, tt, :], in_=bk_.t[:, :], func=AF.Copy),
                                         reads=[bk_.R], writes=[vb.R])
                                if pas == "C":
                                    P.dma("pool", vbs[h, j], vb.t[:, :, :], vb.chq("pool"), reads=[vb.R])
                                if pas == "D":
                                    for fc in range(4):
                                        pg = proj_main(lambda kc: qkgv[:, fc, kc, :], xn, w_qkg.R)
                                        P.op("act", "activation", dict(out=sg.t[:, fc, :], in_=pg.t[:, :],
                                                                           func=AF.Silu), reads=[pg.R], writes=[sg.R])
                                        P.op("dve", "tensor_scalar", dict(out=sg.t[:, fc, :], in0=sg.t[:, fc, :],
                                                                              scalar1=gnv[:, 4 * h + fc:4 * h + fc + 1],
                                                                              scalar2=None, op0=ALU.mult),
                                             reads=[sg.R, cpb.R], writes=[sg.R])
                                    if h == 3:
                                        xr, ygo = xrr.next(), ygor.next()
                                        P.dma("sp", xr.t[:, :, :], x2s[s0:s0 + T, :].rearrange("(t p) d -> p t d", p=128),
                                              xr.chq("sp"), writes=[xr.R])
                                        P.dma("sp", ygo.t[:, :, :], ygs[j], ygo.chq("sp"), writes=[ygo.R])
                                corder = range(3, -1, -1) if pas == "C" else range(4)
                                for cc in corder:
                                    csl = slice(cc * 128, (cc + 1) * 128)
                                    r0 = s0 + cc * 128
                                    ps_ = bk("sc")
                                    for c2 in range(2):
                                        P.op("pe", "matmul", dict(out=ps_.t[:, 0:128], lhsT=kT.t[:, c2, csl],
                                                                      rhs=qx.t[:, c2, csl], start=(c2 == 0),
                                                                      stop=(c2 == 1)),
                                             reads=[kT.R, qx.R], writes=[ps_.R])
                                    sT = scr_.next()
                                    P.op("dve", "tensor_tensor", dict(out=sT.t[:, :], in0=ps_.t[:, 0:128], in1=maskT,
                                                                          op=ALU.mult),
                                         reads=[ps_.R, tabs.R], writes=[sT.R])
                                    po = bk("po")
                                    P.op("pe", "matmul", dict(out=po.t[:, :], lhsT=sT.t[:, :], rhs=vb.t[:, cc, :],
                                                                  start=True, stop=False),
                                         reads=[sT.R, vb.R], writes=[po.R])
                                    for c2 in range(2):
                                        P.op("pe", "matmul", dict(out=po.t[:, :], lhsT=qx.t[:, c2, csl],
                                                                      rhs=stb.t[:, c2, :], start=False, stop=(c2 == 1)),
                                             reads=[qx.R, stb.R], writes=[po.R])
                                    pk = bk("tk")
                                    pkv = pk.t[:, :].bitcast(BF16)
                                    for c2 in range(2):
                                        P.op("pe", "transpose", dict(out=pkv[:, c2 * 128:(c2 + 1) * 128],
                                                                         in_=kT.t[:, c2, csl], identity=identb.t[:, :]),
                                             reads=[kT.R, identb.R], writes=[pk.R])
                                    kz = kzr.next()
                                    P.op("act", "activation", dict(out=kz.t[:, :], in_=pkv[:, 0:256], func=AF.Copy,
                                                                       scale=zcol), reads=[pk.R, tabs.R], writes=[kz.R])
                                    yt = tmps.next()
                                    if pas == "C":
                                        P.op("act", "activation", dict(out=yt.t[:, 0:512], in_=po.t[:, :],
                                                                           func=AF.Copy), reads=[po.R], writes=[yt.R])
                                        P.dma("pool", ybs[h, r0:r0 + 128, :], yt.t[:, 0:512], yt.chq("pool"), reads=[yt.R])
                                    else:
                                        ybt = tmps.next()
                                        P.dma("sp", ybt.t[:, 0:512], ybs[h, r0:r0 + 128, :], ybt.chq("sp"), writes=[ybt.R])
                                        P.op("dve", "tensor_tensor", dict(out=yt.t[:, 0:512], in0=po.t[:, :],
                                                                              in1=ybt.t[:, 0:512], op=ALU.add),
                                             reads=[po.R, ybt.R], writes=[yt.R])
                                        jk = tmps.next()
                                        sm = smr.next()
                                        ss = sm.t
                                        P.op("act", "activation", dict(out=jk.t[:, 0:512], in_=yt.t[:, 0:512],
                                                                           func=AF.Square, accum_out=ss[:, 12:13]),
                                             reads=[yt.R], writes=[jk.R, sm.R])
                                        P.op("dve", "tensor_scalar", dict(out=ss[:, 13:14], in0=ss[:, 12:13],
                                                                          scalar1=1.0 / 512, scalar2=EPS,
                                                                          op0=ALU.mult, op1=ALU.add),
                                             reads=[sm.R], writes=[sm.R])
                                        P.op("pool", "tensor_tensor", dict(out=ss[:, 14:15], in0=ss[:, 13:14],
                                                                           in1=negh.t[:, 0:1], op=ALU.pow),
                                             reads=[sm.R, negh.R], writes=[sm.R])
                                        ynb = tmps.next()
                                        ynv = ynb.t[:, :].bitcast(BF16)
                                        P.op("act", "activation", dict(out=ynv[:, 0:512], in_=yt.t[:, 0:512],
                                                                           func=AF.Copy, scale=ss[:, 14:15]),
                                             reads=[yt.R, sm.R], writes=[ynb.R])
                                        pt = bk("ty")
                                        ptv = pt.t[:, :].bitcast(BF16).rearrange("p (k c) -> p k c", k=8)
                                        for fc in range(4):
                                            P.op("pe", "transpose", dict(out=ptv[:, fc, :],
                                                                             in_=ynv[:, fc * 128:(fc + 1) * 128],
                                                                             identity=identb.t[:, :]),
                                                 reads=[ynb.R, identb.R], writes=[pt.R])
                                        P.op("dve", "tensor_tensor", dict(out=ygT.t[:, :, csl], in0=ptv[:, 0:4, :],
                                                                              in1=sg.t[:, :, csl], op=ALU.mult),
                                             reads=[pt.R, sg.R], writes=[ygT.R])
                                    for c2 in range(2):
                                        pu = bk("su")
                                        P.op("pe", "matmul", dict(out=pu.t[:, :], lhsT=kz.t[:, c2 * 128:(c2 + 1) * 128],
                                                                      rhs=vb.t[:, cc, :], start=True, stop=True),
                                             reads=[kz.R, vb.R], writes=[pu.R])
                                        P.op("dve", "scalar_tensor_tensor", dict(out=st.t[:, c2, :],
                                                                                     in0=st.t[:, c2, :], scalar=g128,
                                                                                     in1=pu.t[:, :], op0=ALU.mult,
                                                                                     op1=ALU.add),
                                             reads=[st.R, tabs.R, pu.R], writes=[st.R])
                                    P.op("act", "activation", dict(out=stb.t[:, :, :], in_=st.t[:, :, :],
                                                                       func=AF.Copy), reads=[st.R], writes=[stb.R])
                                if pas == "D" and h < 3:
                                    P.dma("pool", ygs[j, :, 4 * h:4 * h + 4, :], ygT.t[:, :, :], ygT.chq("pool"), reads=[ygT.R])
                                if pas == "D" and h == 3:
                                    out_proj_resid(lambda kc, tt: (ygo.t[:, kc, tt * 128:(tt + 1) * 128] if kc < 12 else
                                                                   ygT.t[:, kc - 12, tt * 128:(tt + 1) * 128]),
                                                   [ygo.R, ygT.R], 16, w_o, xr)
                                    store_x(xr, x3s, x3R, j)
                        P.barrier()

            if want("E"):
                ffn_pass(1, x3s_all, x3R, yout, None, True)
        P.barrier()
        P.n_total = P.n_ins
        if debug:
            print("[build] channels", len(P.chans), "instructions", P.n_ins, flush=True)
    return nc


_CACHE = {}


def _get_program(S, NSEQ, debug=False):
    key = (S, NSEQ, debug)
    if key not in _CACHE:
        _CACHE[key] = build_program(S, NSEQ, debug)
    return _CACHE[key]


def kernel(**inputs):
    inp = {k: np.asarray(v) for k, v in inputs.items()}
    xp, xs = inp["x_prompt"], inp["x_sample"]
    S = xp.shape[1]
    seqs = [xp[i] for i in range(xp.shape[0])] + [xs[i] for i in range(xs.shape[0])]
    NSEQ = 2
    nslots = N_CORES * NSEQ
    wp = pack_weights(inp)
    cpk = pack_consts(inp)
    rp = rope_tables(S)
    zero = np.zeros((S, D), np.float32)
    in_maps = []
    for c in range(N_CORES):
        a = seqs[c]
        b = seqs[c + N_CORES] if c + N_CORES < len(seqs) else zero
        in_maps.append({"xin": np.ascontiguousarray(np.stack([a, b], axis=0)), "wpack": wp, "cpack": cpk,
                        "rope": rp})
    nc = _get_program(S, NSEQ)
    res = run_bass_kernel_spmd(nc, in_maps, core_ids=list(range(N_CORES)))
    outs = [None] * len(seqs)
    for c in range(N_CORES):
        y = res.results[c]["yout"]
        outs[c] = y[0]
        if c + N_CORES < len(seqs):
            outs[c + N_CORES] = y[1]
    nb = xp.shape[0]
    y_prompt = np.ascontiguousarray(np.stack(outs[:nb], axis=0)).astype(np.float32)
    y_sample = np.ascontiguousarray(np.stack(outs[nb:], axis=0)).astype(np.float32)
    return (y_prompt, y_sample)
```

```python
import math
from contextlib import ExitStack
import numpy as np
import ml_dtypes
import concourse.bass as bass
import concourse.mybir as mybir
from concourse.bass_utils import run_bass_kernel_spmd

F32 = mybir.dt.float32
BF16 = mybir.dt.bfloat16
AF = mybir.ActivationFunctionType
ALU = mybir.AluOpType

D = 1024
T = 512
XW = 515
TMPW = 520
DFF = 2816
NFC = 22
EPS = 1e-6
N_CORES = 8

W_LAYOUT = {}
_off = 0
for _n, _sz in [("lru_in", 16 * 8 * 128), ("ga0", 4 * 2 * 2 * 128), ("gx0", 4 * 2 * 2 * 128),
                ("ga1", 4 * 2 * 2 * 128), ("gx1", 4 * 2 * 2 * 128), ("lru_out", 8 * 1024),
                ("ffn_in0", 11 * 4 * 8 * 128), ("ffn_out0", 22 * 1024),
                ("ffn_in1", 11 * 4 * 8 * 128), ("ffn_out1", 22 * 1024)] + \
               [("ret_qkg%d" % h, 8 * 8 * 128) for h in range(4)] + \
               [("ret_v%d" % h, 8 * 512) for h in range(4)] + \
               [("ret_out%d" % h, 4 * 1024) for h in range(4)]:
    W_LAYOUT[_n] = (_off, _sz)
    _off += _sz
NW = _off

C_LAYOUT = {}
_off = 0
for _n, _sz in [("ident", 128), ("tri", 256), ("nrow", 256), ("pcols", 4), ("gmix", 16), ("gffn", 16),
                ("gfin", 1024), ("lru_cw", 32), ("lru_cb", 8), ("b_a", 16), ("b_x", 16), ("lam", 16),
                ("ffn_cw0", 66), ("ffn_cw1", 66), ("ffn_cb0", 22), ("ffn_cb1", 22), ("ret_norm", 16),
                ("dlogit", 8)]:
    C_LAYOUT[_n] = (_off, _sz)
    _off += _sz
NCP = _off


def _cols(v, nchunk):
    return np.ascontiguousarray(v.reshape(nchunk, 128).T)


def _stat(w, fcs):
    K = w.shape[0]
    w4 = w.reshape(K // 128, 128, w.shape[1] // 128, 128)
    return np.ascontiguousarray(w4[:, :, fcs, :].transpose(1, 2, 0, 3))


def _mov(w):
    K = w.shape[0]
    return np.ascontiguousarray(w.reshape(K // 128, 128, w.shape[1]).transpose(1, 0, 2))


def pack_weights(inp):
    wp = np.empty((128, NW), np.float32)

    def put(name, arr):
        o, sz = W_LAYOUT[name]
        wp[:, o:o + sz] = arr.reshape(128, sz)

    put("lru_in", _stat(inp["lru_w_in"][0], list(range(16))))
    for d in range(2):
        for nm, key in (("ga", "lru_w_a"), ("gx", "lru_w_x")):
            w = inp[key][0, d]
            w5 = w.reshape(4, 2, 128, 2, 128)
            put("%s%d" % (nm, d), np.ascontiguousarray(w5.transpose(2, 0, 1, 3, 4)))
    put("lru_out", _mov(inp["lru_w_out"][0]))
    for l in range(2):
        w = inp["ffn_w_in"][l]
        slabs = []
        for s in range(11):
            slabs.append(_stat(w, [2 * s, 2 * s + 1, NFC + 2 * s, NFC + 2 * s + 1]))
        put("ffn_in%d" % l, np.stack(slabs, axis=1))
        put("ffn_out%d" % l, _mov(inp["ffn_w_out"][l]))
    rw = inp["ret_w_in"][0]
    for h in range(4):
        fcs = [2 * h, 2 * h + 1, 8 + 2 * h, 8 + 2 * h + 1] + [32 + 4 * h + i for i in range(4)]
        put("ret_qkg%d" % h, _stat(rw, fcs))
        put("ret_v%d" % h, _mov(rw[:, 2048 + 512 * h:2048 + 512 * (h + 1)]))
        put("ret_out%d" % h, _mov(inp["ret_w_out"][0][512 * h:512 * (h + 1), :]))
    return wp


def pack_consts(inp):
    cp = np.zeros((128, NCP), np.float32)

    def put(name, arr):
        o, sz = C_LAYOUT[name]
        cp[:, o:o + sz] = np.asarray(arr, np.float32).reshape(128, sz)

    put("ident", np.eye(128, dtype=np.float32))
    m = np.arange(128)[:, None]
    n = np.arange(128)[None, :]
    tri = np.stack([(n >= m), (m > n)], axis=1).astype(np.float32)
    put("tri", tri)
    nrow = np.stack([np.broadcast_to(n + 1.0, (128, 128)), np.broadcast_to(128.0 - n, (128, 128))], axis=1)
    put("nrow", nrow)
    mm = np.arange(128, dtype=np.float32)
    put("pcols", np.stack([-(mm + 1), mm - 128, 127 - mm, mm], axis=1))
    put("gmix", np.stack([_cols(inp["norm_mix"][l], 8) for l in range(2)], axis=1))
    put("gffn", np.stack([_cols(inp["norm_ffn"][l], 8) for l in range(2)], axis=1))
    put("gfin", np.broadcast_to(inp["norm_final"][None, :], (128, 1024)))
    cw = inp["lru_conv_w"][0]
    put("lru_cw", np.stack([_cols(cw[k], 8) for k in range(4)], axis=2))
    put("lru_cb", _cols(inp["lru_conv_b"][0], 8))
    for nm, key in (("b_a", "lru_b_a"), ("b_x", "lru_b_x"), ("lam", "lru_lambda")):
        put(nm, np.stack([_cols(inp[key][0, d], 8) for d in range(2)], axis=1))
    for l in range(2):
        cw = inp["ffn_conv_w"][l]
        put("ffn_cw%d" % l, np.stack([_cols(cw[k], NFC) for k in range(3)], axis=2))
        put("ffn_cb%d" % l, _cols(inp["ffn_conv_b"][l], NFC))
    put("ret_norm", _cols(inp["ret_norm"][0], 16))
    put("dlogit", np.broadcast_to(inp["ret_decay_logit"][0].reshape(1, 8), (128, 8)))
    return cp


def rope_tables(S):
    half = 128
    theta = (10000.0 ** (-np.arange(half, dtype=np.float32) / np.float32(half))).astype(np.float32)
    ang = (np.arange(S, dtype=np.float32)[None, :] * theta[:, None]).astype(np.float32)
    c = np.cos(ang.astype(np.float64)).astype(np.float32)
    s = np.sin(ang.astype(np.float64)).astype(np.float32)
    return np.ascontiguousarray(np.stack([c, s, c / 16.0, s / 16.0], axis=0))


import heapq


class Res:
    __slots__ = ("w", "r", "const")

    def __init__(self, const=False):
        self.w = None
        self.r = []
        self.const = const


class Chan:
    __slots__ = ("sem", "cnt", "last")

    def __init__(self, sem):
        self.sem = sem
        self.cnt = 0
        self.last = None


class Op:
    __slots__ = ("idx", "eng", "name", "kw", "deps", "cost", "lat", "ch", "batch", "n", "fin", "npred", "succ",
                 "ready", "cls", "boost")


_ACT_CLASS = {AF.Tanh: 1, AF.Exp: 1, AF.Sqrt: 2, AF.Gelu_apprx_tanh: 3, AF.Silu: 4, AF.Ln: 5, AF.Sigmoid: 6}
NO_LATE = False
FUSE_WAIT = True
PREP_STQ = "sp"
PREP_MOD = 2
TMPS_A = 44
TMPS_B1 = 21
PW_LATE = 1024
GELU_BOOST = 3.0e5
ACT_SWITCH_NS = 1300.0
ACT_STARVE_NS = 6000.0


def _free_size(ap):
    sh = ap.shape
    n = 1
    for v in sh[1:]:
        n *= int(v)
    return n


def _is_psum(ap):
    try:
        return "psum" in str(ap.space).lower() or "ps" == str(ap.tensor.name)[:2]
    except Exception:
        return False


class Prog:
    SELF_SYNC = ("act", "dve", "pool")
    DMA_BW = 200.0
    DMA_LAT = 2000.0

    def __init__(self, nc, es):
        self.nc = nc
        self.es = es
        self.engs = {"pe": nc.tensor, "act": nc.scalar, "dve": nc.vector, "pool": nc.gpsimd, "sp": nc.sync}
        self.sem = {}
        self.cnt = {}
        self.seen = {}
        for e in self.engs:
            self.sem[e] = es.enter_context(nc.semaphore("s_" + e))
            self.cnt[e] = 0
            self.seen[e] = {}
        self.chans = []
        self.free_ch = {}
        self.scope_ch = []
        self.n_ins = 0
        self.pending = []
        self.batch = 0
        self.sim_total = 0.0
        self.act_cls = [0]
        self.crit = None
        self.crit_batch = -1
        self.verbose = False

    def chan(self, kind="sp"):
        fl = self.free_ch.setdefault(kind, [])
        if fl:
            c = fl.pop()
        else:
            c = Chan(self.es.enter_context(self.nc.semaphore("ch%d" % len(self.chans))))
            self.chans.append(c)
        self.scope_ch.append((kind, c))
        return c

    def end_scope(self):
        for kind, c in self.scope_ch:
            self.free_ch.setdefault(kind, []).append(c)
        self.scope_ch = []

    def _cost(self, e, name, kw):
        if e == "pe":
            if name == "transpose":
                n = _free_size(kw["identity"])
            else:
                n = _free_size(kw["rhs"])
            return max(n * 0.42 + 10.0, 56.0), 120.0
        out = kw.get("out", kw.get("ap"))
        n = _free_size(out)
        src = kw.get("in_", kw.get("in0", kw.get("data0", None)))
        ps = 60.0 if (src is not None and _is_psum(src)) else 0.0
        if e == "act":
            return 190.0 + n * 0.86 + ps, 120.0
        if e == "dve":
            f = 2.2 if name == "tensor_tensor_scan" else 1.2
            if out.dtype == BF16 and src is not None and src.dtype == BF16:
                f *= 0.5
            return 70.0 + n * f + ps, 120.0
        if e == "pool":
            f = 4.2 if name == "tensor_copy" else 2.4
            return 120.0 + n * f, 150.0
        return 500.0, 100.0

    def _mkop(self, e, name, kw, reads, writes):
        o = Op()
        o.idx = len(self.pending)
        o.eng = e
        o.name = name
        o.kw = kw
        o.batch = self.batch
        o.ch = None
        o.cls = 0
        o.boost = 0.0
        deps = set()
        for R in reads:
            if R.w is not None:
                deps.add(R.w)
        for R in writes:
            if R.w is not None:
                deps.add(R.w)
            for x in R.r:
                deps.add(x)
        for R in reads:
            if not R.const:
                R.r.append(o)
        for R in writes:
            R.w = o
            R.r = []
        deps.discard(o)
        o.deps = [d for d in deps if d.batch == self.batch]
        self.pending.append(o)
        return o

    def op(self, e, name, kw, reads=(), writes=(), boost=0.0):
        o = self._mkop(e, name, kw, reads, writes)
        o.cost, o.lat = self._cost(e, name, kw)
        o.boost = boost
        if e == "act" and name == "activation":
            o.cls = _ACT_CLASS.get(kw["func"], 0)

    def dma(self, q, out, in_, ch, reads=(), writes=()):
        o = self._mkop(q, "dma_start", dict(out=out, in_=in_), reads, writes)
        o.ch = ch
        if ch.last is not None and ch.last.batch == self.batch and ch.last not in o.deps:
            o.deps.append(ch.last)
        ch.last = o
        nbytes = _free_size(out) * int(out.shape[0]) * (2 if out.dtype == BF16 else 4)
        o.cost = 150.0 if q == "sp" else 600.0
        o.lat = float(nbytes)

    def flush(self):
        ops = self.pending
        self.pending = []
        self.batch += 1
        if not ops:
            return
        for o in ops:
            o.succ = []
            o.npred = len(o.deps)
            o.fin = None
        for o in ops:
            for d in o.deps:
                d.succ.append(o)
        for o in reversed(ops):
            b = 0.0
            for sc in o.succ:
                if sc.ready > b:
                    b = sc.ready
            tl = (o.lat / self.DMA_BW + self.DMA_LAT) if o.ch is not None else o.lat
            o.ready = b + o.cost + tl + o.boost
        for o in ops:
            o.idx = (-o.ready, o.idx)
        free = {e: 0.0 for e in self.engs}
        fut = {e: [] for e in self.engs}
        avl = {e: {} for e in self.engs}
        busy = {e: 0.0 for e in self.engs}
        for o in ops:
            if o.npred == 0:
                o.ready = 0.0
                heapq.heappush(fut[o.eng], (0.0, o.idx, o))
        streams = {e: [] for e in self.engs}
        dma_free = 0.0
        ncnt = dict(self.cnt)
        nleft = len(ops)
        engl = list(self.engs)
        cur_cls = self.act_cls

        def pick(e, t, pop):
            a = avl[e]
            best = None
            bestc = None
            for c, h in a.items():
                if h and (best is None or h[0][0] < best[0]):
                    best, bestc = h[0], c
            if best is None:
                return None
            if e == "act" and bestc not in (0, cur_cls[0]):
                alt, altc = None, None
                for c in (0, cur_cls[0]):
                    h = a.get(c)
                    if h and (alt is None or h[0][0] < alt[0]):
                        alt, altc = h[0], c
                if alt is not None and t - best[1] < ACT_STARVE_NS:
                    best, bestc = alt, altc
            if pop:
                heapq.heappop(a[bestc])
            return best

        while nleft:
            bsel = None
            for e in engl:
                f, t = fut[e], free[e]
                while f and f[0][0] <= t:
                    x = heapq.heappop(f)
                    heapq.heappush(avl[e].setdefault(x[2].cls, []), (x[1], x[0], x[2]))
                c = pick(e, t, False)
                if c is not None:
                    cs, ci = t, c[0]
                elif f:
                    cs, ci = f[0][0], f[0][1]
                else:
                    continue
                if bsel is None or (cs, ci) < bsel[0]:
                    bsel = ((cs, ci), e)
            (cs, ci), e = bsel
            c = pick(e, cs, True)
            if c is not None:
                o = c[2]
            else:
                o = heapq.heappop(fut[e])[2]
            start = cs
            if self.crit is not None:
                bd = None
                for d in o.deps:
                    if bd is None or d.fin > bd.fin:
                        bd = d
                prev = streams[e][-1] if streams[e] else None
                if bd is not None and bd.fin >= cs - 1e-6:
                    self.crit[o] = ("dep", bd, start)
                else:
                    self.crit[o] = ("eng", prev, start)
            if e == "act" and o.cls != 0 and o.cls != cur_cls[0]:
                start += ACT_SWITCH_NS
                busy[e] += ACT_SWITCH_NS
                cur_cls[0] = o.cls
            free[e] = start + o.cost
            busy[e] += o.cost
            if o.ch is not None:
                t0 = max(start + self.DMA_LAT * 0.5, dma_free)
                dma_free = t0 + o.lat / self.DMA_BW
                o.fin = dma_free + self.DMA_LAT * 0.5
            else:
                o.fin = start + o.cost + o.lat
            streams[e].append(o)
            if o.ch is None:
                ncnt[e] += 1
                o.n = ncnt[e]
            else:
                o.ch.cnt += 1
                o.n = o.ch.cnt
            nleft -= 1
            for sc in o.succ:
                sc.npred -= 1
                if sc.npred == 0:
                    r = 0.0
                    for d in sc.deps:
                        f = d.fin - d.lat if (d.eng == "pe" and sc.eng == "pe") else d.fin
                        if f > r:
                            r = f
                    sc.ready = r
                    heapq.heappush(fut[sc.eng], (r, sc.idx, sc))
        span = max([o.fin for o in ops])
        if self.crit is not None and self.batch == self.crit_batch:
            o = max(ops, key=lambda x: x.fin)
            chain = []
            while o is not None and len(chain) < 400:
                kind, p, st = self.crit[o]
                kw = o.kw or {}
                oo = kw.get("out", kw.get("ap"))
                try:
                    nm = oo.tensor.name
                except Exception:
                    nm = "?"
                chain.append("%9.1f %-4s %-22s cost=%6.0f fin=%9.1f via %s -> %s" % (st, o.eng, o.name, o.cost, o.fin, kind, nm))
                o = p
            print("\n".join(reversed(chain)))
        if self.crit is not None:
            self.crit = {}
        self.sim_total += span
        if self.verbose:
            print("[sched] batch %d: %d ops, sim span %.1f us, busy: %s" % (
                self.batch, len(ops), span / 1e3,
                " ".join("%s=%.0f" % (e, busy[e] / 1e3) for e in engl)), flush=True)
        for e in engl:
            eng = self.engs[e]
            seen = self.seen[e]
            for o in streams[e]:
                need = {}
                for d in o.deps:
                    if d.ch is not None:
                        src, sem, v = d.ch, d.ch.sem, 16 * d.n
                    else:
                        if d.eng == e and e not in self.SELF_SYNC:
                            continue
                        src, sem, v = d.eng, self.sem[d.eng], d.n
                    if seen.get(src, 0) < d.n and (src not in need or need[src][2] < d.n):
                        need[src] = (sem, v, d.n)
                waits = list(need.items())
                fused = None
                if FUSE_WAIT and o.ch is None and waits:
                    fused = waits.pop()
                for src, (sem, v, n_) in waits:
                    eng.wait_ge(sem, v)
                    seen[src] = n_
                    self.n_ins += 1
                ins = getattr(eng, o.name)(**o.kw)
                if fused is not None:
                    src, (sem, v, n_) = fused
                    ins._wait_ge(sem, v)
                    seen[src] = n_
                if o.ch is None:
                    ins.then_inc(self.sem[e], 1)
                    self.cnt[e] = o.n
                else:
                    ins.then_inc(o.ch.sem, 16)
                self.n_ins += 1
                o.kw = None
                o.succ = None

    def barrier(self):
        self.flush()
        for e in self.engs:
            eng = self.engs[e]
            seen = self.seen[e]
            for e2 in self.engs:
                if e2 != e and self.cnt[e2] > seen.get(e2, 0):
                    eng.wait_ge(self.sem[e2], self.cnt[e2])
                    seen[e2] = self.cnt[e2]
            for c in self.chans:
                if c.cnt > seen.get(c, 0):
                    eng.wait_ge(c.sem, 16 * c.cnt)
                    seen[c] = c.cnt
        self.end_scope()


class Buf:
    __slots__ = ("t", "R", "P", "chs")

    def __init__(self, t, P=None):
        self.t = t
        self.R = Res()
        self.P = P
        self.chs = {}

    def chq(self, q):
        c = self.chs.get(q)
        if c is None:
            c = self.P.chan(q)
            self.chs[q] = c
        return c


class Ring:
    def __init__(self, bufs):
        self.bufs = bufs
        self.i = 0

    def next(self):
        b = self.bufs[self.i % len(self.bufs)]
        self.i += 1
        return b


def build_program(S, NSEQ, debug=False, only=None):
    NT = S // T

    def want(p):
        return only is None or p in only
    nc = bass.Bass("TRN2", target_bir_lowering=False)
    dk = "ExternalOutput" if debug else "Internal"
    xin = nc.dram_tensor("xin", [NSEQ, S, D], F32, kind="ExternalInput").ap()
    wpack = nc.dram_tensor("wpack", [128, NW], F32, kind="ExternalInput").ap()
    cpack = nc.dram_tensor("cpack", [128, NCP], F32, kind="ExternalInput").ap()
    rope = nc.dram_tensor("rope", [4, 128, S], F32, kind="ExternalInput").ap()
    yout = nc.dram_tensor("yout", [NSEQ, S, D], F32, kind="ExternalOutput").ap()
    wsc = nc.dram_tensor("wsc", [128, NW], BF16, kind="Internal").ap()
    hb_all = nc.dram_tensor("hb", [NSEQ, NT, 128, 8, T], F32, kind=dk).ap()
    x1s_all = nc.dram_tensor("x1s", [NSEQ, S, D], F32, kind=dk).ap()
    x2s_all = nc.dram_tensor("x2s", [NSEQ, S, D], F32, kind=dk).ap()
    x3s_all = nc.dram_tensor("x3s", [NSEQ, S, D], F32, kind=dk).ap()
    ybs_all = nc.dram_tensor("ybs", [NSEQ, 4, S, 512], F32, kind=dk).ap()
    xnl_all = nc.dram_tensor("xnl", [NSEQ, NT, 128, 8, T], BF16, kind="Internal").ap()
    xnrs_all = nc.dram_tensor("xnrs", [NSEQ, NT, 128, 8, T], BF16, kind="Internal").ap()
    xcs_all = nc.dram_tensor("xcs", [NSEQ, NT, 128, 8, T], F32, kind="Internal").ap()
    qrs_all = nc.dram_tensor("qrs", [NSEQ, 4, NT, 128, 2, T], F32, kind="Internal").ap()
    kTs_all = nc.dram_tensor("kTs", [NSEQ, 4, NT, 128, 2, T], BF16, kind="Internal").ap()
    vbs_all = nc.dram_tensor("vbs", [NSEQ, 4, NT, 128, 4, 512], BF16, kind="Internal").ap()
    ygs_all = nc.dram_tensor("ygs", [NSEQ, NT, 128, 12, T], BF16, kind="Internal").ap()

    ges = ExitStack()
    with ges:
        P = Prog(nc, ges)
        P.verbose = debug

        uid = [0]

        def sb(es, name, shape, dt, ch=False):
            uid[0] += 1
            t = es.enter_context(nc.sbuf_tensor("%s_%d" % (name, uid[0]), shape, dt))
            return Buf(t, P)

        cpb = sb(ges, "cp", [128, NCP], F32, ch=True)
        cpb.R.const = True
        cp = cpb.t

        def cview(name):
            o, sz = C_LAYOUT[name]
            return cp[:, o:o + sz]

        identb = sb(ges, "identb", [128, 128], BF16)
        clru = sb(ges, "clru", [128, 16], F32)
        lgt = sb(ges, "lgt", [128, 8], F32)
        smr = Ring([sb(ges, "smallf%d" % i, [128, 16], F32) for i in range(8)])
        hbias = sb(ges, "hbias", [128, 32], F32)
        clh = sb(ges, "clh", [128, 16], F32)
        negh = sb(ges, "negh", [128, 16], F32)
        banks = [Buf(ges.enter_context(nc.psum_tensor("ps%d" % i, [128, 512], F32)), P) for i in range(8)]
        bring = Ring(banks)
        roles = {}

        def set_roles(cfg):
            roles.clear()
            for names, idxs in cfg:
                r = Ring([banks[i] for i in idxs])
                for nm in names.split(","):
                    roles[nm] = r

        def bk(role):
            return roles[role].next() if role in roles else bring.next()
        wscR = Res()
        hbR = [Res() for _ in range(NT)]
        x1R = [Res() for _ in range(NT)]
        x2R = [Res() for _ in range(NT)]
        x3R = [Res() for _ in range(NT)]
        ybR = [[Res() for _ in range(NT)] for _ in range(4)]
        outR = Res()
        inR = Res()

        P.scope_ch = []
        P.dma("sp", cp[:, :], cpack[:, :], cpb.chq("sp"), writes=[cpb.R])

        PW = 2048
        prep_k = [0]

        def prep_pieces(names, sring, bring2, pw=PW, ldq="sp", stq="pool"):
            for nm in names:
                o0, sz = W_LAYOUT[nm]
                o = o0
                while o < o0 + sz:
                    w = min(pw, o0 + sz - o)
                    s_, b_ = sring.next(), bring2.next()
                    P.dma(ldq, s_.t[:, :w], wpack[:, o:o + w], s_.chq(ldq), writes=[s_.R])
                    if prep_k[0] % PREP_MOD != 1:
                        P.op("act", "activation", dict(out=b_.t[:, :w], in_=s_.t[:, :w], func=AF.Copy),
                             reads=[s_.R], writes=[b_.R])
                    else:
                        P.op("dve", "tensor_copy", dict(out=b_.t[:, :w], in_=s_.t[:, :w]),
                             reads=[s_.R], writes=[b_.R])
                    P.dma(stq, wsc[:, o:o + w], b_.t[:, :w], b_.chq(stq), reads=[b_.R])
                    o += w
                    prep_k[0] += 1

        EARLY_W = ["lru_in", "ga1", "gx1"]
        LATE_A = ["ga0", "gx0", "lru_out", "ffn_in0", "ffn_out0"]
        LATE_B2 = [n for n in W_LAYOUT if n not in EARLY_W and n not in LATE_A]
        with ExitStack() as es:
            sring = Ring([sb(es, "pst%d" % i, [128, PW], F32, ch=True) for i in range(3)])
            bring2 = Ring([sb(es, "pbt%d" % i, [128, PW], BF16, ch=True) for i in range(3)])
            prep_pieces(EARLY_W + LATE_A, sring, bring2)
            P.op("dve", "tensor_copy", dict(out=identb.t[:, :], in_=cview("ident")), reads=[cpb.R],
                 writes=[identb.R])
            P.op("act", "activation", dict(out=clru.t[:, :], in_=cview("lam"), func=AF.Exp, scale=-1.0),
                 reads=[cpb.R], writes=[clru.R])
            P.op("act", "activation", dict(out=clru.t[:, :], in_=clru.t[:, :], func=AF.Ln, bias=1.0),
                 reads=[clru.R], writes=[clru.R])
            P.op("dve", "tensor_scalar", dict(out=clru.t[:, :], in0=clru.t[:, :], scalar1=-8.0, scalar2=None,
                                                  op0=ALU.mult), reads=[clru.R], writes=[clru.R])
            P.op("dve", "tensor_scalar", dict(out=clh.t[:, :], in0=clru.t[:, :], scalar1=0.5, scalar2=None,
                                              op0=ALU.mult), reads=[clru.R], writes=[clh.R])
            P.op("dve", "tensor_scalar", dict(out=hbias.t[:, 0:16], in0=cview("b_a"), scalar1=0.5, scalar2=None,
                                              op0=ALU.mult), reads=[cpb.R], writes=[hbias.R])
            P.op("dve", "tensor_scalar", dict(out=hbias.t[:, 16:32], in0=cview("b_x"), scalar1=0.5, scalar2=None,
                                              op0=ALU.mult), reads=[cpb.R], writes=[hbias.R])
            P.op("dve", "memset", dict(ap=negh.t[:, :], constant=-0.5), writes=[negh.R])
            P.op("act", "activation", dict(out=lgt.t[:, :], in_=cview("dlogit"), func=AF.Exp, scale=-1.0),
                 reads=[cpb.R], writes=[lgt.R])
            P.op("act", "activation", dict(out=lgt.t[:, :], in_=lgt.t[:, :], func=AF.Ln, bias=1.0),
                 reads=[lgt.R], writes=[lgt.R])
            P.op("dve", "tensor_scalar", dict(out=lgt.t[:, :], in0=lgt.t[:, :], scalar1=-1.0, scalar2=None,
                                                  op0=ALU.mult), reads=[lgt.R], writes=[lgt.R])
            P.barrier()

        def load_w(es, name, tag, sub=None):
            o, sz = W_LAYOUT[name]
            if sub is not None:
                o, sz = o + sub[0], sub[1]
            b = sb(es, "w_" + tag, [128, sz], BF16, ch=True)
            P.dma("sp", b.t[:, :], wsc[:, o:o + sz], b.chq("sp"), writes=[b.R])
            return b

        def norm_T(tmps, xt, xR, ntile, nparts, gcol, xn, col_of):
            sm = smr.next()
            ss = sm.t
            for tt in range(ntile):
                jk = tmps.next()
                jv = jk.t[:, :].bitcast(BF16)
                P.op("act", "activation", dict(out=jv[:nparts, 0:1024], in_=xt(tt), func=AF.Square,
                                                   accum_out=ss[:nparts, tt:tt + 1]),
                     reads=[xR], writes=[jk.R, sm.R])
            P.op("dve", "tensor_scalar", dict(out=ss[:nparts, 4:4 + ntile], in0=ss[:nparts, 0:ntile],
                                              scalar1=1.0 / D, scalar2=EPS, op0=ALU.mult, op1=ALU.add),
                 reads=[sm.R], writes=[sm.R])
            P.op("pool", "tensor_tensor", dict(out=ss[:nparts, 8:8 + ntile], in0=ss[:nparts, 4:4 + ntile],
                                               in1=negh.t[:nparts, 0:ntile], op=ALU.pow),
                 reads=[sm.R, negh.R], writes=[sm.R])
            for tt in range(ntile):
                xs = tmps.next()
                xv = xs.t[:, :].bitcast(BF16)
                P.op("dve", "tensor_scalar", dict(out=xv[:nparts, 0:1024], in0=xt(tt),
                                                      scalar1=ss[:nparts, 8 + tt:9 + tt], scalar2=None, op0=ALU.mult),
                     reads=[xR, sm.R], writes=[xs.R])
                bk_ = bk("nt")
                bv = bk_.t[:, :].bitcast(BF16).rearrange("p (k c) -> p k c", k=8)
                for kc in range(8):
                    P.op("pe", "transpose", dict(out=bv[:, kc, 0:nparts], in_=xv[:nparts, kc * 128:(kc + 1) * 128],
                                                     identity=identb.t[:nparts, :nparts]),
                         reads=[xs.R, identb.R], writes=[bk_.R])
                for (c0, n, s0) in col_of(tt):
                    P.op("dve", "tensor_tensor", dict(out=xn.t[:, :, c0:c0 + n], in0=bv[:, :, s0:s0 + n],
                                                          in1=gcol.unsqueeze(2).to_broadcast([128, 8, n]),
                                                          op=ALU.mult),
                         reads=[bk_.R, cpb.R], writes=[xn.R])

        def load_x_and_norm(es_bufs, src, srcR, j, gcol, halo):
            xb, hx, xn, tmps = es_bufs
            s0 = j * T
            P.dma("sp", xb.t[:, :, :], src[s0:s0 + T, :].rearrange("(t p) d -> p t d", p=128), xb.chq("sp"),
                  writes=[xb.R])
            if halo:
                P.op("dve", "memset", dict(ap=hx.t[0:3, :], constant=0.0), writes=[hx.R])
                if j > 0:
                    P.dma("sp", hx.t[0:2, :], src[s0 - 2:s0, :], hx.chq("sp"), writes=[hx.R])
                if j < NT - 1:
                    P.dma("sp", hx.t[2:3, :], src[s0 + T:s0 + T + 1, :], hx.chq("sp"), writes=[hx.R])
                norm_T(tmps, lambda tt: hx.t[0:3, :], hx.R, 1, 3, gcol, xn, lambda tt: [(0, 2, 0), (XW - 1, 1, 2)])
            norm_T(tmps, lambda tt: xb.t[:, tt, :], xb.R, 4, 128, gcol, xn,
                   lambda tt: [(2 + tt * 128, 128, 0)])

        def proj_split(wl, xn, wR):
            bA, bB = bk("ps"), bk("ps")
            for kc in range(8):
                P.op("pe", "matmul", dict(out=bA.t[:, 0:258], lhsT=wl(kc), rhs=xn.t[:, kc, 0:258], start=(kc == 0),
                                              stop=(kc == 7)), reads=[wR, xn.R], writes=[bA.R])
                P.op("pe", "matmul", dict(out=bB.t[:, 0:257], lhsT=wl(kc), rhs=xn.t[:, kc, 258:515], start=(kc == 0),
                                              stop=(kc == 7)), reads=[wR, xn.R], writes=[bB.R])
            return bA, bB

        def proj_main(wl, xn, wR):
            bk_ = bk("pm")
            for kc in range(8):
                P.op("pe", "matmul", dict(out=bk_.t[:, :], lhsT=wl(kc), rhs=xn.t[:, kc, 2:2 + T], start=(kc == 0),
                                              stop=(kc == 7)), reads=[wR, xn.R], writes=[bk_.R])
            return bk_

        def evac_split(bA, bB, tmps):
            ub = tmps.next()
            P.op("act", "activation", dict(out=ub.t[:, 0:258], in_=bA.t[:, 0:258], func=AF.Copy),
                 reads=[bA.R], writes=[ub.R])
            P.op("act", "activation", dict(out=ub.t[:, 258:515], in_=bB.t[:, 0:257], func=AF.Copy),
                 reads=[bB.R], writes=[ub.R])
            return ub

        def out_proj_resid(act_lhs, actR, nk, wout, xb, after_tile=None):
            wv = wout.t[:, :].rearrange("p (k f) -> p k f", k=nk)
            for tt in range(4):
                for hf in range(2):
                    bk_ = bk("op")
                    for kc in range(nk):
                        P.op("pe", "matmul", dict(out=bk_.t[:, :], lhsT=act_lhs(kc, tt),
                                                      rhs=wv[:, kc, hf * 512:(hf + 1) * 512],
                                                      start=(kc == 0), stop=(kc == nk - 1)),
                             reads=(actR if isinstance(actR, list) else [actR]) + [wout.R], writes=[bk_.R])
                    P.op("dve", "tensor_tensor", dict(out=xb.t[:, tt, hf * 512:(hf + 1) * 512], in0=bk_.t[:, :],
                                                          in1=xb.t[:, tt, hf * 512:(hf + 1) * 512], op=ALU.add),
                         reads=[bk_.R, xb.R], writes=[xb.R])
                if after_tile is not None:
                    after_tile(tt)

        def store_x(xb, dst, dstR, j):
            s0 = j * T
            P.dma("pool", dst[s0:s0 + T, :].rearrange("(t p) d -> p t d", p=128), xb.t[:, :, :], xb.chq("pool"),
                  reads=[xb.R])

        for _once in (0,):
            pass
            xinR = [inR] * NT

            for pas in ("A", "B1"):
                if not want(pas):
                    continue
                with ExitStack() as es:
                    d = 1 if pas == "A" else 0
                    if pas == "A":
                        set_roles([("nt,ps", [0, 1, 2, 3]), ("g", [4, 5, 6, 7])])
                    else:
                        set_roles([("pm", [0, 1]), ("g", [2, 3, 4, 5]), ("op", [6, 7])])
                    w_in = load_w(es, "lru_in", "lin", sub=((8 * 8 * 128, 8 * 8 * 128) if pas == "A" else (0, 8 * 8 * 128)))
                    w_ga = load_w(es, "ga%d" % d, "ga")
                    w_gx = load_w(es, "gx%d" % d, "gx")
                    w_out = load_w(es, "lru_out", "lout") if pas == "B1" else None
                    winv = w_in.t[:, :].rearrange("p (fc kc f) -> p fc kc f", fc=8, kc=8)
                    gav = w_ga.t[:, :].rearrange("p (n kc oc f) -> p n kc oc f", n=4, kc=2, oc=2)
                    gxv = w_gx.t[:, :].rearrange("p (n kc oc f) -> p n kc oc f", n=4, kc=2, oc=2)
                    xring = Ring([sb(es, "x%d" % i, [128, 4, D], F32, ch=True) for i in range(2)])
                    hx = sb(es, "hx", [4, D], F32, ch=True) if pas == "A" else None
                    xnr = Ring([sb(es, "xn%d" % i, [128, 8, XW], BF16, ch=True) for i in range(2)])
                    tmps = Ring([sb(es, "tmp%d" % i, [128, TMPW], F32, ch=True)
                                 for i in range(TMPS_A if pas == "A" else TMPS_B1)])
                    psr = pbr = None
                    xcr = Ring([sb(es, "xc%d" % i, [128, 2, T], F32, ch=True) for i in range(4 if pas == "A" else 3)])
                    xcbr = Ring([sb(es, "xcb%d" % i, [128, 2, T], BF16) for i in range(3)])
                    hgr = Ring([sb(es, "hg%d" % i, [128, 8, T], BF16) for i in range(2)]) if pas == "B1" else None
                    gtr = Ring([sb(es, "gt%d" % i, [128, T], F32) for i in range(8)]) if pas == "B1" else None
                    carry = sb(es, "carry", [128, 8], F32)
                    carryR = [Res() for _ in range(8)]
                    cwv = cview("lru_cw").rearrange("p (k t) -> p k t", k=8)
                    cbv = cview("lru_cb")
                    bav = cview("b_a")
                    bxv = cview("b_x")
                    order = range(NT - 1, -1, -1) if pas == "A" else range(NT)
                    for seq in range(NSEQ):
                        hb, x1s, x2s, x3s, ybs = hb_all[seq], x1s_all[seq], x2s_all[seq], x3s_all[seq], ybs_all[seq]
                        xnl, xnrs, xcs = xnl_all[seq], xnrs_all[seq], xcs_all[seq]
                        qrs, kTs, vbs, ygs = qrs_all[seq], kTs_all[seq], vbs_all[seq], ygs_all[seq]
                        xsrc = xin[seq]
                        P.op("dve", "memset", dict(ap=carry.t[:, :], constant=0.0), writes=[carry.R] + carryR)
                        for j in order:
                            xb, xn = xring.next(), xnr.next()
                            hg = hgr.next() if pas == "B1" else None
                            if pas == "A":
                                load_x_and_norm((xb, hx, xn, tmps), xsrc, xinR, j, cview("gmix")[:, 0:8], True)
                                P.dma("pool", xnl[j], xn.t[:, :, 2:2 + T], xn.chq("pool"), reads=[xn.R])
                            else:
                                P.dma("sp", xb.t[:, :, :], xsrc[j * T:(j + 1) * T, :].rearrange("(t p) d -> p t d", p=128),
                                      xb.chq("sp"), writes=[xb.R])
                                P.dma("sp", xn.t[:, :, 2:2 + T], xnl[j], xn.chq("sp"), writes=[xn.R])
                            for n in range(4):
                                xc, xcb = xcr.next(), xcbr.next()
                                if pas == "B1":
                                    P.dma("sp", xc.t[:, :, :], xcs[j, :, 2 * n:2 * n + 2, :], xc.chq("sp"), writes=[xc.R])
                                for o2 in range(2):
                                    c = 2 * n + o2
                                    if pas == "B1":
                                        P.op("dve", "tensor_copy", dict(out=xcb.t[:, o2, :], in_=xc.t[:, o2, :]),
                                             reads=[xc.R], writes=[xcb.R])
                                        continue
                                    bA, bB = proj_split(lambda kc: winv[:, c, kc, :], xn, w_in.R)
                                    ub = evac_split(bA, bB, tmps)
                                    P.op("dve", "tensor_scalar", dict(out=xc.t[:, o2, :], in0=ub.t[:, 0:T],
                                                                          scalar1=cwv[:, c, 0:1], scalar2=cbv[:, c:c + 1],
                                                                          op0=ALU.mult, op1=ALU.add),
                                         reads=[ub.R, cpb.R], writes=[xc.R])
                                    for k in range(1, 4):
                                        P.op("dve", "scalar_tensor_tensor", dict(out=xc.t[:, o2, :],
                                                                                     in0=ub.t[:, k:k + T],
                                                                                     scalar=cwv[:, c, k:k + 1],
                                                                                     in1=xc.t[:, o2, :], op0=ALU.mult,
                                                                                     op1=ALU.add),
                                             reads=[ub.R, cpb.R, xc.R], writes=[xc.R])
                                    P.op("act", "activation", dict(out=xcb.t[:, o2, :], in_=xc.t[:, o2, :], func=AF.Copy),
                                         reads=[xc.R], writes=[xcb.R])
                                if pas == "A":
                                    P.dma("pool", xcs[j, :, 2 * n:2 * n + 2, :], xc.t[:, :, :], xc.chq("pool"), reads=[xc.R])
                                for o2 in range(2):
                                    c = 2 * n + o2
                                    pa, px = bk("g"), bk("g")
                                    for kc in range(2):
                                        P.op("pe", "matmul", dict(out=pa.t[:, :], lhsT=gav[:, n, kc, o2, :],
                                                                      rhs=xcb.t[:, kc, :], start=(kc == 0),
                                                                      stop=(kc == 1)),
                                             reads=[w_ga.R, xcb.R], writes=[pa.R])
                                    for kc in range(2):
                                        P.op("pe", "matmul", dict(out=px.t[:, :], lhsT=gxv[:, n, kc, o2, :],
                                                                      rhs=xcb.t[:, kc, :], start=(kc == 0),
                                                                      stop=(kc == 1)),
                                             reads=[w_gx.R, xcb.R], writes=[px.R])
                                    r_, i_, a_, s_, u_, h_ = [tmps.next() for _ in range(6)]
                                    ia, ix = d * 8 + c, 16 + d * 8 + c
                                    P.op("act", "activation", dict(out=r_.t[:, 0:T], in_=pa.t[:, :], func=AF.Tanh,
                                                                   scale=0.5, bias=hbias.t[:, ia:ia + 1]),
                                         reads=[pa.R, hbias.R], writes=[r_.R])
                                    P.op("act", "activation", dict(out=i_.t[:, 0:T], in_=px.t[:, :], func=AF.Tanh,
                                                                   scale=0.5, bias=hbias.t[:, ix:ix + 1]),
                                         reads=[px.R, hbias.R], writes=[i_.R])
                                    P.op("act", "activation", dict(out=a_.t[:, 0:T], in_=r_.t[:, 0:T], func=AF.Exp,
                                                                   scale=clh.t[:, ia:ia + 1], bias=clh.t[:, ia:ia + 1]),
                                         reads=[r_.R, clh.R], writes=[a_.R])
                                    P.op("act", "activation", dict(out=s_.t[:, 0:T], in_=a_.t[:, 0:T], func=AF.Square),
                                         reads=[a_.R], writes=[s_.R])
                                    P.op("act", "activation", dict(out=s_.t[:, 0:T], in_=s_.t[:, 0:T], func=AF.Sqrt,
                                                                   scale=-0.25, bias=0.25),
                                         reads=[s_.R], writes=[s_.R])
                                    P.op("dve", "scalar_tensor_tensor", dict(out=u_.t[:, 0:T], in0=i_.t[:, 0:T],
                                                                             scalar=1.0, in1=xc.t[:, o2, :],
                                                                             op0=ALU.add, op1=ALU.mult),
                                         reads=[i_.R, xc.R], writes=[u_.R])
                                    P.op("dve", "tensor_tensor", dict(out=u_.t[:, 0:T], in0=u_.t[:, 0:T],
                                                                      in1=s_.t[:, 0:T], op=ALU.mult),
                                         reads=[u_.R, s_.R], writes=[u_.R])
                                    if pas == "A":
                                        P.op("dve", "tensor_tensor_scan", dict(out=h_.t[:, 0:T][:, ::-1],
                                                                                   data0=a_.t[:, 0:T][:, ::-1],
                                                                                   data1=u_.t[:, 0:T][:, ::-1],
                                                                                   initial=carry.t[:, c:c + 1],
                                                                                   op0=ALU.mult, op1=ALU.add),
                                             reads=[a_.R, u_.R, carryR[c]], writes=[h_.R])
                                        P.op("pool", "tensor_copy", dict(out=carry.t[:, c:c + 1], in_=h_.t[:, 0:1]),
                                             reads=[h_.R], writes=[carryR[c]])
                                        P.dma("pool", hb[j, :, c, :], h_.t[:, 0:T], h_.chq("pool"), reads=[h_.R])
                                    else:
                                        hbt, gt = tmps.next(), gtr.next()
                                        P.dma("sp", hbt.t[:, 0:T], hb[j, :, c, :], hbt.chq("sp"), writes=[hbt.R])
                                        P.op("dve", "tensor_tensor_scan", dict(out=h_.t[:, 0:T], data0=a_.t[:, 0:T],
                                                                                   data1=u_.t[:, 0:T],
                                                                                   initial=carry.t[:, c:c + 1],
                                                                                   op0=ALU.mult, op1=ALU.add),
                                             reads=[a_.R, u_.R, carryR[c]], writes=[h_.R])
                                        P.op("dve", "tensor_copy", dict(out=carry.t[:, c:c + 1], in_=h_.t[:, T - 1:T]),
                                             reads=[h_.R], writes=[carryR[c]])
                                        pg = proj_main(lambda kc: winv[:, c, kc, :], xn, w_in.R)
                                        P.op("act", "activation", dict(out=gt.t[:, 0:T], in_=pg.t[:, :],
                                                                           func=AF.Gelu_apprx_tanh),
                                             reads=[pg.R], writes=[gt.R], boost=GELU_BOOST)
                                        P.op("dve", "tensor_tensor", dict(out=h_.t[:, 0:T], in0=h_.t[:, 0:T],
                                                                          in1=hbt.t[:, 0:T], op=ALU.add),
                                             reads=[h_.R, hbt.R], writes=[h_.R])
                                        P.op("dve", "tensor_tensor", dict(out=hg.t[:, c, :], in0=h_.t[:, 0:T],
                                                                              in1=gt.t[:, 0:T], op=ALU.mult),
                                             reads=[h_.R, gt.R], writes=[hg.R])
                            if pas == "B1":
                                out_proj_resid(lambda kc, tt: hg.t[:, kc, tt * 128:(tt + 1) * 128], hg.R, 8, w_out, xb)
                                store_x(xb, x1s, x1R, j)
                    if pas == "A":
                        prep_pieces([], psr, pbr, PW_LATE)
                    P.barrier()

            def ffn_pass(l, src_all, srcR, dst_all, dstR, final):
                with ExitStack() as es:
                    set_roles([])
                    w_out = load_w(es, "ffn_out%d" % l, "fout")
                    o_in, _ = W_LAYOUT["ffn_in%d" % l]
                    SL = 4 * 8 * 128
                    slabs = Ring([sb(es, "slab%d" % i, [128, SL], BF16, ch=True) for i in range(3)])
                    xring = Ring([sb(es, "x%d" % i, [128, 4, D], F32, ch=True) for i in range(2)])
                    hx = sb(es, "hx", [4, D], F32, ch=True)
                    xnr = Ring([sb(es, "xn%d" % i, [128, 8, XW], BF16) for i in range(2)])
                    tmps = Ring([sb(es, "tmp%d" % i, [128, TMPW], F32, ch=True) for i in range(16)])
                    actT = sb(es, "actT", [128, NFC, T], BF16)
                    if l == 0:
                        psr = Ring([sb(es, "pst%d" % i, [128, PW_LATE], F32, ch=True) for i in range(2)])
                        pbr = Ring([sb(es, "pbt%d" % i, [128, PW_LATE], BF16, ch=True) for i in range(2)])
                    oring = Ring([sb(es, "ot%d" % i, [128, D], F32, ch=True) for i in range(2)]) if final else None
                    cwv = cview("ffn_cw%d" % l).rearrange("p (k t) -> p k t", k=NFC)
                    cbv = cview("ffn_cb%d" % l)
                    gfv = cview("gfin")
                    for seq in range(NSEQ):
                        hb, x1s, x2s, x3s, ybs = hb_all[seq], x1s_all[seq], x2s_all[seq], x3s_all[seq], ybs_all[seq]
                        xnl, xnrs, xcs = xnl_all[seq], xnrs_all[seq], xcs_all[seq]
                        qrs, kTs, vbs, ygs = qrs_all[seq], kTs_all[seq], vbs_all[seq], ygs_all[seq]
                        xsrc = xin[seq]
                        src, dst = src_all[seq], dst_all[seq]
                        for j in range(NT):
                            xb, xn = xring.next(), xnr.next()
                            load_x_and_norm((xb, hx, xn, tmps), src, srcR, j, cview("gffn")[:, l * 8:(l + 1) * 8], True)
                            for s in range(11):
                                sl = slabs.next()
                                P.dma("sp", sl.t[:, :], wsc[:, o_in + s * SL:o_in + (s + 1) * SL], sl.chq("sp"), writes=[sl.R])
                                slv = sl.t[:, :].rearrange("p (fc kc f) -> p fc kc f", fc=4, kc=8)
                                for q in range(2):
                                    fc = 2 * s + q
                                    bA, bB = proj_split(lambda kc: slv[:, q, kc, :], xn, sl.R)
                                    ub = evac_split(bA, bB, tmps)
                                    cv, gu = tmps.next(), tmps.next()
                                    P.op("dve", "tensor_scalar", dict(out=cv.t[:, 0:T], in0=ub.t[:, 1:1 + T],
                                                                          scalar1=cwv[:, fc, 0:1],
                                                                          scalar2=cbv[:, fc:fc + 1],
                                                                          op0=ALU.mult, op1=ALU.add),
                                         reads=[ub.R, cpb.R], writes=[cv.R])
                                    for k in range(1, 3):
                                        P.op("dve", "scalar_tensor_tensor", dict(out=cv.t[:, 0:T],
                                                                                     in0=ub.t[:, 1 + k:1 + k + T],
                                                                                     scalar=cwv[:, fc, k:k + 1],
                                                                                     in1=cv.t[:, 0:T], op0=ALU.mult,
                                                                                     op1=ALU.add),
                                             reads=[ub.R, cpb.R, cv.R], writes=[cv.R])
                                    P.op("act", "activation", dict(out=gu.t[:, 0:T], in_=cv.t[:, 0:T],
                                                                       func=AF.Gelu_apprx_tanh),
                                         reads=[cv.R], writes=[gu.R])
                                    pv = proj_main(lambda kc: slv[:, 2 + q, kc, :], xn, sl.R)
                                    P.op("dve", "tensor_tensor", dict(out=actT.t[:, fc, :], in0=pv.t[:, :],
                                                                          in1=gu.t[:, 0:T], op=ALU.mult),
                                         reads=[pv.R, gu.R], writes=[actT.R])

                            def fin_tile(tt):
                                ot = oring.next()
                                jk = tmps.next()
                                jv = jk.t[:, :].bitcast(BF16)
                                sm = smr.next()
                                ss = sm.t
                                P.op("act", "activation", dict(out=jv[:, 0:1024], in_=xb.t[:, tt, :], func=AF.Square,
                                                                   accum_out=ss[:, 12:13]),
                                     reads=[xb.R], writes=[jk.R, sm.R])
                                P.op("dve", "tensor_scalar", dict(out=ss[:, 13:14], in0=ss[:, 12:13], scalar1=1.0 / D,
                                                                  scalar2=EPS, op0=ALU.mult, op1=ALU.add),
                                     reads=[sm.R], writes=[sm.R])
                                P.op("pool", "tensor_tensor", dict(out=ss[:, 14:15], in0=ss[:, 13:14],
                                                                   in1=negh.t[:, 0:1], op=ALU.pow),
                                     reads=[sm.R, negh.R], writes=[sm.R])
                                P.op("dve", "scalar_tensor_tensor", dict(out=ot.t[:, :], in0=xb.t[:, tt, :],
                                                                             scalar=ss[:, 14:15], in1=gfv,
                                                                             op0=ALU.mult, op1=ALU.mult),
                                     reads=[xb.R, sm.R, cpb.R], writes=[ot.R])
                                r0 = j * T + tt * 128
                                P.dma("pool", dst[r0:r0 + 128, :], ot.t[:, :], ot.chq("pool"), reads=[ot.R])

                            out_proj_resid(lambda kc, tt: actT.t[:, kc, tt * 128:(tt + 1) * 128], actT.R, NFC, w_out, xb,
                                           after_tile=fin_tile if final else None)
                            if not final:
                                store_x(xb, dst, dstR, j)
                    if l == 0:
                        prep_pieces(LATE_B2 if not NO_LATE else [], psr, pbr, PW_LATE, ldq="pool", stq="pool")
                    P.barrier()

            if want("B2"):
                ffn_pass(0, x1s_all, x1R, x2s_all, x2R, False)

            for pas in ("C", "D"):
                d = 1 if pas == "C" else 0
                for h in range(4):
                    if not want(pas):
                        continue
                    with ExitStack() as es:
                        set_roles([("nt,pm,op", [0, 1]), ("sc,tk", [2, 3]), ("po,ty", [4, 5]), ("su", [6, 7])])
                        if pas == "C":
                            w_qkg = load_w(es, "ret_qkg%d" % h, "qkg", sub=(0, 4 * 8 * 128))
                            w_v = load_w(es, "ret_v%d" % h, "wv")
                            w_o = None
                            wvv = w_v.t[:, :].rearrange("p (kc f) -> p kc f", kc=8)
                        else:
                            w_qkg = load_w(es, "ret_qkg%d" % h, "qkg", sub=(4 * 8 * 128, 4 * 8 * 128))
                            w_v = None
                            w_o = load_w(es, "ret_out0", "wo", sub=(0, 16 * 1024)) if h == 3 else None
                        qkgv = w_qkg.t[:, :].rearrange("p (fc kc f) -> p fc kc f", fc=4, kc=8)
                        first = (pas == "C" and h == 0)
                        xring = Ring([sb(es, "x%d" % i, [128, 4, D], F32, ch=True) for i in range(2)]) \
                            if first else None
                        xrr = Ring([sb(es, "xr%d" % i, [128, 4, D], F32, ch=True) for i in range(2)]) \
                            if (pas == "D" and h == 3) else None
                        ygor = Ring([sb(es, "ygo%d" % i, [128, 12, T], BF16, ch=True) for i in range(2)]) \
                            if (pas == "D" and h == 3) else None
                        xnr = Ring([sb(es, "xn%d" % i, [128, 8, XW], BF16, ch=True) for i in range(2)])
                        tmps = Ring([sb(es, "tmp%d" % i, [128, TMPW], F32, ch=True)
                                     for i in range(12 if pas == "D" else 16)])
                        rpr = Ring([sb(es, "rp%d" % i, [128, 4, T], F32, ch=True) for i in range(2)]) \
                            if pas == "C" else None
                        qrr = Ring([sb(es, "qr%d" % i, [128, 2, T], F32, ch=True) for i in range(2)])
                        qxr = Ring([sb(es, "qx%d" % i, [128, 2, T], BF16) for i in range(2)])
                        kTr = Ring([sb(es, "kT%d" % i, [128, 2, T], BF16, ch=True) for i in range(2)])
                        vbr = Ring([sb(es, "vb%d" % i, [128, 4, 512], BF16, ch=True) for i in range(2)])
                        kzr = Ring([sb(es, "kz%d" % i, [128, 256], BF16) for i in range(4)])
                        scr_ = Ring([sb(es, "sT%d" % i, [128, 128], BF16) for i in range(3)])
                        st = sb(es, "st", [128, 2, 512], F32)
                        stb = sb(es, "stb", [128, 2, 512], BF16)
                        ygr = Ring([sb(es, "ygT%d" % i, [128, 4, T], BF16, ch=True) for i in range(2)]) \
                            if pas == "D" else None
                        sg = sb(es, "sg", [128, 4, T], F32) if pas == "D" else None
                        tabs = sb(es, "tabs", [128, 128 + 128 + 4], F32)
                        maskT = tabs.t[:, 0:128]
                        xit = tabs.t[:, 128:256]
                        zcol = tabs.t[:, 256:257]
                        g128 = tabs.t[:, 257:258]
                        mcol = tabs.t[:, 258:259]
                        lg = lgt.t[:, d * 4 + h:d * 4 + h + 1]
                        pc = cview("pcols")
                        tri = cview("tri").rearrange("p (d n) -> p d n", d=2)
                        nrow = cview("nrow").rearrange("p (d n) -> p d n", d=2)
                        di = 0 if pas == "D" else 1
                        P.op("act", "activation", dict(out=mcol, in_=pc[:, di:di + 1], func=AF.Exp, scale=lg),
                             reads=[cpb.R, lgt.R], writes=[tabs.R])
                        P.op("act", "activation", dict(out=zcol, in_=pc[:, 2 + di:3 + di], func=AF.Exp, scale=lg),
                             reads=[cpb.R, lgt.R], writes=[tabs.R])
                        P.op("act", "activation", dict(out=xit, in_=nrow[:, di, :], func=AF.Exp, scale=lg),
                             reads=[cpb.R, lgt.R], writes=[tabs.R])
                        P.op("act", "activation", dict(out=g128, in_=lg, func=AF.Exp, scale=128.0),
                             reads=[lgt.R], writes=[tabs.R])
                        P.op("dve", "tensor_scalar", dict(out=maskT, in0=tri[:, di, :], scalar1=mcol, scalar2=None,
                                                              op0=ALU.mult), reads=[cpb.R, tabs.R], writes=[tabs.R])
                        gnv = cview("ret_norm")
                        order = range(NT - 1, -1, -1) if pas == "C" else range(NT)
                        for seq in range(NSEQ):
                            hb, x1s, x2s, x3s, ybs = hb_all[seq], x1s_all[seq], x2s_all[seq], x3s_all[seq], ybs_all[seq]
                            xnl, xnrs, xcs = xnl_all[seq], xnrs_all[seq], xcs_all[seq]
                            qrs, kTs, vbs, ygs = qrs_all[seq], kTs_all[seq], vbs_all[seq], ygs_all[seq]
                            xsrc = xin[seq]
                            P.op("dve", "memset", dict(ap=st.t[:, :, :], constant=0.0), writes=[st.R])
                            P.op("dve", "memset", dict(ap=stb.t[:, :, :], constant=0.0), writes=[stb.R])
                            for j in order:
                                xn = xnr.next()
                                qx, kT, vb, qr = qxr.next(), kTr.next(), vbr.next(), qrr.next()
                                ygT = ygr.next() if pas == "D" else None
                                s0 = j * T
                                xib = xit.unsqueeze(1).to_broadcast([128, 4, 128])
                                if pas == "C":
                                    rp = rpr.next()
                                    P.dma("sp", rp.t[:, :, :], rope[:, :, s0:s0 + T].rearrange("r p t -> p r t"), rp.chq("sp"),
                                          writes=[rp.R])
                                if first:
                                    xb = xring.next()
                                    load_x_and_norm((xb, None, xn, tmps), x2s, x2R, j, cview("gmix")[:, 8:16], False)
                                    P.dma("pool", xnrs[j], xn.t[:, :, 2:2 + T], xn.chq("pool"), reads=[xn.R])
                                else:
                                    P.dma("sp", xn.t[:, :, 2:2 + T], xnrs[j], xn.chq("sp"), writes=[xn.R])
                                if pas == "D":
                                    P.dma("sp", qr.t[:, :, :], qrs[h, j], qr.chq("sp"), writes=[qr.R])
                                    P.dma("sp", kT.t[:, :, :], kTs[h, j], kT.chq("sp"), writes=[kT.R])
                                    P.dma("sp", vb.t[:, :, :], vbs[h, j], vb.chq("sp"), writes=[vb.R])
                                    for c2 in range(2):
                                        P.op("dve", "tensor_tensor", dict(
                                            out=qx.t[:, c2, :].rearrange("p (c n) -> p c n", c=4),
                                            in0=qr.t[:, c2, :].rearrange("p (c n) -> p c n", c=4), in1=xib, op=ALU.mult),
                                             reads=[qr.R, tabs.R], writes=[qx.R])
                                for qk in (range(2) if pas == "C" else ()):
                                    p0 = proj_main(lambda kc: qkgv[:, 2 * qk, kc, :], xn, w_qkg.R)
                                    p1 = proj_main(lambda kc: qkgv[:, 2 * qk + 1, kc, :], xn, w_qkg.R)
                                    cs, sn = rp.t[:, 2 * qk, :], rp.t[:, 2 * qk + 1, :]
                                    e0, e1 = tmps.next(), tmps.next()
                                    t1, t2, t3, t4 = [tmps.next() for _ in range(4)]
                                    dst = qx if qk == 0 else kT
                                    P.op("act", "activation", dict(out=e0.t[:, 0:T], in_=p0.t[:, :], func=AF.Copy),
                                         reads=[p0.R], writes=[e0.R])
                                    P.op("act", "activation", dict(out=e1.t[:, 0:T], in_=p1.t[:, :], func=AF.Copy),
                                         reads=[p1.R], writes=[e1.R])
                                    P.op("dve", "tensor_tensor", dict(out=t1.t[:, 0:T], in0=e0.t[:, 0:T], in1=cs,
                                                                      op=ALU.mult), reads=[e0.R, rp.R], writes=[t1.R])
                                    P.op("dve", "tensor_tensor", dict(out=t2.t[:, 0:T], in0=e1.t[:, 0:T], in1=sn,
                                                                      op=ALU.mult), reads=[e1.R, rp.R], writes=[t2.R])
                                    P.op("dve", "tensor_tensor", dict(out=t3.t[:, 0:T], in0=e1.t[:, 0:T], in1=cs,
                                                                      op=ALU.mult), reads=[e1.R, rp.R], writes=[t3.R])
                                    P.op("dve", "tensor_tensor", dict(out=t4.t[:, 0:T], in0=e0.t[:, 0:T], in1=sn,
                                                                      op=ALU.mult), reads=[e0.R, rp.R], writes=[t4.R])
                                    if qk == 0:
                                        P.op("dve", "tensor_tensor", dict(out=qr.t[:, 0, :], in0=t1.t[:, 0:T],
                                                                              in1=t2.t[:, 0:T], op=ALU.subtract),
                                             reads=[t1.R, t2.R], writes=[qr.R])
                                        P.op("dve", "tensor_tensor", dict(out=qr.t[:, 1, :], in0=t3.t[:, 0:T],
                                                                              in1=t4.t[:, 0:T], op=ALU.add),
                                             reads=[t3.R, t4.R], writes=[qr.R])
                                        P.dma("pool", qrs[h, j], qr.t[:, :, :], qr.chq("pool"), reads=[qr.R])
                                        for c2 in range(2):
                                            P.op("dve", "tensor_tensor", dict(
                                                out=qx.t[:, c2, :].rearrange("p (c n) -> p c n", c=4),
                                                in0=qr.t[:, c2, :].rearrange("p (c n) -> p c n", c=4), in1=xib,
                                                op=ALU.mult), reads=[qr.R, tabs.R], writes=[qx.R])
                                    else:
                                        P.op("dve", "tensor_tensor", dict(out=kT.t[:, 0, :], in0=t1.t[:, 0:T],
                                                                              in1=t2.t[:, 0:T], op=ALU.subtract),
                                             reads=[t1.R, t2.R], writes=[kT.R])
                                        P.op("dve", "tensor_tensor", dict(out=kT.t[:, 1, :], in0=t3.t[:, 0:T],
                                                                              in1=t4.t[:, 0:T], op=ALU.add),
                                             reads=[t3.R, t4.R], writes=[kT.R])
                                        P.dma("pool", kTs[h, j], kT.t[:, :, :], kT.chq("pool"), reads=[kT.R])
                                for tt in (range(4) if pas == "C" else ()):
                                    bk_ = bk("pm")
                                    for kc in range(8):
                                        P.op("pe", "matmul", dict(out=bk_.t[:, :],
                                                                      lhsT=xn.t[:, kc, 2 + tt * 128:2 + (tt + 1) * 128],
                                                                      rhs=wvv[:, kc, :], start=(kc == 0), stop=(kc == 7)),
                                             reads=[xn.R, w_v.R], writes=[bk_.R])
                                    P.op("act", "activation", dict(out=vb.t[:, tt, :], in_=bk_.t[:, :], func=AF.Copy),
                                         reads=[bk_.R], writes=[vb.R])
                                if pas == "C":
                                    P.dma("pool", vbs[h, j], vb.t[:, :, :], vb.chq("pool"), reads=[vb.R])
                                if pas == "D":
                                    for fc in range(4):
                                        pg = proj_main(lambda kc: qkgv[:, fc, kc, :], xn, w_qkg.R)
                                        P.op("act", "activation", dict(out=sg.t[:, fc, :], in_=pg.t[:, :],
                                                                           func=AF.Silu), reads=[pg.R], writes=[sg.R])
                                        P.op("dve", "tensor_scalar", dict(out=sg.t[:, fc, :], in0=sg.t[:, fc, :],
                                                                              scalar1=gnv[:, 4 * h + fc:4 * h + fc + 1],
                                                                              scalar2=None, op0=ALU.mult),
                                             reads=[sg.R, cpb.R], writes=[sg.R])
                                    if h == 3:
                                        xr, ygo = xrr.next(), ygor.next()
                                        P.dma("sp", xr.t[:, :, :], x2s[s0:s0 + T, :].rearrange("(t p) d -> p t d", p=128),
                                              xr.chq("sp"), writes=[xr.R])
                                        P.dma("sp", ygo.t[:, :, :], ygs[j], ygo.chq("sp"), writes=[ygo.R])
                                corder = range(3, -1, -1) if pas == "C" else range(4)
                                for cc in corder:
                                    csl = slice(cc * 128, (cc + 1) * 128)
                                    r0 = s0 + cc * 128
                                    ps_ = bk("sc")
                                    for c2 in range(2):
                                        P.op("pe", "matmul", dict(out=ps_.t[:, 0:128], lhsT=kT.t[:, c2, csl],
                                                                      rhs=qx.t[:, c2, csl], start=(c2 == 0),
                                                                      stop=(c2 == 1)),
                                             reads=[kT.R, qx.R], writes=[ps_.R])
                                    sT = scr_.next()
                                    P.op("dve", "tensor_tensor", dict(out=sT.t[:, :], in0=ps_.t[:, 0:128], in1=maskT,
                                                                          op=ALU.mult),
                                         reads=[ps_.R, tabs.R], writes=[sT.R])
                                    po = bk("po")
                                    P.op("pe", "matmul", dict(out=po.t[:, :], lhsT=sT.t[:, :], rhs=vb.t[:, cc, :],
                                                                  start=True, stop=False),
                                         reads=[sT.R, vb.R], writes=[po.R])
                                    for c2 in range(2):
                                        P.op("pe", "matmul", dict(out=po.t[:, :], lhsT=qx.t[:, c2, csl],
                                                                      rhs=stb.t[:, c2, :], start=False, stop=(c2 == 1)),
                                             reads=[qx.R, stb.R], writes=[po.R])
                                    pk = bk("tk")
                                    pkv = pk.t[:, :].bitcast(BF16)
                                    for c2 in range(2):
                                        P.op("pe", "transpose", dict(out=pkv[:, c2 * 128:(c2 + 1) * 128],
                                                                         in_=kT.t[:, c2, csl], identity=identb.t[:, :]),
                                             reads=[kT.R, identb.R], writes=[pk.R])
                                    kz = kzr.next()
                                    P.op("act", "activation", dict(out=kz.t[:, :], in_=pkv[:, 0:256], func=AF.Copy,
                                                                       scale=zcol), reads=[pk.R, tabs.R], writes=[kz.R])
                                    yt = tmps.next()
                                    if pas == "C":
                                        P.op("act", "activation", dict(out=yt.t[:, 0:512], in_=po.t[:, :],
                                                                           func=AF.Copy), reads=[po.R], writes=[yt.R])
                                        P.dma("pool", ybs[h, r0:r0 + 128, :], yt.t[:, 0:512], yt.chq("pool"), reads=[yt.R])
                                    else:
                                        ybt = tmps.next()
                                        P.dma("sp", ybt.t[:, 0:512], ybs[h, r0:r0 + 128, :], ybt.chq("sp"), writes=[ybt.R])
                                        P.op("dve", "tensor_tensor", dict(out=yt.t[:, 0:512], in0=po.t[:, :],
                                                                              in1=ybt.t[:, 0:512], op=ALU.add),
                                             reads=[po.R, ybt.R], writes=[yt.R])
                                        jk = tmps.next()
                                        sm = smr.next()
                                        ss = sm.t
                                        P.op("act", "activation", dict(out=jk.t[:, 0:512], in_=yt.t[:, 0:512],
                                                                           func=AF.Square, accum_out=ss[:, 12:13]),
                                             reads=[yt.R], writes=[jk.R, sm.R])
                                        P.op("dve", "tensor_scalar", dict(out=ss[:, 13:14], in0=ss[:, 12:13],
                                                                          scalar1=1.0 / 512, scalar2=EPS,
                                                                          op0=ALU.mult, op1=ALU.add),
                                             reads=[sm.R], writes=[sm.R])
                                        P.op("pool", "tensor_tensor", dict(out=ss[:, 14:15], in0=ss[:, 13:14],
                                                                           in1=negh.t[:, 0:1], op=ALU.pow),
                                             reads=[sm.R, negh.R], writes=[sm.R])
                                        ynb = tmps.next()
                                        ynv = ynb.t[:, :].bitcast(BF16)
                                        P.op("act", "activation", dict(out=ynv[:, 0:512], in_=yt.t[:, 0:512],
                                                                           func=AF.Copy, scale=ss[:, 14:15]),
                                             reads=[yt.R, sm.R], writes=[ynb.R])
                                        pt = bk("ty")
                                        ptv = pt.t[:, :].bitcast(BF16).rearrange("p (k c) -> p k c", k=8)
                                        for fc in range(4):
                                            P.op("pe", "transpose", dict(out=ptv[:, fc, :],
                                                                             in_=ynv[:, fc * 128:(fc + 1) * 128],
                                                                             identity=identb.t[:, :]),
                                                 reads=[ynb.R, identb.R], writes=[pt.R])
                                        P.op("dve", "tensor_tensor", dict(out=ygT.t[:, :, csl], in0=ptv[:, 0:4, :],
                                                                              in1=sg.t[:, :, csl], op=ALU.mult),
                                             reads=[pt.R, sg.R], writes=[ygT.R])
                                    for c2 in range(2):
                                        pu = bk("su")
                                        P.op("pe", "matmul", dict(out=pu.t[:, :], lhsT=kz.t[:, c2 * 128:(c2 + 1) * 128],
                                                                      rhs=vb.t[:, cc, :], start=True, stop=True),
                                             reads=[kz.R, vb.R], writes=[pu.R])
                                        P.op("dve", "scalar_tensor_tensor", dict(out=st.t[:, c2, :],
                                                                                     in0=st.t[:, c2, :], scalar=g128,
                                                                                     in1=pu.t[:, :], op0=ALU.mult,
                                                                                     op1=ALU.add),
                                             reads=[st.R, tabs.R, pu.R], writes=[st.R])
                                    P.op("act", "activation", dict(out=stb.t[:, :, :], in_=st.t[:, :, :],
                                                                       func=AF.Copy), reads=[st.R], writes=[stb.R])
                                if pas == "D" and h < 3:
                                    P.dma("pool", ygs[j, :, 4 * h:4 * h + 4, :], ygT.t[:, :, :], ygT.chq("pool"), reads=[ygT.R])
                                if pas == "D" and h == 3:
                                    out_proj_resid(lambda kc, tt: (ygo.t[:, kc, tt * 128:(tt + 1) * 128] if kc < 12 else
                                                                   ygT.t[:, kc - 12, tt * 128:(tt + 1) * 128]),
                                                   [ygo.R, ygT.R], 16, w_o, xr)
                                    store_x(xr, x3s, x3R, j)
                        P.barrier()

            if want("E"):
                ffn_pass(1, x3s_all, x3R, yout, None, True)
        P.barrier()
        P.n_total = P.n_ins
        if debug:
            print("[build] channels", len(P.chans), "instructions", P.n_ins, flush=True)
    return nc


_CACHE = {}


def _get_program(S, NSEQ, debug=False):
    key = (S, NSEQ, debug)
    if key not in _CACHE:
        _CACHE[key] = build_program(S, NSEQ, debug)
    return _CACHE[key]


def kernel(**inputs):
    inp = {k: np.asarray(v) for k, v in inputs.items()}
    xp, xs = inp["x_prompt"], inp["x_sample"]
    S = xp.shape[1]
    seqs = [xp[i] for i in range(xp.shape[0])] + [xs[i] for i in range(xs.shape[0])]
    NSEQ = 2
    nslots = N_CORES * NSEQ
    wp = pack_weights(inp)
    cpk = pack_consts(inp)
    rp = rope_tables(S)
    zero = np.zeros((S, D), np.float32)
    in_maps = []
    for c in range(N_CORES):
        a = seqs[c]
        b = seqs[c + N_CORES] if c + N_CORES < len(seqs) else zero
        in_maps.append({"xin": np.ascontiguousarray(np.stack([a, b], axis=0)), "wpack": wp, "cpack": cpk,
                        "rope": rp})
    nc = _get_program(S, NSEQ)
    res = run_bass_kernel_spmd(nc, in_maps, core_ids=list(range(N_CORES)))
    outs = [None] * len(seqs)
    for c in range(N_CORES):
        y = res.results[c]["yout"]
        outs[c] = y[0]
        if c + N_CORES < len(seqs):
            outs[c + N_CORES] = y[1]
    nb = xp.shape[0]
    y_prompt = np.ascontiguousarray(np.stack(outs[:nb], axis=0)).astype(np.float32)
    y_sample = np.ascontiguousarray(np.stack(outs[nb:], axis=0)).astype(np.float32)
    return (y_prompt, y_sample)
```
